# Optimizing a Trainium2 kernel written in Bass

```python
import math
import jax, jax.numpy as jnp
from jax import lax
import numpy as np

D_MODEL = 1024
BATCH = 16
SEQ = 2048
DEPTH = 1
DEC_BATCH = 32
DEC_SEQ = 1
PAST_LEN = 16384
PAGE_SIZE = 128

N_HEADS = 8
HEAD_DIM = 64
N_KV = 2
HPG = N_HEADS // N_KV
Q_W = N_HEADS * HEAD_DIM
KV_W = N_KV * HEAD_DIM
CMP_BLOCK = 32
CMP_STRIDE = 16
CMP_RATIO = CMP_BLOCK // CMP_STRIDE
CMP_HID = 64
SEL_BLOCK = 64
N_SEL = 16
SEL_BONUS = 1.0e4
WINDOW = 512
Q_BLOCK = 64
ROPE_THETA = 10000.0
D_RNN = D_MODEL
RG_BLOCKS = 16
RG_BW = D_RNN // RG_BLOCKS
RG_C = 8.0
CONV_W = 4
D_FF = ((8 * D_MODEL // 3 + 255) // 256) * 256
IN_W = Q_W + 6 * KV_W + 3 * N_HEADS + 2 * D_RNN + 2 * D_MODEL
ALPHA = (2.0 * DEPTH) ** 0.25
BETA = (8.0 * DEPTH) ** -0.25
LN_EPS = 1e-5

kernel_name = 'nsa_rglru_macaron_deepnorm_step'


def layer_norm(x, g, b):
    xf = x.astype(jnp.float32)
    mu = jnp.mean(xf, axis=-1, keepdims=True)
    var = jnp.mean(jnp.square(xf - mu), axis=-1, keepdims=True)
    return ((xf - mu) * lax.rsqrt(var + LN_EPS) * g.astype(jnp.float32) + b.astype(jnp.float32)).astype(x.dtype)


def ffn_half_step(x, g, b, w_gate, w_up, w_down):
    f = (jax.nn.silu(x @ w_gate) * (x @ w_up)) @ w_down
    return layer_norm(ALPHA * x + 0.5 * f, g, b)


def rope(x, pos):
    half = HEAD_DIM // 2
    freqs = ROPE_THETA ** (-jnp.arange(half, dtype=jnp.float32) / half)
    ang = pos.astype(jnp.float32)[:, None] * freqs[None, :]
    shape = (1, pos.shape[0]) + (1,) * (x.ndim - 3) + (half,)
    cos = jnp.cos(ang).reshape(shape)
    sin = jnp.sin(ang).reshape(shape)
    xf = x.astype(jnp.float32)
    x1, x2 = xf[..., :half], xf[..., half:]
    return jnp.concatenate([x1 * cos - x2 * sin, x2 * cos + x1 * sin], axis=-1).astype(x.dtype)


def masked_softmax(s, mask):
    s = jnp.where(mask, s.astype(jnp.float32), -jnp.inf)
    m = jnp.max(s, axis=-1, keepdims=True)
    m = jnp.where(jnp.isfinite(m), m, 0.0)
    e = jnp.where(mask, jnp.exp(s - m), 0.0)
    d = jnp.sum(e, axis=-1, keepdims=True)
    return e / jnp.where(d > 0, d, 1.0)


def split_in_proj(h, w_in, pos):
    B, T, _ = h.shape
    z = h @ w_in
    sizes = (Q_W, 6 * KV_W, 3 * N_HEADS, D_RNN, D_RNN, D_MODEL, D_MODEL)
    cuts = [int(c) for c in np.cumsum(sizes)[:-1]]
    q, kv, ng, xr, gr, ga, gb = jnp.split(z, cuts, axis=-1)
    q = q.reshape(B, T, N_KV, HPG, HEAD_DIM)
    kv = kv.reshape(B, T, 6, N_KV, HEAD_DIM)
    q_rot = rope(q, pos)
    cmp_kv = kv[:, :, 0:2]
    sel_kv = jnp.stack([rope(kv[:, :, 2], pos), kv[:, :, 3]], axis=2)
    win_kv = jnp.stack([rope(kv[:, :, 4], pos), kv[:, :, 5]], axis=2)
    gates = jax.nn.sigmoid(ng.astype(jnp.float32)).reshape(B, T, N_KV, HPG, 3)
    return q, q_rot, gates, cmp_kv, sel_kv, win_kv, xr, gr, ga, gb


def compress(rows, pe, w1, w2):
    B, L, G, Dh = rows.shape
    nch = L // CMP_STRIDE
    nc = nch - CMP_RATIO + 1
    ch = rows[:, :nch * CMP_STRIDE].reshape(B, nch, CMP_STRIDE, G, Dh)
    ch = jnp.swapaxes(ch, 2, 3).reshape(B, nch, G, CMP_STRIDE * Dh)
    w1r = w1.reshape(CMP_RATIO, CMP_STRIDE * Dh, CMP_HID)
    u = pe.reshape(-1) @ w1
    for r in range(CMP_RATIO):
        u = u + jnp.einsum('bcgf,fh->bcgh', ch[:, r:r + nc], w1r[r])
    return jax.nn.gelu(u) @ w2


def to_blocks(rows):
    B, L, G, Dh = rows.shape
    nsel = -(-L // SEL_BLOCK)
    rows = jnp.pad(rows, ((0, 0), (0, nsel * SEL_BLOCK - L), (0, 0), (0, 0)))
    return rows.reshape(B, nsel, SEL_BLOCK, G, Dh).transpose(0, 3, 1, 2, 4)


def cover_matrix(nc, nsel):
    start = jnp.arange(nc)[:, None] * CMP_STRIDE
    j = jnp.arange(nsel)[None, :]
    return ((start < (j + 1) * SEL_BLOCK) & (start + CMP_BLOCK > j * SEL_BLOCK)).astype(jnp.float32)


def nsa_attend(q_raw, q_rot, gates, q_pos, ck, cv, kb, vb, kw, vw, kw_pos):
    B, T = q_raw.shape[:2]
    scale = HEAD_DIM ** -0.5
    nc, nsel = ck.shape[1], kb.shape[2]
    c_end = jnp.arange(nc) * CMP_STRIDE + CMP_BLOCK - 1
    s_c = jnp.einsum('btghd,bcgd->bghtc', q_raw, ck) * scale
    p_c = masked_softmax(s_c, c_end[None, :] <= q_pos[:, None])
    o_c = jnp.einsum('bghtc,bcgd->btghd', p_c, cv)
    imp = jnp.einsum('bghtc,cj->bgtj', p_c, cover_matrix(nc, nsel))
    j = jnp.arange(nsel)[None, :]
    cur = (q_pos // SEL_BLOCK)[:, None]
    valid = j * SEL_BLOCK <= q_pos[:, None]
    forced = (j == 0) | (j == cur) | (j == cur - 1)
    score = jnp.where(valid, imp + jnp.where(forced, SEL_BONUS, 0.0), -jnp.inf)
    _, idx = lax.top_k(score, min(N_SEL, nsel))
    n = idx.shape[-1]
    bi = jnp.arange(B)[:, None, None, None]
    gi = jnp.arange(N_KV)[None, :, None, None]
    ks = kb[bi, gi, idx]
    vs = vb[bi, gi, idx]
    kpos = idx[..., None] * SEL_BLOCK + jnp.arange(SEL_BLOCK)
    s_s = jnp.einsum('btghd,bgtnkd->bghtnk', q_rot, ks) * scale
    m_s = (kpos <= q_pos[None, None, :, None, None])[:, :, None]
    p_s = masked_softmax(s_s.reshape(B, N_KV, HPG, T, n * SEL_BLOCK), m_s.reshape(B, N_KV, 1, T, n * SEL_BLOCK))
    o_s = jnp.einsum('bghtnk,bgtnkd->btghd', p_s.reshape(B, N_KV, HPG, T, n, SEL_BLOCK), vs)
    rel = q_pos[:, None] - kw_pos[None, :]
    m_w = (rel >= 0) & (rel <= WINDOW) & (kw_pos[None, :] >= 0)
    s_w = jnp.einsum('btghd,bsgd->bghts', q_rot, kw) * scale
    p_w = masked_softmax(s_w, m_w)
    o_w = jnp.einsum('bghts,bsgd->btghd', p_w, vw)
    return gates[..., 0:1] * o_c + gates[..., 1:2] * o_s + gates[..., 2:3] * o_w


def nsa_prompt(q_raw, q_rot, gates, cmp_kv, sel_kv, win_kv, pe_k, w1_k, w2_k, pe_v, w1_v, w2_v):
    B, T = q_raw.shape[:2]
    ck = compress(cmp_kv[:, :, 0], pe_k, w1_k, w2_k)
    cv = compress(cmp_kv[:, :, 1], pe_v, w1_v, w2_v)
    kb = to_blocks(sel_kv[:, :, 0])
    vb = to_blocks(sel_kv[:, :, 1])
    win_pad = jnp.pad(win_kv, ((0, 0), (WINDOW, 0), (0, 0), (0, 0), (0, 0)))

    def one_block(i):
        s0 = i * Q_BLOCK
        sl = lambda a: lax.dynamic_slice_in_dim(a, s0, Q_BLOCK, axis=1)
        wkv = lax.dynamic_slice_in_dim(win_pad, s0, WINDOW + Q_BLOCK, axis=1)
        q_pos = s0 + jnp.arange(Q_BLOCK)
        w_pos = s0 - WINDOW + jnp.arange(WINDOW + Q_BLOCK)
        return nsa_attend(sl(q_raw), sl(q_rot), sl(gates), q_pos, ck, cv, kb, vb,
                          wkv[:, :, 0], wkv[:, :, 1], w_pos)

    o = lax.map(one_block, jnp.arange(T // Q_BLOCK))
    return jnp.moveaxis(o, 0, 1).reshape(B, T, Q_W)


def nsa_sample(q_raw, q_rot, gates, q_pos, full_cmp, full_sel, win_all, win_pos,
               pe_k, w1_k, w2_k, pe_v, w1_v, w2_v):
    B, T = q_raw.shape[:2]
    ck = compress(full_cmp[:, :, 0], pe_k, w1_k, w2_k)
    cv = compress(full_cmp[:, :, 1], pe_v, w1_v, w2_v)
    kb = to_blocks(full_sel[:, :, 0])
    vb = to_blocks(full_sel[:, :, 1])
    o = nsa_attend(q_raw, q_rot, gates, q_pos, ck, cv, kb, vb, win_all[:, :, 0], win_all[:, :, 1], win_pos)
    return o.reshape(B, T, Q_W)


def rglru_branch(xr, conv_buf, h0, conv_w, conv_b, w_a, b_a, w_x, b_x, lam):
    B, T, _ = xr.shape
    xp = jnp.concatenate([conv_buf.astype(xr.dtype), xr], axis=1)
    xc = conv_b + sum(xp[:, k:k + T] * conv_w[k] for k in range(CONV_W))
    new_buf = xp[:, T:]
    xb = xc.reshape(B, T, RG_BLOCKS, RG_BW)
    r = jax.nn.sigmoid((jnp.einsum('btnd,nde->btne', xb, w_a).reshape(B, T, D_RNN) + b_a).astype(jnp.float32))
    i = jax.nn.sigmoid((jnp.einsum('btnd,nde->btne', xb, w_x).reshape(B, T, D_RNN) + b_x).astype(jnp.float32))
    log_a = -RG_C * r * jax.nn.softplus(-lam.astype(jnp.float32))
    a = jnp.exp(log_a)
    u = jnp.sqrt(-jnp.expm1(2.0 * log_a)) * i * xc.astype(jnp.float32)

    def step(h, au):
        a_t, u_t = au
        h = a_t * h + u_t
        return h, h

    h_last, hs = lax.scan(step, h0.astype(jnp.float32), (jnp.swapaxes(a, 0, 1), jnp.swapaxes(u, 0, 1)))
    return jnp.swapaxes(hs, 0, 1), h_last, new_buf


def mix_merge(x, o_attn, y_r, gr, ga, gb, w_br_attn, w_br_rnn, w_out, g, b):
    dt = x.dtype
    y_rnn = y_r.astype(dt) * jax.nn.gelu(gr)
    m = jax.nn.sigmoid(ga) * (o_attn.astype(dt) @ w_br_attn) + jax.nn.sigmoid(gb) * (y_rnn @ w_br_rnn)
    return layer_norm(ALPHA * x + m @ w_out, g, b)


def setup_inputs(seed: int = 0) -> dict:
    key = jax.random.key(seed)
    ks = iter(jax.random.split(key, 48))
    f32 = jnp.float32

    def nrm(shape, scale):
        return scale * jax.random.normal(next(ks), shape, f32)

    def w(shape, scale):
        return nrm((DEPTH,) + shape, scale)

    n_pages = PAST_LEN // PAGE_SIZE
    n_used = DEC_BATCH * n_pages
    n_pool = n_used + max(1, n_used // 4)
    wb = min(WINDOW, PAST_LEN)
    inp = {}
    inp['x_prompt'] = nrm((BATCH, SEQ, D_MODEL), 1.0)
    inp['x_sample'] = nrm((DEC_BATCH, DEC_SEQ, D_MODEL), 1.0)
    inp['cache_cmp_kv'] = w((n_pool, PAGE_SIZE, 2, N_KV, HEAD_DIM), 1.0)
    inp['cache_sel_kv'] = w((n_pool, PAGE_SIZE, 2, N_KV, HEAD_DIM), 1.0)
    inp['cache_win_kv'] = w((DEC_BATCH, wb, 2, N_KV, HEAD_DIM), 1.0)
    inp['state_conv'] = w((DEC_BATCH, CONV_W - 1, D_RNN), 1.0)
    inp['state_h'] = w((DEC_BATCH, D_RNN), 0.5)
    perm = jax.random.permutation(next(ks), n_pool)
    inp['page_table'] = perm[:n_used].reshape(DEC_BATCH, n_pages).astype(jnp.int32)
    inp['ffn1_w_gate'] = w((D_MODEL, D_FF), D_MODEL ** -0.5)
    inp['ffn1_w_up'] = w((D_MODEL, D_FF), D_MODEL ** -0.5)
    inp['ffn1_w_down'] = w((D_FF, D_MODEL), BETA * D_FF ** -0.5)
    inp['ln1_g'] = 1.0 + w((D_MODEL,), 0.02)
    inp['ln1_b'] = w((D_MODEL,), 0.02)
    inp['w_in'] = w((D_MODEL, IN_W), D_MODEL ** -0.5)
    inp['cmp_pe_k'] = w((CMP_BLOCK, HEAD_DIM), 0.1)
    inp['cmp_w1_k'] = w((CMP_BLOCK * HEAD_DIM, CMP_HID), (CMP_BLOCK * HEAD_DIM) ** -0.5)
    inp['cmp_w2_k'] = w((CMP_HID, HEAD_DIM), CMP_HID ** -0.5)
    inp['cmp_pe_v'] = w((CMP_BLOCK, HEAD_DIM), 0.1)
    inp['cmp_w1_v'] = w((CMP_BLOCK * HEAD_DIM, CMP_HID), (CMP_BLOCK * HEAD_DIM) ** -0.5)
    inp['cmp_w2_v'] = w((CMP_HID, HEAD_DIM), CMP_HID ** -0.5)
    inp['conv_w'] = w((CONV_W, D_RNN), CONV_W ** -0.5)
    inp['conv_b'] = w((D_RNN,), 0.02)
    inp['rg_w_a'] = w((RG_BLOCKS, RG_BW, RG_BW), RG_BW ** -0.5)
    inp['rg_b_a'] = w((D_RNN,), 0.02)
    inp['rg_w_x'] = w((RG_BLOCKS, RG_BW, RG_BW), RG_BW ** -0.5)
    inp['rg_b_x'] = w((D_RNN,), 0.02)
    u = jax.random.uniform(next(ks), (DEPTH, D_RNN), f32, minval=0.9, maxval=0.999)
    inp['rg_lam'] = jnp.log(u) - jnp.log1p(-u)
    inp['w_br_attn'] = w((Q_W, D_MODEL), BETA * Q_W ** -0.5)
    inp['w_br_rnn'] = w((D_RNN, D_MODEL), BETA * D_RNN ** -0.5)
    inp['w_out'] = w((D_MODEL, D_MODEL), BETA * D_MODEL ** -0.5)
    inp['ln2_g'] = 1.0 + w((D_MODEL,), 0.02)
    inp['ln2_b'] = w((D_MODEL,), 0.02)
    inp['ffn2_w_gate'] = w((D_MODEL, D_FF), D_MODEL ** -0.5)
    inp['ffn2_w_up'] = w((D_MODEL, D_FF), D_MODEL ** -0.5)
    inp['ffn2_w_down'] = w((D_FF, D_MODEL), BETA * D_FF ** -0.5)
    inp['ln3_g'] = 1.0 + w((D_MODEL,), 0.02)
    inp['ln3_b'] = w((D_MODEL,), 0.02)
    return inp


def reference(x_prompt, x_sample, cache_cmp_kv, cache_sel_kv, cache_win_kv, state_conv, state_h, page_table,
              ffn1_w_gate, ffn1_w_up, ffn1_w_down, ln1_g, ln1_b,
              w_in, cmp_pe_k, cmp_w1_k, cmp_w2_k, cmp_pe_v, cmp_w1_v, cmp_w2_v,
              conv_w, conv_b, rg_w_a, rg_b_a, rg_w_x, rg_b_x, rg_lam,
              w_br_attn, w_br_rnn, w_out, ln2_g, ln2_b,
              ffn2_w_gate, ffn2_w_up, ffn2_w_down, ln3_g, ln3_b):
    bp, tp, _ = x_prompt.shape
    bs, ts, _ = x_sample.shape
    pos_p = jnp.arange(tp)
    pos_s = PAST_LEN + jnp.arange(ts)
    wb = cache_win_kv.shape[2]
    win_pos_s = PAST_LEN - wb + jnp.arange(wb + ts)
    wbp = min(WINDOW, tp)
    cmp_p_l, cmp_s_l, sel_p_l, sel_s_l, win_p_l, win_s_l = [], [], [], [], [], []
    conv_p_l, conv_s_l, h_p_l, h_s_l = [], [], [], []
    xp, xs = x_prompt, x_sample
    for l in range(DEPTH):
        ffn1 = (ln1_g[l], ln1_b[l], ffn1_w_gate[l], ffn1_w_up[l], ffn1_w_down[l])
        ffn2 = (ln3_g[l], ln3_b[l], ffn2_w_gate[l], ffn2_w_up[l], ffn2_w_down[l])
        cmp_p = (cmp_pe_k[l], cmp_w1_k[l], cmp_w2_k[l], cmp_pe_v[l], cmp_w1_v[l], cmp_w2_v[l])
        rg_p = (conv_w[l], conv_b[l], rg_w_a[l], rg_b_a[l], rg_w_x[l], rg_b_x[l], rg_lam[l])
        mrg_p = (w_br_attn[l], w_br_rnn[l], w_out[l], ln2_g[l], ln2_b[l])

        hp = ffn_half_step(xp, *ffn1)
        q, qr, gt, ckv, skv, wkv, xr, gr, ga, gb = split_in_proj(hp, w_in[l], pos_p)
        o_attn = nsa_prompt(q, qr, gt, ckv, skv, wkv, *cmp_p)
        y_r, h_last, conv_new = rglru_branch(xr, jnp.zeros((bp, CONV_W - 1, D_RNN), xr.dtype),
                                             jnp.zeros((bp, D_RNN), jnp.float32), *rg_p)
        hp = mix_merge(hp, o_attn, y_r, gr, ga, gb, *mrg_p)
        xp = ffn_half_step(hp, *ffn2)
        cmp_p_l.append(ckv)
        sel_p_l.append(skv)
        win_p_l.append(wkv[:, tp - wbp:])
        conv_p_l.append(conv_new)
        h_p_l.append(h_last.astype(xp.dtype))

        hs = ffn_half_step(xs, *ffn1)
        q, qr, gt, ckv, skv, wkv, xr, gr, ga, gb = split_in_proj(hs, w_in[l], pos_s)
        past_cmp = cache_cmp_kv[l][page_table].reshape(bs, -1, 2, N_KV, HEAD_DIM).astype(ckv.dtype)
        past_sel = cache_sel_kv[l][page_table].reshape(bs, -1, 2, N_KV, HEAD_DIM).astype(skv.dtype)
        win_all = jnp.concatenate([cache_win_kv[l].astype(wkv.dtype), wkv], axis=1)
        o_attn = nsa_sample(q, qr, gt, pos_s, jnp.concatenate([past_cmp, ckv], axis=1),
                            jnp.concatenate([past_sel, skv], axis=1), win_all, win_pos_s, *cmp_p)
        y_r, h_last_s, conv_new_s = rglru_branch(xr, state_conv[l], state_h[l], *rg_p)
        hs = mix_merge(hs, o_attn, y_r, gr, ga, gb, *mrg_p)
        xs = ffn_half_step(hs, *ffn2)
        cmp_s_l.append(ckv)
        sel_s_l.append(skv)
        win_s_l.append(win_all[:, ts:])
        conv_s_l.append(conv_new_s)
        h_s_l.append(h_last_s.astype(xs.dtype))
    return (xp, xs, jnp.stack(cmp_p_l), jnp.stack(cmp_s_l), jnp.stack(sel_p_l), jnp.stack(sel_s_l),
            jnp.stack(win_p_l), jnp.stack(win_s_l), jnp.stack(conv_p_l), jnp.stack(conv_s_l),
            jnp.stack(h_p_l), jnp.stack(h_s_l))
```

```python
import contextlib
import numpy as np
import concourse.bass as bass
import concourse.mybir as mybir
from concourse.bass_utils import run_bass_kernel_spmd

F32 = mybir.dt.float32
BF16 = mybir.dt.bfloat16
I32 = mybir.dt.int32
U32 = mybir.dt.uint32
AF = mybir.ActivationFunctionType
ALU = mybir.AluOpType
AX = mybir.AxisListType

NCORES = 8
D = 1024
DFF = 2816
SEQ = 2048
NSEQ = 2
NSMP = 4
TT = 512
NT = SEQ // TT
INW = 5400
ALPHA = 2.0 ** 0.25
LN_EPS = 1e-5
C_RES = 0.5 / ALPHA
EPS2 = LN_EPS / (ALPHA * ALPHA)
PAST = 16384
NPOOL = 5120
SLOT = 4096
NSLOT = 4

C_Q, C_KV, C_NG, C_XR, C_GR, C_GA, C_GB = 0, 512, 1280, 1304, 2328, 3352, 4376


class Buf:
    __slots__ = ("name", "w", "rs", "track_w", "sem", "semcnt", "excl")

    def __init__(self, name, track_w=True):
        self.name = name
        self.excl = False
        self.w = None
        self.rs = {}
        self.track_w = track_w
        self.sem = None
        self.semcnt = 0


class Prog:
    ENG = ("pe", "act", "dve", "pool", "sp")
    EPOCH = 1 << 30

    def __init__(self, nc, es):
        self.nc = nc
        self.es = es
        self.streams = {e: [] for e in self.ENG}
        self.cur_sem = {}
        self.cnt = {}
        self.seen = {e: {} for e in self.ENG}
        self.nsem = 0
        self.all_sems = []
        for e in ("pe", "act", "dve", "pool"):
            self.cur_sem[e] = self._newsem("c_" + e)
            self.cnt[e] = 0
        self.dma_sems = []
        self.n_ins = 0

    def _newsem(self, name):
        self.nsem += 1
        return self.es.enter_context(self.nc.semaphore(f"{name}_{self.nsem}"))

    def _deps(self, r, w):
        deps = {}

        def add(ev):
            if ev is None:
                return
            s, v = ev
            if deps.get(s, 0) < v:
                deps[s] = v
        for b in r:
            add(b.w)
            if b.excl:
                for s, v in b.rs.items():
                    add((s, v))
        for b in w:
            if b.track_w:
                add(b.w)
            for s, v in b.rs.items():
                add((s, v))
        return deps

    def _emit_waits(self, eng, deps):
        own = self.cur_sem.get(eng)
        for s, v in deps.items():
            if eng == "pe" and s is own:
                continue
            if self.seen[eng].get(s, 0) >= v:
                continue
            self.seen[eng][s] = v
            self.streams[eng].append(lambda e, s=s, v=v: e.wait_ge(s, v))

    def _record(self, ev, r, w):
        s, v = ev
        for b in r:
            if b.rs.get(s, 0) < v:
                b.rs[s] = v
        for b in w:
            if b.track_w:
                b.w = ev
                b.rs = {}
            else:
                b.w = ev

    def op(self, eng, fn, r=(), w=(), signal=True):
        self.n_ins += 1
        self._emit_waits(eng, self._deps(r, w))
        sem = self.cur_sem[eng]
        if signal:
            self.cnt[eng] += 1
            v = self.cnt[eng]
            self.streams[eng].append(lambda e, fn=fn, sem=sem: fn(e).then_inc(sem, 1))
        else:
            v = self.cnt[eng] + 1
            self.streams[eng].append(lambda e, fn=fn: fn(e))
        self._record((sem, v), r, w)

    def dma(self, q, out, in_, r=(), w=(), sem=None, **kw):
        self.n_ins += 1
        self._emit_waits(q, self._deps(r, w))
        b = sem if sem is not None else w[0]
        if b.sem is None:
            b.sem = {}
        if q not in b.sem:
            b.sem[q] = [self._newsem("d_" + b.name + "_" + q), 0]
            self.dma_sems.append(b.sem[q])
        b.sem[q][1] += 16
        sem, v = b.sem[q]
        self.streams[q].append(lambda e, out=out, in_=in_, sem=sem, kw=kw: e.dma_start(out=out, in_=in_, **kw).then_inc(sem, 16))
        self._record((sem, v), r, w)

    def idma(self, out, in_, idx_ap, r=(), w=(), sem=None):
        q = "pool"
        self.n_ins += 1
        self._emit_waits(q, self._deps(r, w))
        b = sem if sem is not None else w[0]
        if b.sem is None:
            b.sem = {}
        if "idma" not in b.sem:
            b.sem["idma"] = [self._newsem("i_" + b.name), 0]
            self.dma_sems.append(b.sem["idma"])
        b.sem["idma"][1] += 16
        sm, v = b.sem["idma"]
        self.streams[q].append(lambda e: e.indirect_dma_start(out=out, out_offset=None, in_=in_, in_offset=bass.IndirectOffsetOnAxis(ap=idx_ap, axis=0)).then_inc(sm, 16))
        self._record((sm, v), r, w)

    def raw(self, eng, fn):
        self.streams[eng].append(fn)

    def barrier(self, engines=("pe", "act", "dve", "pool", "sp")):
        evs = {}
        for e in ("pe", "act", "dve", "pool"):
            if self.cnt[e] > 0:
                evs[self.cur_sem[e]] = self.cnt[e]
        for sm, v in self.dma_sems:
            evs[sm] = v
        for e in engines:
            for s, v in evs.items():
                if e == "pe" and s is self.cur_sem["pe"]:
                    continue
                if self.seen[e].get(s, 0) >= v:
                    continue
                self.seen[e][s] = v
                self.streams[e].append(lambda en, s=s, v=v: en.wait_ge(s, v))

    def replay(self):
        nc = self.nc
        with nc.Block() as block:
            @block.tensor
            def _(e):
                for f in self.streams["pe"]:
                    f(e)

            @block.scalar
            def _(e):
                for f in self.streams["act"]:
                    f(e)

            @block.vector
            def _(e):
                for f in self.streams["dve"]:
                    f(e)

            @block.gpsimd
            def _(e):
                for f in self.streams["pool"]:
                    f(e)

            @block.sync
            def _(e):
                for f in self.streams["sp"]:
                    f(e)


class T:
    def __init__(self, nc, es, name, shape, dtype, psum=False):
        self.b = Buf(name)
        if psum:
            self.b.excl = True
            self.t = es.enter_context(nc.psum_tensor(name, shape, dtype))
        else:
            self.t = es.enter_context(nc.sbuf_tensor(name, shape, dtype))

    def __getitem__(self, k):
        return self.t[k]


def _consts():
    c = {}
    half = 32
    freqs = (np.float32(10000.0) ** (-np.arange(half, dtype=np.float32) / np.float32(half))).astype(np.float32)
    pos = np.concatenate([np.arange(SEQ), [PAST]]).astype(np.float32)
    ang = (pos[:, None] * freqs[None, :]).astype(np.float32)
    c["k_cos"] = np.cos(ang).astype(np.float32)
    c["k_sin"] = np.sin(ang).astype(np.float32)
    c["k_ident"] = np.eye(128, dtype=np.float32)
    c["k_ones"] = np.full((128, 128), 1.0 / D, dtype=np.float32)
    p = np.arange(128)[:, None]
    f = np.arange(512)[None, :]
    ms = []
    for i in range(4):
        ms.append((f - p - 128 * i >= 0))
    for i in range(4):
        ms.append((f - p - 128 * i <= 0))
    for t in range(4):
        ms.append((512 * t + f - 16 * p - 31 >= 0))
    c["k_masks"] = np.stack(ms).astype(np.float32)
    x = np.arange(SEQ)[None, :]
    j = np.arange(32)[:, None]
    c["k_e32"] = (x // 64 == j).astype(np.float32)
    q = np.arange(SEQ)[:, None]
    jj = np.arange(32)[None, :]
    cur = q // 64
    valid = jj * 64 <= q
    forced = (jj == 0) | (jj == cur) | (jj == cur - 1)
    c["k_sb"] = np.where(valid, np.where(forced, 1.0e4, 0.0), -1.0e30).astype(np.float32)
    cc = np.arange(128)[:, None]
    cov = ((cc * 16 < (jj + 1) * 64) & (cc * 16 + 32 > jj * 64)) & (cc < 127)
    c["k_cover"] = cov.astype(np.float32)
    pg = np.arange(128)[:, None, None]
    jq = np.arange(8)[None, :, None]
    j2 = np.arange(256)[None, None, :]
    cs = pg * 8 + jq
    covs = ((cs * 16 < (j2 + 1) * 64) & (cs * 16 + 32 > j2 * 64)) & (cs < 1023)
    c["k_covs"] = covs.astype(np.float32)
    sm = np.zeros((128, 32), np.float32)
    sm[:, 0] = np.arange(128) % 64
    sm[:, 1] = (np.arange(128) < 64)
    sm[:, 8:16] = 1.0
    sm[127, 15] = 0.0
    sm[0:4, 16:20] = np.eye(4)
    sm[:, 20] = np.arange(128)
    c["k_small"] = sm
    c["k_iota16"] = np.tile(np.arange(16, dtype=np.float32)[None, :], (128, 1))
    sbs = np.zeros((1, 256), np.float32)
    sbs[0, 0] = 1.0e4
    sbs[0, 255] = 1.0e4
    c["k_sbs"] = sbs
    a16 = np.zeros((16, 128), np.float32)
    for j_ in range(16):
        a16[j_, (j_ % 2) * 64:(j_ % 2) * 64 + 64] = 1.0
    c["k_a16"] = a16
    m2 = np.zeros((16, 8), np.float32)
    for j_ in range(16):
        m2[j_, j_ // 2] = 1.0
    c["k_m2"] = m2
    return c


CONST_SHAPES = {"k_cos": [SEQ + 1, 32], "k_sin": [SEQ + 1, 32], "k_ident": [128, 128], "k_ones": [128, 128],
                "k_masks": [12, 128, 512], "k_e32": [32, SEQ], "k_sb": [SEQ, 32], "k_cover": [128, 32],
                "k_covs": [128, 8, 256], "k_small": [128, 32], "k_iota16": [128, 16], "k_sbs": [1, 256],
                "k_a16": [16, 128], "k_m2": [16, 8]}

IN_SHAPES = {
    "xp": ([NSEQ, SEQ, D], F32), "xs": ([NSMP, D], F32),
    "ccmp": ([NPOOL, 128, 256], F32), "csel": ([NPOOL, 128, 256], F32), "cwin": ([NSMP, 512, 256], F32),
    "sconv": ([NSMP, 3, D], F32), "sh": ([NSMP, D], F32), "ptab": ([NSMP, 128], I32),
    "ffn1_w_gate": ([D, DFF], F32), "ffn1_w_up": ([D, DFF], F32), "ffn1_w_down": ([DFF, D], F32),
    "ln1_g": ([D], F32), "ln1_b": ([D], F32), "w_in": ([D, INW], F32),
    "cmp_pe_k": ([32, 64], F32), "cmp_w1_k": ([2048, 64], F32), "cmp_w2_k": ([64, 64], F32),
    "cmp_pe_v": ([32, 64], F32), "cmp_w1_v": ([2048, 64], F32), "cmp_w2_v": ([64, 64], F32),
    "conv_w": ([4, D], F32), "conv_b": ([D], F32), "rg_w_a": ([16, 64, 64], F32), "rg_b_a": ([D], F32),
    "rg_w_x": ([16, 64, 64], F32), "rg_b_x": ([D], F32), "rg_lam": ([D], F32),
    "w_br_attn": ([512, D], F32), "w_br_rnn": ([D, D], F32), "w_out": ([D, D], F32),
    "ln2_g": ([D], F32), "ln2_b": ([D], F32),
    "ffn2_w_gate": ([D, DFF], F32), "ffn2_w_up": ([D, DFF], F32), "ffn2_w_down": ([DFF, D], F32),
    "ln3_g": ([D], F32), "ln3_b": ([D], F32),
}
OUT_SHAPES = {
    "y_p": [NSEQ, SEQ, D], "y_s": [NSMP, D], "cmp_p": [NSEQ, SEQ, 256], "cmp_s": [NSMP, 256],
    "sel_p": [NSEQ, SEQ, 256], "sel_s": [NSMP, 256], "win_p": [NSEQ, 512, 256], "win_s": [NSMP, 512, 256],
    "conv_p": [NSEQ, 3, D], "conv_s": [NSMP, 3, D], "h_p": [NSEQ, D], "h_s": [NSMP, D],
}


GROUPS = {}
PASS_ORDER = []


def _mk_groups():
    def add(name, src, k0, kc, c0, nb):
        GROUPS[name] = (src, k0, kc, c0, nb)
        PASS_ORDER.append(name)

    def ffn(f):
        for i in range(6):
            nb = 512 if i < 5 else 256
            add(f"g{f}_{i}", f"ffn{f}_w_gate", 0, 8, 512 * i, nb)
            add(f"u{f}_{i}", f"ffn{f}_w_up", 0, 8, 512 * i, nb)
        for h in range(2):
            for kh in range(3):
                add(f"d{f}_{h}{kh}", f"ffn{f}_w_down", kh * 1024, 8 if kh < 2 else 6, 512 * h, 512)
    ffn(1)
    add("in_q", "w_in", 0, 8, 0, 512)
    add("in_kv", "w_in", 0, 8, 512, 512)
    add("in_kv2", "w_in", 0, 8, 1024, 280)
    add("w1bd_k", "cmp_w1_k", 0, 32, 0, 128)
    add("w1bd_v", "cmp_w1_v", 0, 32, 0, 128)
    add("in_xr0", "w_in", 0, 8, C_XR, 512)
    add("in_gr0", "w_in", 0, 8, C_GR, 512)
    add("in_xr1", "w_in", 0, 8, C_XR + 512, 512)
    add("in_gr1", "w_in", 0, 8, C_GR + 512, 512)
    for j in range(2):
        add(f"in_ga{j}", "w_in", 0, 8, C_GA + 512 * j, 512)
        add(f"in_gb{j}", "w_in", 0, 8, C_GB + 512 * j, 512)
        add(f"ba{j}", "w_br_attn", 0, 4, 512 * j, 512)
        add(f"br{j}", "w_br_rnn", 0, 8, 512 * j, 512)
    add("wo0", "w_out", 0, 8, 0, 512)
    add("wo1", "w_out", 0, 8, 512, 512)
    ffn(2)


_mk_groups()


class TC(T):
    def __init__(self, nc, es, name, shape, dtype):
        super().__init__(nc, es, name, shape, dtype)
        self.bs = [Buf(f"{name}{i}") for i in range(shape[1])]


class CV:
    def __init__(self, parent, c0, n, name):
        self.p, self.c0, self.n = parent, c0, n
        self.bs = [Buf(f"{name}{i}") for i in range(n)]
        self.b = Buf(name)

    def __getitem__(self, k):
        p, c, t = k
        if isinstance(c, slice):
            c = slice((c.start or 0) + self.c0, (c.stop if c.stop is not None else self.n) + self.c0)
        else:
            c = c + self.c0
        return self.p.t[p, c, t]


class WPipe:
    def __init__(self, P, slots, scratch, order):
        self.P, self.slots, self.scratch, self.order = P, slots, scratch, order
        self.pos = 0
        self.issued = 0
        for _ in range(min(len(slots), len(order))):
            self._issue()

    def _issue(self):
        i = self.issued
        name = self.order[i]
        slot = self.slots[i % len(self.slots)]
        _, _, kc, _, nb = GROUPS[name]
        n = kc * nb
        self.P.dma("sp", slot[:, 0:n], self.scratch[name], w=[slot.b])
        self.issued += 1

    def acquire(self, name):
        assert self.order[self.pos] == name, (self.order[self.pos], name)
        i = self.pos
        self.pos += 1
        return i, self.slots[i % len(self.slots)]

    def release(self, i):
        j = i + len(self.slots)
        assert j == self.issued or j >= len(self.order), (j, self.issued)
        if j < len(self.order):
            self._issue()


def build(cfg):
    nc = bass.Bass("TRN2", target_bir_lowering=False)
    es = contextlib.ExitStack()
    dr = {}
    npool = cfg.get("npool", NPOOL)
    for k, (shape, dt) in IN_SHAPES.items():
        if k in ("ccmp", "csel"):
            shape = [npool] + shape[1:]
        dr[k] = nc.dram_tensor(k, shape, dt, kind="ExternalInput").ap()
    for k, shape in CONST_SHAPES.items():
        dr[k] = nc.dram_tensor(k, shape, F32, kind="ExternalInput").ap()
    out = {}
    outb = {}
    for k, shape in OUT_SHAPES.items():
        out[k] = nc.dram_tensor(k, shape, F32, kind="ExternalOutput").ap()
        outb[k] = Buf("o_" + k, track_w=False)
    scratch = {}
    for name, (src, k0, kc, c0, nb) in GROUPS.items():
        scratch[name] = nc.dram_tensor("ws_" + name, [128, kc * nb], BF16, kind="Internal").ap()
    scr_buf = Buf("scratch", track_w=False)

    P = Prog(nc, es)
    tiles = cfg["tiles"]
    do_sample = cfg.get("sample", False)
    stop_after = cfg.get("stop_after")

    identf = T(nc, es, "identf", [128, 128], F32)
    onesb = T(nc, es, "onesb", [128, 128], BF16)
    lnp = T(nc, es, "lnp", [128, 6, 8], F32)
    PS = [T(nc, es, f"ps{i}", [128, 512], F32, psum=True) for i in range(8)]

    W2BD = [T(nc, es, f"w2bd{i}", [128, 128], BF16) for i in range(2)]
    PEB = T(nc, es, "peb", [128, 2], F32)
    CVA = T(nc, es, "cva", [128, 2, 97], BF16)
    RGP = T(nc, es, "rgp", [128, 8, 8], F32)
    WAX = [T(nc, es, f"wax{i}", [128, 8, 128], BF16) for i in range(2)]
    P.dma("sp", identf[:, :], dr["k_ident"], w=[identf.b])
    for i, nm in enumerate(["ln1_g", "ln1_b", "ln2_g", "ln2_b", "ln3_g", "ln3_b"]):
        P.dma("sp", lnp[:, i, :], dr[nm].rearrange("(c p) -> p c", p=128), w=[lnp.b], allow_slow_non_contiguous=True)

    with contextlib.ExitStack() as es1:
        st = [T(nc, es1, f"pst{i}", [128, SLOT], F32) for i in range(2)]
        sb = [T(nc, es1, f"psb{i}", [128, SLOT], BF16) for i in range(2)]
        sbb = [[Buf(f"psb{i}_{j}") for j in range(3)] for i in range(2)]
        onesf = T(nc, es1, "onesf", [128, 128], F32)
        P.dma("sp", onesf[:, :], dr["k_ones"], w=[onesf.b])
        P.op("dve", lambda e: e.tensor_copy(out=onesb[:, :], in_=onesf[:, :]), r=[onesf.b], w=[onesb.b])
        cvs = T(nc, es1, "cvs", [128, 32], F32)
        P.dma("sp", cvs[:, :], dr["k_cover"], w=[cvs.b])
        P.op("pool", lambda e: e.memset(CVA[:, :, :], 1.0), w=[CVA.b])
        for g in range(2):
            P.op("dve", lambda e, g=g: e.tensor_copy(out=CVA[:, g, 65:97], in_=cvs[:, :]), r=[cvs.b], w=[CVA.b])
        pes = T(nc, es1, "pes", [128, 2, 32], F32)
        peb16 = T(nc, es1, "peb16", [128, 2, 32], BF16)
        w1s = T(nc, es1, "w1s", [128, 32, 128], F32)
        w1b = T(nc, es1, "w1b", [128, 32, 128], BF16)
        for kv, nm in enumerate(["k", "v"]):
            w2s = T(nc, es1, f"w2s{kv}", [128, 128], F32)
            P.op("pool", lambda e, w2s=w2s: e.memset(w2s[:, :], 0.0), w=[w2s.b])
            P.dma("sp", w2s[0:64, 0:64], dr[f"cmp_w2_{nm}"], w=[w2s.b])
            P.dma("sp", w2s[64:128, 64:128], dr[f"cmp_w2_{nm}"], w=[w2s.b])
            P.op("dve", lambda e, w2s=w2s, kv=kv: e.tensor_copy(out=W2BD[kv][:, :], in_=w2s[:, :]), r=[w2s.b], w=[W2BD[kv].b])
            for g in range(2):
                P.dma("sp", pes[64 * g:64 * g + 64, kv, :], dr[f"cmp_pe_{nm}"].rearrange("r d -> d r"), w=[pes.b], allow_slow_non_contiguous=True)
        P.op("dve", lambda e: e.tensor_copy(out=peb16[:, :, :], in_=pes[:, :, :]), r=[pes.b], w=[peb16.b])
        for kv, nm in enumerate(["k", "v"]):
            P.op("pool", lambda e: e.memset(w1s[:, :, :], 0.0), w=[w1s.b])
            wsrc = dr[f"cmp_w1_{nm}"].rearrange("(r d) h -> d r h", d=64)
            P.dma("sp", w1s[0:64, :, 0:64], wsrc, w=[w1s.b])
            P.dma("sp", w1s[64:128, :, 64:128], wsrc, w=[w1s.b])
            P.op("dve", lambda e: e.tensor_copy(out=w1b[:, :, :], in_=w1s[:, :, :]), r=[w1s.b], w=[w1b.b])
            for rs in range(32):
                mm_ = lambda e, rs=rs, kv=kv: e.matmul(PS[0][:, kv * 2:kv * 2 + 2], lhsT=w1b[:, rs, :], rhs=peb16[:, kv, rs:rs + 1].to_broadcast([128, 2]) if False else peb16[:, kv, rs:rs + 1], start=(rs == 0), stop=(rs == 31))
                P.op("pe", lambda e, rs=rs, kv=kv: e.matmul(PS[0][:, kv:kv + 1], lhsT=w1b[:, rs, :], rhs=peb16[:, kv, rs:rs + 1], start=(rs == 0), stop=(rs == 31)),
                     r=[w1b.b, peb16.b], w=[PS[0].b])
            P.op("dve", lambda e, kv=kv: e.tensor_copy(out=PEB[:, kv:kv + 1], in_=PS[0][:, kv:kv + 1]), r=[PS[0].b], w=[PEB.b])
        for k in range(4):
            P.dma("sp", RGP[:, k, :], dr["conv_w"][k].rearrange("(c p) -> p c", p=128), w=[RGP.b], allow_slow_non_contiguous=True)
        for k, nm in enumerate(["conv_b", "rg_b_a", "rg_b_x", "rg_lam"]):
            P.dma("sp", RGP[:, 4 + k, :], dr[nm].rearrange("(c p) -> p c", p=128), w=[RGP.b], allow_slow_non_contiguous=True)
        P.op("act", lambda e: e.activation(out=RGP[:, 7, :], in_=RGP[:, 7, :], func=AF.Exp, scale=-1.0), r=[RGP.b], w=[RGP.b])
        P.op("act", lambda e: e.activation(out=RGP[:, 7, :], in_=RGP[:, 7, :], func=AF.Ln, bias=1.0), r=[RGP.b], w=[RGP.b])
        P.op("dve", lambda e: e.tensor_scalar(out=RGP[:, 7, :], in0=RGP[:, 7, :], scalar1=-8.0, scalar2=None, op0=ALU.mult), r=[RGP.b], w=[RGP.b])
        for i, nm in enumerate(["rg_w_a", "rg_w_x"]):
            ws_ = T(nc, es1, f"waxs{i}", [128, 8, 128], F32)
            P.op("pool", lambda e, ws_=ws_: e.memset(ws_[:, :, :], 0.0), w=[ws_.b])
            wv = dr[nm].rearrange("(i j) d e -> j d i e", j=2)
            for j in range(2):
                P.dma("sp", ws_[64 * j:64 * j + 64, :, 64 * j:64 * j + 64], wv[j], w=[ws_.b])
            P.op("dve", lambda e, ws_=ws_, i=i: e.tensor_copy(out=WAX[i][:, :, :], in_=ws_[:, :, :]), r=[ws_.b], w=[WAX[i].b])
        for gi, name in enumerate(PASS_ORDER):
            src, k0, kc, c0, nb = GROUPS[name]
            n = kc * nb
            s, o, ob = st[gi % 2], sb[gi % 2], sbb[gi % 2]
            if name.startswith("w1bd"):
                P.op("pool", lambda e, s=s: e.memset(s[:, 0:4096], 0.0), w=[s.b])
                sv = s[:, 0:4096].rearrange("p (r h) -> p r h", h=128)
                wsrc = dr[src].rearrange("(r d) h -> d r h", d=64)
                P.dma("sp", sv[0:64, :, 0:64], wsrc, w=[s.b])
                P.dma("sp", sv[64:128, :, 64:128], wsrc, w=[s.b])
            else:
                P.dma("sp", s[:, 0:n].rearrange("p (k n) -> p k n", k=kc),
                      dr[src][k0:k0 + kc * 128, c0:c0 + nb].rearrange("(k p) n -> p k n", p=128), w=[s.b])
            if name == "in_q":
                for k_ in range(8):
                    eng = ("dve", "pool", "dve")[k_ % 3]
                    P.op(eng, lambda e, s=s, o=o, k_=k_: e.tensor_copy(out=o[:, k_ * 512:(k_ + 1) * 512].rearrange("p (hh j d) -> p hh j d", hh=4, j=2, d=64),
                                                                        in_=s[:, k_ * 512:(k_ + 1) * 512].rearrange("p (j hh d) -> p hh j d", hh=4, j=2, d=64)),
                         r=[s.b], w=[ob[k_ % 3]])
                P.dma("act", scratch[name], o[:, 0:n], r=ob, w=[scr_buf], sem=ob[0])
                continue
            a = (n // 3) // 2 * 2
            cuts = [0, a, 2 * a, n]
            P.op("dve", lambda e, s=s, o=o, c=cuts: e.tensor_copy(out=o[:, c[0]:c[1]], in_=s[:, c[0]:c[1]]), r=[s.b], w=[ob[0]])
            P.op("act", lambda e, s=s, o=o, c=cuts: e.activation(out=o[:, c[1]:c[2]], in_=s[:, c[1]:c[2]], func=AF.Copy), r=[s.b], w=[ob[1]])
            P.op("pool", lambda e, s=s, o=o, c=cuts: e.tensor_copy(out=o[:, c[2]:c[3]], in_=s[:, c[2]:c[3]]), r=[s.b], w=[ob[2]])
            P.dma("act", scratch[name], o[:, 0:n], r=ob, w=[scr_buf], sem=ob[0])
        P.barrier()

    R = TC(nc, es, "R", [128, 8, TT], F32)
    A = TC(nc, es, "A", [128, 8, TT], BF16)
    H = TC(nc, es, "H", [128, 22, TT], BF16)
    WS = [T(nc, es, f"wslot{i}", [128, SLOT], BF16) for i in range(NSLOT)]
    STG = [T(nc, es, f"stg{i}", [128, 1024], F32) for i in range(2)]
    TMPF = [T(nc, es, f"tmpf{i}", [128, TT], F32) for i in range(4)]
    TMPB = [T(nc, es, f"tmpb{i}", [128, TT], BF16) for i in range(4)]
    RSTD = T(nc, es, "rstd", [128, TT], F32)
    NMR = T(nc, es, "nmr", [128, TT], F32)
    CS = T(nc, es, "cs", [128, 4, 64], F32)
    ZS = [T(nc, es, f"zs{i}", [128, 512], F32) for i in range(3)]
    RT = [T(nc, es, f"rt{i}", [128, 256], F32) for i in range(4)]
    GATES = T(nc, es, "gates", [128, 4, 24], F32)
    QR = T(nc, es, "qr", [128, 4, TT], BF16)
    QW = T(nc, es, "qw", [128, 4, TT], BF16)
    GU = T(nc, es, "gu", [128, 2, 128], BF16)
    CKT = T(nc, es, "ckt", [128, 128], BF16)
    SM = [T(nc, es, f"sm{i}", [128, 160], F32) for i in range(4)]
    XH = T(nc, es, "xh", [128, 8, 3], F32)
    HST = T(nc, es, "hst", [128, 8], F32)
    XB = [T(nc, es, f"xb{i}", [128, 3 + TT], F32) for i in range(2)]
    XCB = [T(nc, es, f"xcb{i}", [128, TT], BF16) for i in range(2)]
    SG = [T(nc, es, f"sg{i}", [128, 4, TT], BF16) for i in range(2)]
    es_p = contextlib.ExitStack()
    MSK = T(nc, es_p, "msk", [128, 12, 512], BF16)
    E32 = T(nc, es_p, "e32", [32, SEQ], BF16)
    SELK = T(nc, es_p, "selk", [128, SEQ], BF16)
    WINK = T(nc, es_p, "wink", [128, SEQ], BF16)
    SELV = T(nc, es_p, "selv", [128, 16, 2, 65], BF16)
    WINV = T(nc, es_p, "winv", [128, 16, 2, 65], BF16)
    CMPT = T(nc, es_p, "cmpt", [128, 2, SEQ + 16], BF16)
    EX = [T(nc, es_p, f"ex{i}", [128, 512], BF16) for i in range(3)]
    PT = [T(nc, es_p, f"pt{i}", [128, 512], BF16) for i in range(3)]
    MKS = [T(nc, es_p, f"mks{i}", [128, 512], BF16) for i in range(2)]
    OA = [T(nc, es_p, f"oa{i}", [128, 512], F32) for i in range(4)]
    IMP = T(nc, es_p, "imp", [128, 4, 2, 32], F32)
    SBT = T(nc, es_p, "sbt", [128, 4, 32], F32)
    SELT = T(nc, es_p, "selt", [32, 2, TT], BF16)
    YR = CV(H, 0, 8, "yr")
    M = CV(H, 8, 8, "m")
    OT = CV(H, 16, 4, "ot")
    for mi in range(12):
        tm = TMPF[mi % 4]
        P.dma("sp", tm[:, 0:512], dr["k_masks"][mi], w=[tm.b])
        P.op("dve" if mi % 2 == 0 else "pool", lambda e, tm=tm, mi=mi: e.tensor_copy(out=MSK[:, mi, :], in_=tm[:, 0:512]), r=[tm.b], w=[MSK.b])
    for qi in range(4):
        tm = TMPF[qi % 4]
        P.dma("sp", tm[0:32, 0:512], dr["k_e32"][:, qi * 512:(qi + 1) * 512], w=[tm.b])
        P.op("dve", lambda e, tm=tm, qi=qi: e.tensor_copy(out=E32[:, qi * 512:(qi + 1) * 512], in_=tm[0:32, 0:512]), r=[tm.b], w=[E32.b])
    for tt_ in (SELV, WINV):
        P.op("pool", lambda e, tt_=tt_: e.memset(tt_[:, :, :, :], 1.0), w=[tt_.b])
    P.op("pool", lambda e: e.memset(CMPT[:, :, :], 0.0), w=[CMPT.b])

    npass = len(tiles) + (1 if do_sample else 0)
    WP = WPipe(P, WS, scratch, PASS_ORDER * npass)
    ctr = {"stg": 0, "tf": 0, "tb": 0, "zs": 0, "rt": 0, "ex": 0, "pt": 0, "mks": 0, "sm": 0, "xb": 0, "xcb": 0}

    def nxt(key, lst):
        i = ctr[key]
        ctr[key] += 1
        return lst[i % len(lst)]

    def mm(ps_ap, lhsT, rhs, start, stop, r, w, **kw):
        P.op("pe", lambda e: e.matmul(ps_ap, lhsT=lhsT, rhs=rhs, start=start, stop=stop, **kw), r=r, w=w, signal=True)

    def load_x(src_rows_fn, ntok):
        nsub = (ntok + 127) // 128
        for sub in range(nsub):
            m = min(128, ntok - sub * 128)
            stg = nxt("stg", STG)
            P.dma("sp", stg[0:m, :], src_rows_fn(sub, m), w=[stg.b])
            for half in range(2):
                ps = PS[half]
                for j in range(4):
                    c = half * 4 + j
                    P.op("pe", lambda e, ps=ps, stg=stg, j=j, c=c, m=m: e.transpose(out=ps[:, j * 128:j * 128 + m], in_=stg[0:m, c * 128:(c + 1) * 128], identity=identf[0:m, 0:m]),
                         r=[stg.b, identf.b], w=[ps.b], signal=True)
                src = ps[:, :].rearrange("p (j t) -> p j t", j=4)[:, :, 0:m]
                P.op("dve", lambda e, src=src, half=half, sub=sub, m=m: e.tensor_copy(out=R[:, half * 4:half * 4 + 4, sub * 128:sub * 128 + m], in_=src),
                     r=[ps.b], w=R.bs[half * 4:half * 4 + 4])
                P.op("act", lambda e, src=src, half=half, sub=sub, m=m: e.activation(out=A[:, half * 4:half * 4 + 4, sub * 128:sub * 128 + m], in_=src, func=AF.Copy),
                     r=[ps.b], w=A.bs[half * 4:half * 4 + 4])

    def store_y(dst_rows_fn, ntok, ob):
        nsub = (ntok + 127) // 128
        for sub in range(nsub):
            m = min(128, ntok - sub * 128)
            stg = nxt("stg", STG)
            for half in range(2):
                ps = PS[half]
                for j in range(4):
                    c = half * 4 + j
                    P.op("pe", lambda e, ps=ps, j=j, c=c, m=m, sub=sub: e.transpose(out=ps[0:m, j * 128:(j + 1) * 128], in_=R[:, c, sub * 128:sub * 128 + m], identity=identf[:, :]),
                         r=[R.bs[c], identf.b], w=[ps.b], signal=True)
                if half == 0:
                    P.op("dve", lambda e, ps=ps, stg=stg, m=m: e.tensor_copy(out=stg[0:m, 0:512], in_=ps[0:m, :]), r=[ps.b], w=[stg.b])
                else:
                    P.op("act", lambda e, ps=ps, stg=stg, m=m: e.activation(out=stg[0:m, 512:1024], in_=ps[0:m, :], func=AF.Copy), r=[ps.b], w=[stg.b])
            P.dma("act", dst_rows_fn(sub, m), stg[0:m, :], r=[stg.b], w=[ob], sem=stg.b)

    def layernorm(li, ntok):
        n = ntok
        pm, pq = PS[2], PS[3]
        for c in range(8):
            zb = nxt("tb", TMPB)
            sq = nxt("tb", TMPB)
            P.op("act", lambda e, zb=zb, c=c: e.activation(out=zb[:, 0:n], in_=R[:, c, 0:n], func=AF.Copy), r=[R.bs[c]], w=[zb.b])
            P.op("pool", lambda e, sq=sq, c=c: e.tensor_tensor(out=sq[:, 0:n], in0=R[:, c, 0:n], in1=R[:, c, 0:n], op=ALU.mult), r=[R.bs[c]], w=[sq.b])
            mm(pm[:, 0:n], onesb[:, :], zb[:, 0:n], c == 0, c == 7, [onesb.b, zb.b], [pm.b])
            mm(pq[:, 0:n], onesb[:, :], sq[:, 0:n], c == 0, c == 7, [onesb.b, sq.b], [pq.b])
        t1 = nxt("tf", TMPF)
        t2 = nxt("tf", TMPF)
        P.op("act", lambda e: e.activation(out=t1[:, 0:n], in_=pm[:, 0:n], func=AF.Square), r=[pm.b], w=[t1.b])
        P.op("dve", lambda e: e.tensor_tensor(out=t2[:, 0:n], in0=pq[:, 0:n], in1=t1[:, 0:n], op=ALU.subtract), r=[pq.b, t1.b], w=[t2.b])
        P.op("dve", lambda e: e.tensor_scalar(out=t2[:, 0:n], in0=t2[:, 0:n], scalar1=EPS2, scalar2=None, op0=ALU.add), r=[t2.b], w=[t2.b])
        P.op("act", lambda e: e.activation(out=t2[:, 0:n], in_=t2[:, 0:n], func=AF.Sqrt), r=[t2.b], w=[t2.b])
        P.op("dve", lambda e: e.reciprocal(out=RSTD[:, 0:n], in_=t2[:, 0:n]), r=[t2.b], w=[RSTD.b])
        P.op("dve", lambda e: e.scalar_tensor_tensor(out=NMR[:, 0:n], in0=pm[:, 0:n], scalar=-1.0, in1=RSTD[:, 0:n], op0=ALU.mult, op1=ALU.mult),
             r=[pm.b, RSTD.b], w=[NMR.b])
        for c in range(8):
            t = nxt("tf", TMPF)
            eng = "dve" if c % 2 == 0 else "pool"
            P.op(eng, lambda e, t=t, c=c: e.tensor_tensor(out=t[:, 0:n], in0=R[:, c, 0:n], in1=RSTD[:, 0:n], op=ALU.mult), r=[R.bs[c], RSTD.b], w=[t.b])
            P.op(eng, lambda e, t=t: e.tensor_tensor(out=t[:, 0:n], in0=t[:, 0:n], in1=NMR[:, 0:n], op=ALU.add), r=[t.b, NMR.b], w=[t.b])
            P.op("dve", lambda e, t=t, c=c: e.tensor_scalar(out=R[:, c, 0:n], in0=t[:, 0:n], scalar1=lnp[:, 2 * li, c:c + 1], scalar2=lnp[:, 2 * li + 1, c:c + 1], op0=ALU.mult, op1=ALU.add),
                 r=[t.b, lnp.b], w=[R.bs[c]])
            P.op("act", lambda e, c=c: e.activation(out=A[:, c, 0:n], in_=R[:, c, 0:n], func=AF.Copy), r=[R.bs[c]], w=[A.bs[c]])

    def ffn(f, ntok):
        n = ntok
        for i in range(6):
            nb = 512 if i < 5 else 256
            ig, sg = WP.acquire(f"g{f}_{i}")
            iu, su = WP.acquire(f"u{f}_{i}")
            for j in range(nb // 128):
                c = i * 4 + j
                pg, pu = PS[c % 2], PS[2 + c % 2]
                for kc in range(8):
                    mm(pg[:, 0:n], sg[:, kc * nb + j * 128:kc * nb + (j + 1) * 128], A[:, kc, 0:n], kc == 0, kc == 7, [sg.b, A.bs[kc]], [pg.b])
                for kc in range(8):
                    mm(pu[:, 0:n], su[:, kc * nb + j * 128:kc * nb + (j + 1) * 128], A[:, kc, 0:n], kc == 0, kc == 7, [su.b, A.bs[kc]], [pu.b])
                t = nxt("tf", TMPF)
                P.op("act", lambda e, t=t, pg=pg: e.activation(out=t[:, 0:n], in_=pg[:, 0:n], func=AF.Silu), r=[pg.b], w=[t.b])
                P.op("dve", lambda e, t=t, pu=pu, c=c: e.tensor_tensor(out=H[:, c, 0:n], in0=t[:, 0:n], in1=pu[:, 0:n], op=ALU.mult), r=[t.b, pu.b], w=[H.bs[c]])
            WP.release(ig)
            WP.release(iu)
        for h in range(2):
            banks = PS[4:8]
            for kh in range(3):
                idd, sd = WP.acquire(f"d{f}_{h}{kh}")
                for j in range(4):
                    for kc in range(8 if kh < 2 else 6):
                        kg = kh * 8 + kc
                        mm(banks[j][:, 0:n], sd[:, kc * 512 + j * 128:kc * 512 + (j + 1) * 128], H[:, kg, 0:n], kg == 0, kg == 21, [sd.b, H.bs[kg]], [banks[j].b])
                WP.release(idd)
            for j in range(4):
                c = h * 4 + j
                eng = "dve" if j % 2 == 0 else "pool"
                if eng == "pool":
                    t = nxt("tf", TMPF)
                    P.op("act", lambda e, t=t, j=j: e.activation(out=t[:, 0:n], in_=banks[j][:, 0:n], func=AF.Copy, scale=C_RES), r=[banks[j].b], w=[t.b])
                    P.op("pool", lambda e, t=t, c=c: e.tensor_tensor(out=R[:, c, 0:n], in0=R[:, c, 0:n], in1=t[:, 0:n], op=ALU.add), r=[t.b, R.bs[c]], w=[R.bs[c]])
                else:
                    P.op("dve", lambda e, j=j, c=c: e.scalar_tensor_tensor(out=R[:, c, 0:n], in0=banks[j][:, 0:n], scalar=C_RES, in1=R[:, c, 0:n], op0=ALU.mult, op1=ALU.add),
                         r=[banks[j].b, R.bs[c]], w=[R.bs[c]])

    def evac(eng, out_ap, in_ap, r, w, **kw):
        if eng == "act":
            P.op("act", lambda e: e.activation(out=out_ap, in_=in_ap, func=AF.Copy, **kw), r=r, w=w)
        else:
            P.op(eng, lambda e: e.tensor_copy(out=out_ap, in_=in_ap), r=r, w=w)

    def tt(eng, out_ap, a, b, op, r, w):
        P.op(eng, lambda e: e.tensor_tensor(out=out_ap, in0=a, in1=b, op=op), r=r, w=w)

    def rope_inplace(zs, view, cos, sin, shape):
        nd = len(shape)
        x1 = view[(slice(None),) * nd + (slice(0, 32),)]
        x2 = view[(slice(None),) * nd + (slice(32, 64),)]
        m = shape[0]
        nel = int(np.prod(shape[1:])) * 32
        tshape = list(shape) + [32]
        c, s_ = cos, sin
        for _ in range(nd - 1):
            c = c.unsqueeze(1)
            s_ = s_.unsqueeze(1)
        cb = c.to_broadcast(tshape)
        sb_ = s_.to_broadcast(tshape)
        ts = [nxt("rt", RT) for _ in range(4)]

        def tv(t_):
            v = t_[0:m, 0:nel]
            if nd == 2:
                return v.rearrange("p (a d) -> p a d", d=32)
            return v.rearrange("p (a b d) -> p a b d", a=shape[1], d=32)
        tt("dve", tv(ts[0]), x1, cb, ALU.mult, [zs.b, CS.b], [ts[0].b])
        tt("pool", tv(ts[1]), x2, sb_, ALU.mult, [zs.b, CS.b], [ts[1].b])
        tt("dve", tv(ts[2]), x2, cb, ALU.mult, [zs.b, CS.b], [ts[2].b])
        tt("pool", tv(ts[3]), x1, sb_, ALU.mult, [zs.b, CS.b], [ts[3].b])
        tt("dve", x1, tv(ts[0]), tv(ts[1]), ALU.subtract, [ts[0].b, ts[1].b], [zs.b])
        tt("pool", x2, tv(ts[2]), tv(ts[3]), ALU.add, [ts[2].b, ts[3].b], [zs.b])

    prev_post = [None]

    def win_tok_prompt(seq, t):
        P.dma("sp", CS[:, :, 0:32], dr["k_cos"][t * TT:(t + 1) * TT].rearrange("(s p) d -> p s d", p=128), w=[CS.b])
        P.dma("sp", CS[:, :, 32:64], dr["k_sin"][t * TT:(t + 1) * TT].rearrange("(s p) d -> p s d", p=128), w=[CS.b])
        iq, sq = WP.acquire("in_q")
        for sub in range(4):
            ps = PS[sub % 2]
            for kc in range(8):
                mm(ps[:, :], A[:, kc, sub * 128:(sub + 1) * 128], sq[:, kc * 512:(kc + 1) * 512], kc == 0, kc == 7, [A.bs[kc], sq.b], [ps.b])
            zs = nxt("zs", ZS)
            evac("act", zs[:, :], ps[:, :], [ps.b], [zs.b])
            rope_inplace(zs, zs[:, :].rearrange("p (h d) -> p h d", d=64), CS[:, sub, 0:32], CS[:, sub, 32:64], [128, 8])
            def post(sub=sub, zs=zs):
                pt_ = PS[2 + sub % 2]
                for hh in range(4):
                    P.op("pe", lambda e, pt_=pt_, zs=zs, hh=hh: e.transpose(out=pt_[:, hh * 128:(hh + 1) * 128], in_=zs[:, hh * 128:(hh + 1) * 128], identity=identf[:, :]),
                         r=[zs.b, identf.b], w=[pt_.b])
                evac("act", QR[:, :, sub * 128:(sub + 1) * 128], pt_[:, :].rearrange("p (h t) -> p h t", h=4), [pt_.b], [QR.b])
            if prev_post[0] is not None:
                prev_post[0]()
            prev_post[0] = post
        prev_post[0]()
        prev_post[0] = None
        for hh in range(4):
            pf = PS[4 + hh % 2]
            for kc in range(8):
                lw = sq[:, kc * 512 + hh * 128:kc * 512 + (hh + 1) * 128]
                mm(pf[:, :], lw, A[:, kc, :], kc == 0, kc == 7, [sq.b, A.bs[kc]], [pf.b])
            evac("dve", QW[:, hh, :], pf[:, :], [pf.b], [QW.b])
        WP.release(iq)
        ikv, skv = WP.acquire("in_kv")
        for sub in range(4):
            ps = PS[sub % 2]
            row0 = t * TT + sub * 128
            for kc in range(8):
                mm(ps[:, :], A[:, kc, sub * 128:(sub + 1) * 128], skv[:, kc * 512:(kc + 1) * 512], kc == 0, kc == 7, [A.bs[kc], skv.b], [ps.b])
            zs = nxt("zs", ZS)
            evac("act", zs[:, :], ps[:, :], [ps.b], [zs.b])
            P.dma("act", out["cmp_p"][seq, row0:row0 + 128, :], zs[:, 0:256], r=[zs.b], w=[outb["cmp_p"]], sem=zs.b)
            rope_inplace(zs, zs[:, 256:384].rearrange("p (h d) -> p h d", d=64), CS[:, sub, 0:32], CS[:, sub, 32:64], [128, 2])
            P.dma("act", out["sel_p"][seq, row0:row0 + 128, :], zs[:, 256:512], r=[zs.b], w=[outb["sel_p"]], sem=zs.b)
            def post(sub=sub, zs=zs, row0=row0):
                pt_ = PS[2 + sub % 2]
                P.op("pe", lambda e, pt_=pt_, zs=zs: e.transpose(out=pt_[:, 0:128], in_=zs[:, 256:384], identity=identf[:, :]), r=[zs.b, identf.b], w=[pt_.b])
                evac("dve", SELK[:, row0:row0 + 128], pt_[:, 0:128], [pt_.b], [SELK.b])
                evac("pool", SELV[:, 4 * t + sub, :, 0:64], zs[:, 384:512].rearrange("p (g d) -> p g d", g=2), [zs.b], [SELV.b])
            if prev_post[0] is not None:
                prev_post[0]()
            prev_post[0] = post
        prev_post[0]()
        prev_post[0] = None
        for kv in range(2):
            pf = PS[4 + kv]
            for kc in range(8):
                mm(pf[:, :], skv[:, kc * 512 + kv * 128:kc * 512 + (kv + 1) * 128], A[:, kc, :], kc == 0, kc == 7, [skv.b, A.bs[kc]], [pf.b])
            evac("dve" if kv == 0 else "act", CMPT[:, kv, t * TT:(t + 1) * TT], pf[:, :], [pf.b], [CMPT.b])
        WP.release(ikv)
        ik2, sk2 = WP.acquire("in_kv2")
        for sub in range(4):
            ps = PS[sub % 2]
            for kc in range(8):
                mm(ps[:, 0:280], A[:, kc, sub * 128:(sub + 1) * 128], sk2[:, kc * 280:(kc + 1) * 280], kc == 0, kc == 7, [A.bs[kc], sk2.b], [ps.b])
            zs = nxt("zs", ZS)
            evac("act", zs[:, 0:256], ps[:, 0:256], [ps.b], [zs.b])
            P.op("act", lambda e, ps=ps, sub=sub: e.activation(out=GATES[:, sub, :], in_=ps[:, 256:280], func=AF.Sigmoid), r=[ps.b], w=[GATES.b])
            rope_inplace(zs, zs[:, 0:128].rearrange("p (h d) -> p h d", d=64), CS[:, sub, 0:32], CS[:, sub, 32:64], [128, 2])
            if t == NT - 1:
                P.dma("act", out["win_p"][seq, sub * 128:(sub + 1) * 128, :], zs[:, 0:256], r=[zs.b], w=[outb["win_p"]], sem=zs.b)
            def post(sub=sub, zs=zs):
                pt_ = PS[2 + sub % 2]
                P.op("pe", lambda e, pt_=pt_, zs=zs: e.transpose(out=pt_[:, 0:128], in_=zs[:, 0:128], identity=identf[:, :]), r=[zs.b, identf.b], w=[pt_.b])
                row0 = t * TT + sub * 128
                evac("dve", WINK[:, row0:row0 + 128], pt_[:, 0:128], [pt_.b], [WINK.b])
                evac("pool", WINV[:, 4 * t + sub, :, 0:64], zs[:, 128:256].rearrange("p (g d) -> p g d", g=2), [zs.b], [WINV.b])
            if prev_post[0] is not None:
                prev_post[0]()
            prev_post[0] = post
        prev_post[0]()
        prev_post[0] = None
        WP.release(ik2)

    def compress(ncol, cmpt_rhs_fn):
        for kv in range(2):
            iw, sw = WP.acquire("w1bd_k" if kv == 0 else "w1bd_v")
            pu = PS[6 + kv]
            for rs in range(32):
                mm(pu[:, 0:ncol], sw[:, rs * 128:(rs + 1) * 128], cmpt_rhs_fn(kv, rs), rs == 0, rs == 31, [sw.b, CMPT.b], [pu.b])
            WP.release(iw)
            P.op("act", lambda e, pu=pu, kv=kv: e.activation(out=GU[:, kv, 0:ncol], in_=pu[:, 0:ncol], func=AF.Gelu_apprx_tanh, bias=PEB[:, kv:kv + 1]),
                 r=[pu.b, PEB.b], w=[GU.b])

    def compress_prompt():
        compress(128, lambda kv, rs: CMPT[:, kv, rs:rs + 2033:16])
        mm(PS[4][:, 0:128], W2BD[0][:, :], GU[:, 0, :], True, True, [W2BD[0].b, GU.b], [PS[4].b])
        evac("dve", CKT[:, :], PS[4][:, 0:128], [PS[4].b], [CKT.b])
        mm(PS[5][:, 0:128], GU[:, 1, :], W2BD[1][:, :], True, True, [W2BD[1].b, GU.b], [PS[5].b])
        evac("act", CVA[:, :, 0:64], PS[5][:, 0:128].rearrange("p (g d) -> p g d", g=2), [PS[5].b], [CVA.b])

    def attn_evac(g, br, ncol, first_branch, imp_sub=None):
        for sub in range(4):
            pv = PS[sub][:, 0:4 * ncol].rearrange("p (h c) -> p h c", c=ncol)
            sm = nxt("sm", SM)
            rs_ = sm[:, 0:4]
            rg = sm[:, 4:8]
            P.op("dve", lambda e, pv=pv, rs_=rs_: e.tensor_scalar(out=rs_.unsqueeze(2), in0=pv[:, :, 64:65], scalar1=1e-30, scalar2=None, op0=ALU.max), r=[PS[sub].b], w=[sm.b])
            P.op("dve", lambda e, rs_=rs_: e.reciprocal(out=rs_, in_=rs_), r=[sm.b], w=[sm.b])
            gv = GATES[:, sub, :].rearrange("p (g h b) -> p g h b", g=2, h=4, b=3)[:, g, :, br]
            tt("dve", rg, rs_, gv, ALU.mult, [sm.b, GATES.b], [sm.b])
            oav = OA[sub][:, g * 256:(g + 1) * 256].rearrange("p (h d) -> p h d", d=64)
            rgb = rg.unsqueeze(2).to_broadcast([128, 4, 64])
            if first_branch:
                tt("dve", oav, pv[:, :, 0:64], rgb, ALU.mult, [PS[sub].b, sm.b], [OA[sub].b])
            else:
                t_ = nxt("tf", TMPF)
                tv = t_[:, 0:256].rearrange("p (h d) -> p h d", d=64)
                tt("dve", tv, pv[:, :, 0:64], rgb, ALU.mult, [PS[sub].b, sm.b], [t_.b])
                tt("pool", oav, oav, tv, ALU.add, [t_.b, OA[sub].b], [OA[sub].b])
            if ncol == 97:
                iv = sm[:, 8:136].rearrange("p (h j) -> p h j", j=32)
                tt("dve", iv, pv[:, :, 65:97], rs_.unsqueeze(2).to_broadcast([128, 4, 32]), ALU.mult, [PS[sub].b, sm.b], [sm.b])
                P.op("dve", lambda e, iv=iv, sub=sub: e.tensor_reduce(out=IMP[:, sub, g, :], in_=iv.rearrange("p h j -> p j h"), axis=AX.X, op=ALU.add), r=[sm.b], w=[IMP.b])

    pend = []
    kctr = [0]
    SBANK = (4, 5, 7)

    def flush_pv(keep=0):
        while len(pend) > keep:
            pend.pop(0)()

    def attn_chunk(g, hh, kT, q, mask, mbufs, vaug, ncol, subs, first, scale=0.125):
        hp = slice(64 * g, 64 * g + 64)
        S = PS[SBANK[kctr[0] % 3]]
        kctr[0] += 1
        mm(S[:, :], kT[0][hp, kT[1]], q[hp, hh, :], True, True, [kT[2], q.b], [S.b])
        ex = nxt("ex", EX)
        P.op("act", lambda e: e.activation(out=ex[:, :], in_=S[:, :], func=AF.Exp, scale=scale), r=[S.b], w=[ex.b])
        if mask is not None:
            pt = nxt("pt", PT)
            tt("dve" if hh % 2 == 0 else "pool", pt[:, :], ex[:, :], mask, ALU.mult, [ex.b] + mbufs, [pt.b])
        else:
            pt = ex
        vb = list(vaug_b[0])
        subs = list(subs)

        def pv():
            for sub in subs:
                P.op("pe", lambda e, sub=sub, st=first[sub]: e.matmul(PS[sub][:, hh * ncol:(hh + 1) * ncol], lhsT=pt[:, sub * 128:(sub + 1) * 128], rhs=vaug, start=st, stop=True, skip_group_check=True),
                     r=[pt.b] + vb, w=[PS[sub].b])
                first[sub] = False
        flush_pv(keep=1)
        pend.append(pv)

    vaug_b = [[]]

    def attention_prompt(seq, t):
        P.dma("sp", SBT[:, :, :], dr["k_sb"][t * TT:(t + 1) * TT].rearrange("(s p) j -> p s j", p=128), w=[SBT.b])
        for g in range(2):
            first = [True] * 4
            vaug_b[0] = [CVA.b]
            for hh in range(4):
                attn_chunk(g, hh, (CKT, slice(0, 128), CKT.b), QW, MSK[:, 8 + t, :], [MSK.b], CVA[:, g, :], 97, range(4), first)
            flush_pv()
            attn_evac(g, 0, 97, True)
            if t >= 2:
                for sub in range(4):
                    sm = nxt("sm", SM)
                    sc, sc2, m1, m2, fl = sm[:, 0:32], sm[:, 32:64], sm[:, 64:72], sm[:, 72:80], sm[:, 96:128]
                    tt("dve", sc, IMP[:, sub, g, :], SBT[:, sub, :], ALU.add, [IMP.b, SBT.b], [sm.b])
                    P.op("dve", lambda e, sc=sc, m1=m1: e.max(out=m1, in_=sc), r=[sm.b], w=[sm.b])
                    P.op("dve", lambda e, sc=sc, sc2=sc2, m1=m1: e.match_replace(out=sc2, in_to_replace=m1, in_values=sc, imm_value=-1.0e30), r=[sm.b], w=[sm.b])
                    P.op("dve", lambda e, sc2=sc2, m2=m2: e.max(out=m2, in_=sc2), r=[sm.b], w=[sm.b])
                    P.op("dve", lambda e, sc=sc, m2=m2, fl=fl: e.tensor_scalar(out=fl, in0=sc, scalar1=m2[:, 7:8], scalar2=None, op0=ALU.is_ge), r=[sm.b], w=[sm.b])
                    P.op("pe", lambda e, fl=fl, sub=sub: e.transpose(out=PS[7][0:32, sub * 128:(sub + 1) * 128], in_=fl, identity=identf[:, :]), r=[sm.b, identf.b], w=[PS[7].b])
                evac("act", SELT[:, g, :], PS[7][0:32, :], [PS[7].b], [SELT.b])
            first = [True] * 4
            vaug_b[0] = [WINV.b]
            for c in range(max(0, 4 * t - 4), 4 * t + 4):
                if c >= 4 * t:
                    i = c - 4 * t
                    mask, subs = MSK[:, i, :], range(i, 4)
                else:
                    i = c - (4 * t - 4)
                    mask, subs = MSK[:, 4 + i, :], range(0, i + 1)
                for hh in range(4):
                    attn_chunk(g, hh, (WINK, slice(c * 128, (c + 1) * 128), WINK.b), QR, mask, [MSK.b], WINV[:, c, g, :], 65, subs, first)
            flush_pv()
            attn_evac(g, 2, 65, False)
            first = [True] * 4
            vaug_b[0] = [SELV.b]
            for c in range(0, 4 * t + 4):
                i = c - 4 * t
                subs = range(max(i, 0), 4)
                mbufs = [MSK.b]
                if t >= 2:
                    mm(PS[6][:, :], E32[:, c * 128:(c + 1) * 128], SELT[:, g, :], True, True, [E32.b, SELT.b], [PS[6].b])
                    mk = nxt("mks", MKS)
                    if i >= 0:
                        tt("dve", mk[:, :], PS[6][:, :], MSK[:, i, :], ALU.mult, [PS[6].b, MSK.b], [mk.b])
                    else:
                        evac("act", mk[:, :], PS[6][:, :], [PS[6].b], [mk.b])
                    mask, mbufs = mk[:, :], [mk.b]
                else:
                    mask = MSK[:, i, :] if i >= 0 else None
                for hh in range(4):
                    attn_chunk(g, hh, (SELK, slice(c * 128, (c + 1) * 128), SELK.b), QR, mask, mbufs, SELV[:, c, g, :], 65, subs, first)
            flush_pv()
            attn_evac(g, 1, 65, False)
        for sub in range(4):
            for i in range(4):
                P.op("pe", lambda e, sub=sub, i=i: e.transpose(out=PS[7][:, i * 128:(i + 1) * 128], in_=OA[sub][:, i * 128:(i + 1) * 128], identity=identf[:, :]),
                     r=[OA[sub].b, identf.b], w=[PS[7].b])
            evac("act", OT[:, :, sub * 128:(sub + 1) * 128], PS[7][:, :].rearrange("p (i t) -> p i t", i=4), [PS[7].b], [OT.b])

    def rglru(n, first_tile):
        for half in range(2):
            ix, sx = WP.acquire(f"in_xr{half}")
            ig, sgr = WP.acquire(f"in_gr{half}")
            def proj(j, sx=sx, sgr=sgr):
                px, pg = PS[j % 2], PS[2 + j % 2]
                for kc in range(8):
                    mm(px[:, 0:n], sx[:, kc * 512 + j * 128:kc * 512 + (j + 1) * 128], A[:, kc, 0:n], kc == 0, kc == 7, [sx.b, A.bs[kc]], [px.b])
                for kc in range(8):
                    mm(pg[:, 0:n], sgr[:, kc * 512 + j * 128:kc * 512 + (j + 1) * 128], A[:, kc, 0:n], kc == 0, kc == 7, [sgr.b, A.bs[kc]], [pg.b])
            proj(0)
            for j in range(4):
                i = half * 4 + j
                px, pg = PS[j % 2], PS[2 + j % 2]
                xb = nxt("xb", XB)
                evac("pool", xb[:, 0:3], XH[:, i, :], [XH.b], [xb.b])
                evac("act", xb[:, 3:3 + n], px[:, 0:n], [px.b], [xb.b])
                evac("pool", XH[:, i, :], xb[:, n:n + 3], [xb.b], [XH.b])
                xc = nxt("tf", TMPF)
                e1 = "dve"
                P.op(e1, lambda e, xb=xb, xc=xc, i=i: e.tensor_scalar(out=xc[:, 0:n], in0=xb[:, 0:n], scalar1=RGP[:, 0, i:i + 1], scalar2=RGP[:, 4, i:i + 1], op0=ALU.mult, op1=ALU.add),
                     r=[xb.b, RGP.b], w=[xc.b])
                for k in range(1, 4):
                    P.op(e1, lambda e, xb=xb, xc=xc, i=i, k=k: e.scalar_tensor_tensor(out=xc[:, 0:n], in0=xb[:, k:k + n], scalar=RGP[:, k, i:i + 1], in1=xc[:, 0:n], op0=ALU.mult, op1=ALU.add),
                         r=[xb.b, RGP.b, xc.b], w=[xc.b])
                xcb = nxt("xcb", XCB)
                evac("act", xcb[:, 0:n], xc[:, 0:n], [xc.b], [xcb.b])
                if j < 3:
                    proj(j + 1)
                pr, pi = PS[4 + j % 2], PS[6 + j % 2]
                mm(pr[:, 0:n], WAX[0][:, i, :], xcb[:, 0:n], True, True, [WAX[0].b, xcb.b], [pr.b])
                mm(pi[:, 0:n], WAX[1][:, i, :], xcb[:, 0:n], True, True, [WAX[1].b, xcb.b], [pi.b])
                ra = nxt("tf", TMPF)
                ii = nxt("tf", TMPF)
                P.op("act", lambda e, ra=ra, pr=pr, i=i: e.activation(out=ra[:, 0:n], in_=pr[:, 0:n], func=AF.Sigmoid, bias=RGP[:, 5, i:i + 1]), r=[pr.b, RGP.b], w=[ra.b])
                P.op("act", lambda e, ii=ii, pi=pi, i=i: e.activation(out=ii[:, 0:n], in_=pi[:, 0:n], func=AF.Sigmoid, bias=RGP[:, 6, i:i + 1]), r=[pi.b, RGP.b], w=[ii.b])
                P.op("act", lambda e, ra=ra, i=i: e.activation(out=ra[:, 0:n], in_=ra[:, 0:n], func=AF.Exp, scale=RGP[:, 7, i:i + 1]), r=[ra.b, RGP.b], w=[ra.b])
                sq_ = nxt("tf", TMPF)
                tt("pool", sq_[:, 0:n], ra[:, 0:n], ra[:, 0:n], ALU.mult, [ra.b], [sq_.b])
                P.op("act", lambda e, sq_=sq_: e.activation(out=sq_[:, 0:n], in_=sq_[:, 0:n], func=AF.Sqrt, scale=-1.0, bias=1.0), r=[sq_.b], w=[sq_.b])
                tt("pool", ii[:, 0:n], ii[:, 0:n], sq_[:, 0:n], ALU.mult, [ii.b, sq_.b], [ii.b])
                tt("dve", ii[:, 0:n], ii[:, 0:n], xc[:, 0:n], ALU.mult, [ii.b, xc.b], [ii.b])
                P.op("dve", lambda e, sq_=sq_, ra=ra, ii=ii, i=i: e.tensor_tensor_scan(out=sq_[:, 0:n], data0=ra[:, 0:n], data1=ii[:, 0:n], initial=HST[:, i:i + 1], op0=ALU.mult, op1=ALU.add),
                     r=[ra.b, ii.b, HST.b, sq_.b], w=[sq_.b])
                evac("pool", HST[:, i:i + 1], sq_[:, n - 1:n], [sq_.b], [HST.b])
                P.op("act", lambda e, ra=ra, pg=pg: e.activation(out=ra[:, 0:n], in_=pg[:, 0:n], func=AF.Gelu_apprx_tanh), r=[pg.b, ra.b], w=[ra.b])
                tt("dve", YR[:, i, 0:n], sq_[:, 0:n], ra[:, 0:n], ALU.mult, [sq_.b, ra.b], [YR.bs[i]])
            WP.release(ix)
            WP.release(ig)

    def merge(n, ot_fn):
        for j in range(2):
            iga, sga = WP.acquire(f"in_ga{j}")
            igb, sgb = WP.acquire(f"in_gb{j}")
            for q in range(4):
                for which, sw in ((0, sga), (1, sgb)):
                    ps = PS[2 * which + q % 2]
                    for kc in range(8):
                        mm(ps[:, 0:n], sw[:, kc * 512 + q * 128:kc * 512 + (q + 1) * 128], A[:, kc, 0:n], kc == 0, kc == 7, [sw.b, A.bs[kc]], [ps.b])
                    P.op("act", lambda e, ps=ps, which=which, q=q: e.activation(out=SG[which][:, q, 0:n], in_=ps[:, 0:n], func=AF.Sigmoid), r=[ps.b], w=[SG[which].b])
            WP.release(iga)
            WP.release(igb)
            iba, sba = WP.acquire(f"ba{j}")
            ibr, sbr = WP.acquire(f"br{j}")
            for q in range(4):
                c = j * 4 + q
                pa, pb = PS[4 + q % 2], PS[6 + q % 2]
                ots = ot_fn()
                for kc, (oap, obufs, krow0, kn) in enumerate(ots):
                    mm(pa[:, 0:n], sba[krow0 % 128:krow0 % 128 + kn, (krow0 // 128) * 512 + q * 128:(krow0 // 128) * 512 + (q + 1) * 128], oap, kc == 0, kc == len(ots) - 1, [sba.b] + obufs, [pa.b])
                for kc in range(8):
                    mm(pb[:, 0:n], sbr[:, kc * 512 + q * 128:kc * 512 + (q + 1) * 128], YR[:, kc, 0:n], kc == 0, kc == 7, [sbr.b, YR.bs[kc]], [pb.b])
                t1 = nxt("tf", TMPF)
                t2 = nxt("tf", TMPF)
                tt("dve", t1[:, 0:n], pa[:, 0:n], SG[0][:, q, 0:n], ALU.mult, [pa.b, SG[0].b], [t1.b])
                tt("dve", t2[:, 0:n], pb[:, 0:n], SG[1][:, q, 0:n], ALU.mult, [pb.b, SG[1].b], [t2.b])
                tt("pool", M[:, c, 0:n], t1[:, 0:n], t2[:, 0:n], ALU.add, [t1.b, t2.b], [M.bs[c]])
            WP.release(iba)
            WP.release(ibr)
        for h in range(2):
            iw, sw = WP.acquire(f"wo{h}")
            for q in range(4):
                c = h * 4 + q
                ps = PS[q]
                for kc in range(8):
                    mm(ps[:, 0:n], sw[:, kc * 512 + q * 128:kc * 512 + (q + 1) * 128], M[:, kc, 0:n], kc == 0, kc == 7, [sw.b, M.bs[kc]], [ps.b])
                P.op("dve", lambda e, ps=ps, c=c: e.scalar_tensor_tensor(out=R[:, c, 0:n], in0=ps[:, 0:n], scalar=1.0 / ALPHA, in1=R[:, c, 0:n], op0=ALU.mult, op1=ALU.add),
                     r=[ps.b, R.bs[c]], w=[R.bs[c]])
            WP.release(iw)

    class _Stop(Exception):
        pass

    def ck(tag):
        if cfg.get("sstop") == tag:
            raise _Stop()

    def sample_pass():
        ess = contextlib.ExitStack()
        try:
            sample_body(ess)
        except _Stop:
            pass
        P.barrier()
        ess.close()

    def sample_body(ess):
        n = NSMP

        def TS(name, shape, dt):
            return T(nc, ess, "s_" + name, shape, dt)
        PTB, PTF = TS("ptb", [128, 4], I32), TS("ptf", [128, 4], F32)
        PTI, PTHL = TS("pti", [128, 4, 2], I32), TS("pthl", [128, 4, 2], BF16)
        IDXF, IDXG = TS("idxf", [128, 16], F32), TS("idxg", [128, 16], I32)
        KSM, K16 = TS("ksm", [128, 32], F32), TS("k16", [128, 16], F32)
        PGS = [TS(f"pg{i}", [128, 8 * 256], F32) for i in range(2)]
        PG = PGS[0]
        XT = [TS(f"xt{i}", [128, 2, 8, 128], BF16) for i in range(2)]
        CKS, GUS = TS("cks", [128, 1024], BF16), TS("gus", [128, 2, 1024], BF16)
        CVS = TS("cvs", [128, 8, 2, 65], BF16)
        COV = TS("cov", [128, 8, 256], BF16)
        RS = [TS(f"rs{i}", [128, 256], F32) for i in range(2)]
        KTS, VAS = TS("kts", [128, 1024], BF16), TS("vas", [128, 8, 65], BF16)
        KTW, VAW = TS("ktw", [128, 512], BF16), TS("vaw", [128, 4, 2, 65], BF16)
        KSELF, VSELF = TS("kself", [128, 2, 4], BF16), TS("vself", [4, 2, 2, 65], BF16)
        GT, OASP, OTS = TS("gt", [4, 2, 3, 4], F32), TS("oasp", [4, 4, 2, 128], F32), TS("ots", [128, 2, 4, 4], BF16)
        EXS, PTS = TS("exs", [128, 8, 4], F32), TS("pts", [128, 8, 4], BF16)
        EX4, PT4 = TS("ex4", [4, 4], F32), TS("pt4", [4, 4], BF16)
        COVF = TS("covf", [4, 256], F32)
        SBS, SC, SC2 = TS("sbs", [1, 256], F32), TS("sc", [1, 256], F32), TS("sc2", [1, 256], F32)
        MX, IXU, JROW, JROWB = TS("mx", [1, 16], F32), TS("ixu", [1, 16], U32), TS("jrow", [1, 16], F32), TS("jrowb", [1, 16], BF16)
        ONES1 = TS("ones1", [1, 128], BF16)
        JB, JBI, OH = TS("jb", [128, 16], F32), TS("jbi", [128, 16], I32), TS("oh", [128, 16], BF16)
        JF, JI, VAL, BB = TS("jf", [16, 1], F32), TS("ji", [16, 1], I32), TS("val", [16, 3], F32), TS("bb", [16, 24], BF16)
        A16F, A16B, M2 = TS("a16f", [16, 128], F32), TS("a16b", [16, 128], BF16), TS("m2", [16, 8], F32)
        ROWF, ROWI = TS("rowf", [128, 8], F32), TS("rowi", [128, 8], I32)
        SCV, SHS, XRS, HS = TS("scv", [128, 8, 3, 4], F32), TS("shs", [128, 8, 4], F32), TS("xrs", [128, 8, 4], F32), TS("hs", [128, 8, 4], F32)
        d2d = Buf("d2d", track_w=False)

        P.dma("sp", KSM[:, :], dr["k_small"], w=[KSM.b])
        P.dma("sp", K16[:, :], dr["k_iota16"], w=[K16.b])
        P.dma("sp", SBS[:, :], dr["k_sbs"], w=[SBS.b])
        P.dma("sp", A16F[:, :], dr["k_a16"], w=[A16F.b])
        P.dma("sp", M2[:, :], dr["k_m2"], w=[M2.b])
        evac("dve", A16B[:, :], A16F[:, :], [A16F.b], [A16B.b])
        P.op("pool", lambda e: e.memset(ONES1[:, :], 1.0), w=[ONES1.b])
        P.dma("sp", PTB[:, :], dr["ptab"].rearrange("b p -> p b"), w=[PTB.b], allow_slow_non_contiguous=True)
        evac("dve", PTF[:, :], PTB[:, :], [PTB.b], [PTF.b])
        P.op("dve", lambda e: e.tensor_scalar(out=PTI[:, :, 0], in0=PTB[:, :], scalar1=6, scalar2=None, op0=ALU.arith_shift_right), r=[PTB.b], w=[PTI.b])
        P.op("dve", lambda e: e.tensor_scalar(out=PTI[:, :, 1], in0=PTB[:, :], scalar1=63, scalar2=None, op0=ALU.bitwise_and), r=[PTB.b], w=[PTI.b])
        evac("dve", PTHL[:, :, :], PTI[:, :, :], [PTI.b], [PTHL.b])
        P.op("pool", lambda e: e.memset(CVS[:, :, :, :], 1.0), w=[CVS.b])
        P.dma("sp", PG[:, :], dr["k_covs"].rearrange("p j c -> p (j c)"), w=[PG.b])
        evac("dve", COV[:, :, :], PG[:, :].rearrange("p (j c) -> p j c", j=8), [PG.b], [COV.b])
        P.op("pool", lambda e: e.memset(VSELF[:, :, :, :], 1.0), w=[VSELF.b])
        P.op("pool", lambda e: e.memset(VAS[:, :, :], 1.0), w=[VAS.b])
        P.op("pool", lambda e: e.memset(VAW[:, :, :, :], 1.0), w=[VAW.b])
        for b in range(n):
            for k in range(3):
                P.dma("sp", SCV[:, :, k, b], dr["sconv"][b, k].rearrange("(c p) -> p c", p=128), w=[SCV.b], allow_slow_non_contiguous=True)
            P.dma("sp", SHS[:, :, b], dr["sh"][b].rearrange("(c p) -> p c", p=128), w=[SHS.b], allow_slow_non_contiguous=True)
            P.dma("sp", out["win_s"][b, 0:511, :], dr["cwin"][b, 1:512, :], w=[d2d], sem=KTW.b)
            P.dma("sp", out["conv_s"][b, 0:2, :], dr["sconv"][b, 1:3, :], w=[d2d], sem=KTW.b)

        load_x(lambda sub, m: dr["xs"][0:m, :], n)
        ffn(1, n)
        layernorm(0, n)

        ck("s1")
        for b in range(n):
            P.dma("sp", CS[b:b + 1, 0, 0:32], dr["k_cos"][SEQ:SEQ + 1, :], w=[CS.b])
            P.dma("sp", CS[b:b + 1, 0, 32:64], dr["k_sin"][SEQ:SEQ + 1, :], w=[CS.b])
        cosv, sinv = CS[0:n, 0, 0:32], CS[0:n, 0, 32:64]
        iq, sq = WP.acquire("in_q")
        ps = PS[0]
        for kc in range(8):
            mm(ps[0:n, :], A[:, kc, 0:n], sq[:, kc * 512:(kc + 1) * 512], kc == 0, kc == 7, [A.bs[kc], sq.b], [ps.b])
        zs = nxt("zs", ZS)
        evac("act", zs[0:n, :], ps[0:n, :], [ps.b], [zs.b])
        rope_inplace(zs, zs[0:n, :].rearrange("p (h d) -> p h d", d=64), cosv, sinv, [n, 8])
        for hh in range(4):
            P.op("pe", lambda e, zs=zs, hh=hh: e.transpose(out=PS[2][:, hh * n:(hh + 1) * n], in_=zs[0:n, hh * 128:(hh + 1) * 128], identity=identf[0:n, 0:n]),
                 r=[zs.b, identf.b], w=[PS[2].b])
        evac("act", QR[:, :, 0:n], PS[2][:, 0:4 * n].rearrange("p (h t) -> p h t", h=4), [PS[2].b], [QR.b])
        for hh in range(4):
            pf = PS[4 + hh % 2]
            for kc in range(8):
                mm(pf[:, 0:n], sq[:, kc * 512 + hh * 128:kc * 512 + (hh + 1) * 128], A[:, kc, 0:n], kc == 0, kc == 7, [sq.b, A.bs[kc]], [pf.b])
            evac("dve", QW[:, hh, 0:n], pf[:, 0:n], [pf.b], [QW.b])
        WP.release(iq)
        ikv, skv = WP.acquire("in_kv")
        ps = PS[1]
        for kc in range(8):
            mm(ps[0:n, :], A[:, kc, 0:n], skv[:, kc * 512:(kc + 1) * 512], kc == 0, kc == 7, [A.bs[kc], skv.b], [ps.b])
        zs = nxt("zs", ZS)
        evac("act", zs[0:n, :], ps[0:n, :], [ps.b], [zs.b])
        P.dma("act", out["cmp_s"][:, :], zs[0:n, 0:256], r=[zs.b], w=[outb["cmp_s"]], sem=zs.b)
        rope_inplace(zs, zs[0:n, 256:384].rearrange("p (h d) -> p h d", d=64), cosv, sinv, [n, 2])
        P.dma("act", out["sel_s"][:, :], zs[0:n, 256:512], r=[zs.b], w=[outb["sel_s"]], sem=zs.b)
        P.op("pe", lambda e, zs=zs: e.transpose(out=PS[3][:, 0:n], in_=zs[0:n, 256:384], identity=identf[0:n, 0:n]), r=[zs.b, identf.b], w=[PS[3].b])
        evac("dve", KSELF[:, 0, :], PS[3][:, 0:n], [PS[3].b], [KSELF.b])
        evac("pool", VSELF[0:n, 0, :, 0:64], zs[0:n, 384:512].rearrange("p (g d) -> p g d", g=2), [zs.b], [VSELF.b])
        WP.release(ikv)
        ik2, sk2 = WP.acquire("in_kv2")
        ps = PS[0]
        for kc in range(8):
            mm(ps[0:n, 0:280], A[:, kc, 0:n], sk2[:, kc * 280:(kc + 1) * 280], kc == 0, kc == 7, [A.bs[kc], sk2.b], [ps.b])
        zs = nxt("zs", ZS)
        evac("act", zs[0:n, 0:256], ps[0:n, 0:256], [ps.b], [zs.b])
        P.op("act", lambda e, ps=ps: e.activation(out=GATES[0:n, 0, :], in_=ps[0:n, 256:280], func=AF.Sigmoid), r=[ps.b], w=[GATES.b])
        rope_inplace(zs, zs[0:n, 0:128].rearrange("p (h d) -> p h d", d=64), cosv, sinv, [n, 2])
        P.dma("act", out["win_s"][:, 511, :], zs[0:n, 0:256], r=[zs.b], w=[outb["win_s"]], sem=zs.b)
        P.op("pe", lambda e, zs=zs: e.transpose(out=PS[3][:, 0:n], in_=zs[0:n, 0:128], identity=identf[0:n, 0:n]), r=[zs.b, identf.b], w=[PS[3].b])
        evac("dve", KSELF[:, 1, :], PS[3][:, 0:n], [PS[3].b], [KSELF.b])
        evac("pool", VSELF[0:n, 1, :, 0:64], zs[0:n, 128:256].rearrange("p (g d) -> p g d", g=2), [zs.b], [VSELF.b])
        WP.release(ik2)
        gv = GATES[0:n, 0, :].rearrange("p (g h b) -> p g h b", g=2, h=4, b=3)
        for g in range(2):
            for br in range(3):
                P.op("pe", lambda e, g=g, br=br: e.transpose(out=PS[3][0:4, (g * 3 + br) * 4:(g * 3 + br) * 4 + 4], in_=gv[:, g, :, br], identity=identf[0:n, 0:n]),
                     r=[GATES.b, identf.b], w=[PS[3].b])
        evac("dve", GT[:, :, :, :], PS[3][0:4, 0:24].rearrange("p (g r b) -> p g r b", g=2, r=3), [PS[3].b], [GT.b])

        ck("s2")
        ik, swk = WP.acquire("w1bd_k")
        iv, swv = WP.acquire("w1bd_v")
        sw1 = (swk, swv)
        ccv = dr["ccmp"].rearrange("n (c r) x -> (n c) (r x)", r=8)
        crow = dr["csel"].rearrange("n r x -> (n r) x")

        def small_evac(bank, ncol, g, br, b, first_branch):
            sm = nxt("sm", SM)
            rs_, rg = sm[0:4, 0:1], sm[0:4, 1:2]
            P.op("dve", lambda e: e.tensor_scalar(out=rs_, in0=bank[0:4, 64:65], scalar1=1e-30, scalar2=None, op0=ALU.max), r=[bank.b], w=[sm.b])
            P.op("dve", lambda e: e.reciprocal(out=rs_, in_=rs_), r=[sm.b], w=[sm.b])
            tt("dve", rg, rs_, GT[0:4, g, br, b:b + 1], ALU.mult, [sm.b, GT.b], [sm.b])
            if first_branch:
                P.op("dve", lambda e: e.tensor_scalar(out=OASP[0:4, b, g, 0:64], in0=bank[0:4, 0:64], scalar1=rg, scalar2=None, op0=ALU.mult), r=[bank.b, sm.b], w=[OASP.b])
            else:
                P.op("dve", lambda e: e.scalar_tensor_tensor(out=OASP[0:4, b, g, 0:64], in0=bank[0:4, 0:64], scalar=rg, in1=OASP[0:4, b, g, 0:64], op0=ALU.mult, op1=ALU.add),
                     r=[bank.b, sm.b, OASP.b], w=[OASP.b])
            return sm, rs_

        def small_attn(b, g, kt, kcols, nchunk, vfn, kself_i, br, last_mask, first_branch):
            hp = slice(64 * g, 64 * g + 64)
            qv = QR[hp, :, b]
            for c in range(nchunk):
                S = PS[4 + c % 2]
                mm(S[:, 0:4], kt[hp, c * 128:(c + 1) * 128], qv, True, True, [kt.b, QR.b], [S.b])
                P.op("act", lambda e, S=S, c=c: e.activation(out=EXS[:, c, :], in_=S[:, 0:4], func=AF.Exp, scale=0.125), r=[S.b], w=[EXS.b])
                if last_mask and c == nchunk - 1:
                    P.op("dve", lambda e, c=c: e.tensor_scalar(out=PTS[:, c, :], in0=EXS[:, c, :], scalar1=KSM[:, 1:2], scalar2=None, op0=ALU.mult), r=[EXS.b, KSM.b], w=[PTS.b])
                else:
                    evac("dve", PTS[:, c, :], EXS[:, c, :], [EXS.b], [PTS.b])
                vap, vb = vfn(c)
                P.op("pe", lambda e, c=c, vap=vap: e.matmul(PS[0][0:4, 0:65], lhsT=PTS[:, c, :], rhs=vap, start=(c == 0), stop=False, skip_group_check=True), r=[PTS.b, vb], w=[PS[0].b])
            S = PS[6]
            mm(S[0:4, 0:4], KSELF[hp, kself_i, :], qv, True, True, [KSELF.b, QR.b], [S.b])
            P.op("act", lambda e: e.activation(out=EX4[:, :], in_=S[0:4, 0:4], func=AF.Exp, scale=0.125), r=[S.b], w=[EX4.b])
            P.op("dve", lambda e: e.tensor_scalar(out=PT4[:, :], in0=EX4[:, :], scalar1=KSM[0:4, 16 + b:17 + b], scalar2=None, op0=ALU.mult), r=[EX4.b, KSM.b], w=[PT4.b])
            P.op("pe", lambda e: e.matmul(PS[0][0:4, 0:65], lhsT=PT4[:, :], rhs=VSELF[0:4, kself_i, g, :], start=False, stop=True, skip_group_check=True), r=[PT4.b, VSELF.b], w=[PS[0].b])
            small_evac(PS[0], 65, g, br, b, first_branch)

        for b in range(n):
            P.op("dve", lambda e, b=b: e.tensor_scalar(out=IDXF[:, :], in0=K16[:, :], scalar1=PTF[:, b:b + 1], scalar2=None, op0=ALU.bypass) if False else
                 e.scalar_tensor_tensor(out=IDXF[:, :], in0=PTF[:, b:b + 1].to_broadcast([128, 16]), scalar=16.0, in1=K16[:, :], op0=ALU.mult, op1=ALU.add),
                 r=[PTF.b, K16.b, IDXG.b], w=[IDXF.b])
            evac("dve", IDXG[:, :], IDXF[:, :], [IDXF.b], [IDXG.b])
            started = [False] * 4
            for jj in range(8):
                for sh in range(2):
                    PGc = PGS[(jj * 2 + sh) % 2]
                    P.idma(PGc[:, :], ccv, IDXG[:, jj * 2 + sh:jj * 2 + sh + 1], r=[IDXG.b], w=[PGc.b])
                    xt = XT[(jj * 2 + sh) % 2]
                    for kv in range(2):
                        for rq in range(2):
                            bank = PS[4 + kv * 2 + rq]
                            for r4 in range(4):
                                row = rq * 4 + r4
                                P.op("pe", lambda e, bank=bank, r4=r4, row=row, kv=kv, PGc=PGc: e.transpose(out=bank[:, r4 * 128:(r4 + 1) * 128], in_=PGc[:, row * 256 + kv * 128:row * 256 + (kv + 1) * 128], identity=identf[:, :]),
                                     r=[PGc.b, identf.b], w=[bank.b])
                            evac("act" if rq == 0 else "dve", xt[:, kv, rq * 4:(rq + 1) * 4, :], bank[:, :].rearrange("p (r c) -> p r c", r=4), [bank.b], [xt.b])
                    for kv in range(2):
                        for sl in range(8):
                            s_ = sh * 8 + sl
                            bk = kv * 2 + jj // 4
                            P.op("pe", lambda e, bk=bk, kv=kv, sl=sl, s_=s_, jj=jj, xt=xt, st=not started[bk]: e.matmul(PS[bk][:, (jj % 4) * 128:(jj % 4 + 1) * 128], lhsT=sw1[kv][:, s_ * 128:(s_ + 1) * 128], rhs=xt[:, kv, sl, :], start=st, stop=False, skip_group_check=True),
                                 r=[sw1[kv].b, xt.b], w=[PS[bk].b])
                            started[bk] = True
                            if jj >= 1:
                                j1 = jj - 1
                                bk = kv * 2 + j1 // 4
                                oap, rap = PS[bk][:, (j1 % 4) * 128:(j1 % 4 + 1) * 128], xt[:, kv, sl, :]
                            else:
                                bk = kv * 2 + 1
                                oap, rap = PS[bk][:, 3 * 128:3 * 128 + 127], xt[:, kv, sl, 1:128]
                            P.op("pe", lambda e, oap=oap, rap=rap, kv=kv, s_=s_, st=not started[bk]: e.matmul(oap, lhsT=sw1[kv][:, (16 + s_) * 128:(17 + s_) * 128], rhs=rap, start=st, stop=False, skip_group_check=True),
                                 r=[sw1[kv].b, xt.b], w=[PS[bk].b])
                            started[bk] = True
            for kv in range(2):
                for hb in range(2):
                    bank = PS[kv * 2 + hb]
                    P.op("act", lambda e, bank=bank, kv=kv, hb=hb: e.activation(out=GUS[:, kv, hb * 512:(hb + 1) * 512], in_=bank[:, :], func=AF.Gelu_apprx_tanh, bias=PEB[:, kv:kv + 1]),
                         r=[bank.b, PEB.b], w=[GUS.b])
            for hb in range(2):
                mm(PS[4 + hb][:, :], W2BD[0][:, :], GUS[:, 0, hb * 512:(hb + 1) * 512], True, True, [W2BD[0].b, GUS.b], [PS[4 + hb].b])
                evac("dve", CKS[:, hb * 512:(hb + 1) * 512], PS[4 + hb][:, :], [PS[4 + hb].b], [CKS.b])
            for jj in range(8):
                bank = PS[6 + jj % 2]
                mm(bank[:, 0:128], GUS[:, 1, jj * 128:(jj + 1) * 128], W2BD[1][:, :], True, True, [W2BD[1].b, GUS.b], [bank.b])
                evac("act", CVS[:, jj, :, 0:64], bank[:, 0:128].rearrange("p (g d) -> p g d", g=2), [bank.b], [CVS.b])
            ck("s3")
            WGv = PG[:, 0:1024].rearrange("p (c x) -> p c x", c=4)
            P.dma("sp", WGv, dr["cwin"][b].rearrange("(c p) x -> p c x", p=128), w=[PG.b])
            for c in range(4):
                bank = PS[2 + c % 2]
                P.op("pe", lambda e, bank=bank, c=c: e.transpose(out=bank[:, 0:128], in_=PG[:, c * 256:c * 256 + 128], identity=identf[:, :]), r=[PG.b, identf.b], w=[bank.b])
                evac("dve", KTW[:, c * 128:(c + 1) * 128], bank[:, 0:128], [bank.b], [KTW.b])
                evac("pool", VAW[:, c, :, 0:64], PG[:, c * 256 + 128:c * 256 + 256].rearrange("p (g d) -> p g d", g=2), [PG.b], [VAW.b])
            for g in range(2):
                hp = slice(64 * g, 64 * g + 64)
                qw = QW[hp, :, b]
                for jj in range(8):
                    S = PS[4 + jj % 2]
                    mm(S[:, 0:4], CKS[hp, jj * 128:(jj + 1) * 128], qw, True, True, [CKS.b, QW.b], [S.b])
                    P.op("act", lambda e, S=S, jj=jj: e.activation(out=EXS[:, jj, :], in_=S[:, 0:4], func=AF.Exp, scale=0.125), r=[S.b], w=[EXS.b])
                    P.op("dve", lambda e, jj=jj: e.tensor_scalar(out=PTS[:, jj, :], in0=EXS[:, jj, :], scalar1=KSM[:, 8 + jj:9 + jj], scalar2=None, op0=ALU.mult), r=[EXS.b, KSM.b], w=[PTS.b])
                    P.op("pe", lambda e, jj=jj, g=g: e.matmul(PS[0][0:4, 0:65], lhsT=PTS[:, jj, :], rhs=CVS[:, jj, g, :], start=(jj == 0), stop=(jj == 7), skip_group_check=True), r=[PTS.b, CVS.b], w=[PS[0].b])
                    P.op("pe", lambda e, jj=jj: e.matmul(PS[0][0:4, 65:321], lhsT=PTS[:, jj, :], rhs=COV[:, jj, :], start=False, stop=(jj == 7), skip_group_check=True), r=[PTS.b, COV.b], w=[PS[0].b])
                sm, rs_ = small_evac(PS[0], 322, g, 0, b, True)
                evac("act", COVF[:, :], PS[0][0:4, 65:321], [PS[0].b], [COVF.b])
                mm(PS[1][0:1, 0:256], rs_, COVF[:, :], True, True, [sm.b, COVF.b], [PS[1].b])
                ck("s4")
                tt("dve", SC[:, :], PS[1][0:1, 0:256], SBS[:, :], ALU.add, [PS[1].b, SBS.b], [SC.b])
                P.op("dve", lambda e: e.max(out=MX[:, 0:8], in_=SC[:, :]), r=[SC.b], w=[MX.b])
                P.op("dve", lambda e: e.max_index(out=IXU[:, 0:8], in_max=MX[:, 0:8], in_values=SC[:, :]), r=[SC.b, MX.b], w=[IXU.b])
                P.op("dve", lambda e: e.match_replace(out=SC2[:, :], in_to_replace=MX[:, 0:8], in_values=SC[:, :], imm_value=-1.0e30), r=[SC.b, MX.b], w=[SC2.b])
                P.op("dve", lambda e: e.max(out=MX[:, 8:16], in_=SC2[:, :]), r=[SC2.b], w=[MX.b])
                P.op("dve", lambda e: e.max_index(out=IXU[:, 8:16], in_max=MX[:, 8:16], in_values=SC2[:, :]), r=[SC2.b, MX.b], w=[IXU.b])
                evac("dve", JROW[:, :], IXU[:, :], [IXU.b], [JROW.b])
                evac("dve", JROWB[:, :], JROW[:, :], [JROW.b], [JROWB.b])
                mm(PS[1][:, 256:272], ONES1[:, :], JROWB[:, :], True, True, [ONES1.b, JROWB.b], [PS[1].b])
                evac("dve", JBI[:, :], PS[1][:, 256:272], [PS[1].b], [JBI.b])
                P.op("dve", lambda e: e.tensor_scalar(out=JBI[:, :], in0=JBI[:, :], scalar1=1, scalar2=None, op0=ALU.arith_shift_right), r=[JBI.b], w=[JBI.b])
                evac("dve", JB[:, :], JBI[:, :], [JBI.b], [JB.b])
                P.op("dve", lambda e: e.tensor_scalar(out=OH[:, :], in0=JB[:, :], scalar1=KSM[:, 20:21], scalar2=None, op0=ALU.is_equal), r=[JB.b, KSM.b], w=[OH.b])
                mm(PS[1][0:16, 272:274], OH[:, :], PTHL[:, b, :], True, True, [OH.b, PTHL.b], [PS[1].b])
                evac("dve", VAL[:, 0:2], PS[1][0:16, 272:274], [PS[1].b], [VAL.b])
                P.op("pe", lambda e: e.transpose(out=PS[1][0:16, 274:275], in_=JROW[0:1, :], identity=identf[0:1, 0:1]), r=[JROW.b, identf.b], w=[PS[1].b])
                evac("dve", JI[:, :], PS[1][0:16, 274:275], [PS[1].b], [JI.b])
                P.op("dve", lambda e: e.tensor_scalar(out=JI[:, :], in0=JI[:, :], scalar1=1, scalar2=None, op0=ALU.bitwise_and), r=[JI.b], w=[JI.b])
                evac("dve", VAL[:, 2:3], JI[:, :], [JI.b], [VAL.b])
                for w_ in range(3):
                    P.op("dve", lambda e, w_=w_: e.tensor_scalar(out=BB[:, w_ * 8:(w_ + 1) * 8], in0=M2[:, :], scalar1=VAL[:, w_:w_ + 1], scalar2=None, op0=ALU.mult), r=[M2.b, VAL.b], w=[BB.b])
                mm(PS[1][:, 280:304], A16B[:, :], BB[:, :], True, True, [A16B.b, BB.b], [PS[1].b])
                P.op("dve", lambda e: e.tensor_scalar(out=ROWF[:, :], in0=PS[1][:, 280:288], scalar1=8192.0, scalar2=KSM[:, 0:1], op0=ALU.mult, op1=ALU.add), r=[PS[1].b, KSM.b], w=[ROWF.b])
                P.op("dve", lambda e: e.scalar_tensor_tensor(out=ROWF[:, :], in0=PS[1][:, 288:296], scalar=128.0, in1=ROWF[:, :], op0=ALU.mult, op1=ALU.add), r=[PS[1].b, ROWF.b], w=[ROWF.b])
                P.op("dve", lambda e: e.scalar_tensor_tensor(out=ROWF[:, :], in0=PS[1][:, 296:304], scalar=64.0, in1=ROWF[:, :], op0=ALU.mult, op1=ALU.add), r=[PS[1].b, ROWF.b], w=[ROWF.b])
                evac("dve", ROWI[:, :], ROWF[:, :], [ROWF.b], [ROWI.b])
                ck("s5")
                for c in range(8):
                    rs = RS[c % 2]
                    P.idma(rs[:, :], crow, ROWI[:, c:c + 1], r=[ROWI.b], w=[rs.b])
                    bank = PS[2 + c % 2]
                    P.op("pe", lambda e, bank=bank, rs=rs: e.transpose(out=bank[:, 0:128], in_=rs[:, 0:128], identity=identf[:, :]), r=[rs.b, identf.b], w=[bank.b])
                    evac("dve", KTS[hp, c * 128:(c + 1) * 128], bank[hp, 0:128], [bank.b], [KTS.b])
                    evac("act", VAS[:, c, 0:64], rs[:, 128 + g * 64:192 + g * 64], [rs.b], [VAS.b])
                ck("s6")
                small_attn(b, g, KTS, None, 8, lambda c: (VAS[:, c, :], VAS.b), 0, 1, True, False)
                ck("s7")
                small_attn(b, g, KTW, None, 4, lambda c, g=g: (VAW[:, c, g, :], VAW.b), 1, 2, False, False)
                ck("s8")
        ck("s9")
        WP.release(ik)
        WP.release(iv)
        evac("dve", OASP[:, :, :, 64:128], OASP[:, :, :, 0:64], [OASP.b], [OASP.b])
        for b in range(n):
            for g in range(2):
                P.op("pe", lambda e, b=b, g=g: e.transpose(out=PS[7][:, (b * 2 + g) * 4:(b * 2 + g) * 4 + 4], in_=OASP[0:4, b, g, :], identity=identf[0:4, 0:4]), r=[OASP.b, identf.b], w=[PS[7].b])
        for b in range(n):
            pv8 = PS[7][:, b * 8:(b + 1) * 8].rearrange("p (i e) -> p i e", e=2)
            evac("dve", OTS[0:64, 0, :, b], pv8[0:64, :, 0], [PS[7].b], [OTS.b])
            evac("dve", OTS[64:128, 0, :, b], pv8[64:128, :, 1], [PS[7].b], [OTS.b])

        ck("s10")
        for half in range(2):
            ix, sx = WP.acquire(f"in_xr{half}")
            ig, sgr = WP.acquire(f"in_gr{half}")
            for j in range(4):
                i = half * 4 + j
                px, pg = PS[j % 2], PS[2 + j % 2]
                for kc in range(8):
                    mm(px[:, 0:n], sx[:, kc * 512 + j * 128:kc * 512 + (j + 1) * 128], A[:, kc, 0:n], kc == 0, kc == 7, [sx.b, A.bs[kc]], [px.b])
                for kc in range(8):
                    mm(pg[:, 0:n], sgr[:, kc * 512 + j * 128:kc * 512 + (j + 1) * 128], A[:, kc, 0:n], kc == 0, kc == 7, [sgr.b, A.bs[kc]], [pg.b])
                evac("act", XRS[:, i, :], px[:, 0:n], [px.b], [XRS.b])
                xc = nxt("tf", TMPF)
                P.op("dve", lambda e, xc=xc, i=i: e.tensor_scalar(out=xc[:, 0:n], in0=XRS[:, i, :], scalar1=RGP[:, 3, i:i + 1], scalar2=RGP[:, 4, i:i + 1], op0=ALU.mult, op1=ALU.add), r=[XRS.b, RGP.b], w=[xc.b])
                for k in range(3):
                    P.op("dve", lambda e, xc=xc, i=i, k=k: e.scalar_tensor_tensor(out=xc[:, 0:n], in0=SCV[:, i, k, :], scalar=RGP[:, k, i:i + 1], in1=xc[:, 0:n], op0=ALU.mult, op1=ALU.add),
                         r=[SCV.b, RGP.b, xc.b], w=[xc.b])
                xcb = nxt("xcb", XCB)
                evac("act", xcb[:, 0:n], xc[:, 0:n], [xc.b], [xcb.b])
                pr, pi = PS[4 + j % 2], PS[6 + j % 2]
                mm(pr[:, 0:n], WAX[0][:, i, :], xcb[:, 0:n], True, True, [WAX[0].b, xcb.b], [pr.b])
                mm(pi[:, 0:n], WAX[1][:, i, :], xcb[:, 0:n], True, True, [WAX[1].b, xcb.b], [pi.b])
                ra, ii, sq_ = nxt("tf", TMPF), nxt("tf", TMPF), nxt("tf", TMPF)
                P.op("act", lambda e, ra=ra, pr=pr, i=i: e.activation(out=ra[:, 0:n], in_=pr[:, 0:n], func=AF.Sigmoid, bias=RGP[:, 5, i:i + 1]), r=[pr.b, RGP.b], w=[ra.b])
                P.op("act", lambda e, ii=ii, pi=pi, i=i: e.activation(out=ii[:, 0:n], in_=pi[:, 0:n], func=AF.Sigmoid, bias=RGP[:, 6, i:i + 1]), r=[pi.b, RGP.b], w=[ii.b])
                P.op("act", lambda e, ra=ra, i=i: e.activation(out=ra[:, 0:n], in_=ra[:, 0:n], func=AF.Exp, scale=RGP[:, 7, i:i + 1]), r=[ra.b, RGP.b], w=[ra.b])
                tt("pool", sq_[:, 0:n], ra[:, 0:n], ra[:, 0:n], ALU.mult, [ra.b], [sq_.b])
                P.op("act", lambda e, sq_=sq_: e.activation(out=sq_[:, 0:n], in_=sq_[:, 0:n], func=AF.Sqrt, scale=-1.0, bias=1.0), r=[sq_.b], w=[sq_.b])
                tt("pool", ii[:, 0:n], ii[:, 0:n], sq_[:, 0:n], ALU.mult, [ii.b, sq_.b], [ii.b])
                tt("dve", ii[:, 0:n], ii[:, 0:n], xc[:, 0:n], ALU.mult, [ii.b, xc.b], [ii.b])
                tt("dve", ra[:, 0:n], ra[:, 0:n], SHS[:, i, :], ALU.mult, [ra.b, SHS.b], [ra.b])
                tt("dve", HS[:, i, :], ra[:, 0:n], ii[:, 0:n], ALU.add, [ra.b, ii.b], [HS.b])
                P.op("act", lambda e, sq_=sq_, pg=pg: e.activation(out=sq_[:, 0:n], in_=pg[:, 0:n], func=AF.Gelu_apprx_tanh), r=[pg.b, sq_.b], w=[sq_.b])
                tt("dve", YR[:, i, 0:n], HS[:, i, :], sq_[:, 0:n], ALU.mult, [HS.b, sq_.b], [YR.bs[i]])
            WP.release(ix)
            WP.release(ig)
        for b in range(n):
            P.dma("act", out["conv_s"][b, 2].rearrange("(c p) -> p c", p=128), XRS[:, :, b], r=[XRS.b], w=[outb["conv_s"]], sem=XRS.b, allow_slow_non_contiguous=True)
            P.dma("act", out["h_s"][b].rearrange("(c p) -> p c", p=128), HS[:, :, b], r=[HS.b], w=[outb["h_s"]], sem=HS.b, allow_slow_non_contiguous=True)

        ck("s11")
        merge(n, lambda: [(OTS[:, 0, kc, :], [OTS.b], kc * 128, 128) for kc in range(4)])
        layernorm(1, n)
        ffn(2, n)
        layernorm(2, n)
        store_y(lambda sub, m: out["y_s"][0:m, :], n, outb["y_s"])

    for (seq, t) in tiles:
        load_x(lambda sub, m, seq=seq, t=t: dr["xp"][seq, t * TT + sub * 128:t * TT + sub * 128 + m, :], TT)
        ffn(1, TT)
        layernorm(0, TT)
        if stop_after == "ln1":
            store_y(lambda sub, m, seq=seq, t=t: out["y_p"][seq, t * TT + sub * 128:t * TT + sub * 128 + m, :], TT, outb["y_p"])
            skip = PASS_ORDER[18:]
            for name in skip:
                i, _ = WP.acquire(name)
                WP.release(i)
            continue
        if t == 0:
            P.op("pool", lambda e: e.memset(XH[:, :, :], 0.0), w=[XH.b])
            P.op("pool", lambda e: e.memset(HST[:, :], 0.0), w=[HST.b])
        win_tok_prompt(seq, t)
        compress_prompt()
        attention_prompt(seq, t)
        rglru(TT, t == 0)
        if t == NT - 1:
            for k in range(3):
                P.dma("act", out["conv_p"][seq, k].rearrange("(c p) -> p c", p=128), XH[:, :, k], r=[XH.b], w=[outb["conv_p"]], sem=XH.b, allow_slow_non_contiguous=True)
            P.dma("act", out["h_p"][seq].rearrange("(c p) -> p c", p=128), HST[:, :], r=[HST.b], w=[outb["h_p"]], sem=HST.b, allow_slow_non_contiguous=True)
        merge(TT, lambda: [(OT[:, kc, :], [OT.b], kc * 128, 128) for kc in range(4)])
        layernorm(1, TT)
        ffn(2, TT)
        layernorm(2, TT)
        store_y(lambda sub, m, seq=seq, t=t: out["y_p"][seq, t * TT + sub * 128:t * TT + sub * 128 + m, :], TT, outb["y_p"])

    P.barrier()
    es_p.close()
    if do_sample:
        sample_pass()
    P.barrier(engines=("act",))
    P.replay()
    es.close()
    return nc


_NC_CACHE = {}


def kernel(**inputs):
    cfg = dict(tiles=[(s_, t_) for s_ in range(NSEQ) for t_ in range(NT)], sample=True)
    if "nc" not in _NC_CACHE:
        _NC_CACHE["nc"] = build(cfg)
    nc = _NC_CACHE["nc"]
    f32 = lambda a: np.ascontiguousarray(np.asarray(a), dtype=np.float32)
    consts = _consts()
    ccmp = f32(inputs["cache_cmp_kv"])[0].reshape(NPOOL, 128, 256)
    csel = f32(inputs["cache_sel_kv"])[0].reshape(NPOOL, 128, 256)
    xp = f32(inputs["x_prompt"])
    xs = f32(inputs["x_sample"])[:, 0]
    cwin = f32(inputs["cache_win_kv"])[0].reshape(NCORES * NSMP, 512, 256)
    sconv = f32(inputs["state_conv"])[0]
    sh = f32(inputs["state_h"])[0]
    ptab = np.ascontiguousarray(np.asarray(inputs["page_table"]), dtype=np.int32)
    shared = {}
    for k in IN_SHAPES:
        if k in ("xp", "xs", "ccmp", "csel", "cwin", "sconv", "sh", "ptab"):
            continue
        shared[k] = f32(inputs[k])[0]
    in_maps = []
    for c in range(NCORES):
        m = dict(shared)
        m.update(consts)
        m["xp"] = xp[NSEQ * c:NSEQ * (c + 1)]
        m["xs"] = xs[NSMP * c:NSMP * (c + 1)]
        m["ccmp"] = ccmp
        m["csel"] = csel
        m["cwin"] = cwin[NSMP * c:NSMP * (c + 1)]
        m["sconv"] = sconv[NSMP * c:NSMP * (c + 1)]
        m["sh"] = sh[NSMP * c:NSMP * (c + 1)]
        m["ptab"] = ptab[NSMP * c:NSMP * (c + 1)]
        in_maps.append(m)
    res = run_bass_kernel_spmd(nc, in_maps, core_ids=list(range(NCORES)))
    r = res.results
    cat = lambda k: np.concatenate([np.asarray(r[c][k]) for c in range(NCORES)], axis=0)
    B, BS = NCORES * NSEQ, NCORES * NSMP
    return (
        cat("y_p").reshape(B, SEQ, D).astype(np.float32),
        cat("y_s").reshape(BS, 1, D).astype(np.float32),
        cat("cmp_p").reshape(1, B, SEQ, 2, 2, 64).astype(np.float32),
        cat("cmp_s").reshape(1, BS, 1, 2, 2, 64).astype(np.float32),
        cat("sel_p").reshape(1, B, SEQ, 2, 2, 64).astype(np.float32),
        cat("sel_s").reshape(1, BS, 1, 2, 2, 64).astype(np.float32),
        cat("win_p").reshape(1, B, 512, 2, 2, 64).astype(np.float32),
        cat("win_s").reshape(1, BS, 512, 2, 2, 64).astype(np.float32),
        cat("conv_p").reshape(1, B, 3, D).astype(np.float32),
        cat("conv_s").reshape(1, BS, 3, D).astype(np.float32),
        cat("h_p").reshape(1, B, D).astype(np.float32),
        cat("h_s").reshape(1, BS, D).astype(np.float32),
    )
```

```python
import contextlib
import numpy as np
import concourse.bass as bass
import concourse.mybir as mybir
from concourse.bass_utils import run_bass_kernel_spmd

F32 = mybir.dt.float32
BF16 = mybir.dt.bfloat16
I32 = mybir.dt.int32
U32 = mybir.dt.uint32
AF = mybir.ActivationFunctionType
ALU = mybir.AluOpType
AX = mybir.AxisListType

NCORES = 8
D = 1024
DFF = 2816
SEQ = 2048
NSEQ = 2
NSMP = 4
TT = 512
NT = SEQ // TT
INW = 5400
ALPHA = 2.0 ** 0.25
LN_EPS = 1e-5
C_RES = 0.5 / ALPHA
EPS2 = LN_EPS / (ALPHA * ALPHA)
PAST = 16384
NPOOL = 5120
SLOT = 4096
NSLOT = 4

C_Q, C_KV, C_NG, C_XR, C_GR, C_GA, C_GB = 0, 512, 1280, 1304, 2328, 3352, 4376


class Buf:
    __slots__ = ("name", "w", "rs", "track_w", "sem", "semcnt", "excl")

    def __init__(self, name, track_w=True):
        self.name = name
        self.excl = False
        self.w = None
        self.rs = {}
        self.track_w = track_w
        self.sem = None
        self.semcnt = 0


class Prog:
    ENG = ("pe", "act", "dve", "pool", "sp")
    EPOCH = 1 << 30

    def __init__(self, nc, es):
        self.nc = nc
        self.es = es
        self.streams = {e: [] for e in self.ENG}
        self.cur_sem = {}
        self.cnt = {}
        self.seen = {e: {} for e in self.ENG}
        self.nsem = 0
        self.all_sems = []
        for e in ("pe", "act", "dve", "pool"):
            self.cur_sem[e] = self._newsem("c_" + e)
            self.cnt[e] = 0
        self.dma_sems = []
        self.n_ins = 0

    def _newsem(self, name):
        self.nsem += 1
        return self.es.enter_context(self.nc.semaphore(f"{name}_{self.nsem}"))

    def _deps(self, r, w):
        deps = {}

        def add(ev):
            if ev is None:
                return
            s, v = ev
            if deps.get(s, 0) < v:
                deps[s] = v
        for b in r:
            add(b.w)
            if b.excl:
                for s, v in b.rs.items():
                    add((s, v))
        for b in w:
            if b.track_w:
                add(b.w)
            for s, v in b.rs.items():
                add((s, v))
        return deps

    def _emit_waits(self, eng, deps):
        own = self.cur_sem.get(eng)
        for s, v in deps.items():
            if eng == "pe" and s is own:
                continue
            if self.seen[eng].get(s, 0) >= v:
                continue
            self.seen[eng][s] = v
            self.streams[eng].append(lambda e, s=s, v=v: e.wait_ge(s, v))

    def _record(self, ev, r, w):
        s, v = ev
        for b in r:
            if b.rs.get(s, 0) < v:
                b.rs[s] = v
        for b in w:
            if b.track_w:
                b.w = ev
                b.rs = {}
            else:
                b.w = ev

    def op(self, eng, fn, r=(), w=(), signal=True):
        self.n_ins += 1
        self._emit_waits(eng, self._deps(r, w))
        sem = self.cur_sem[eng]
        if signal:
            self.cnt[eng] += 1
            v = self.cnt[eng]
            self.streams[eng].append(lambda e, fn=fn, sem=sem: fn(e).then_inc(sem, 1))
        else:
            v = self.cnt[eng] + 1
            self.streams[eng].append(lambda e, fn=fn: fn(e))
        self._record((sem, v), r, w)

    def dma(self, q, out, in_, r=(), w=(), sem=None, **kw):
        self.n_ins += 1
        self._emit_waits(q, self._deps(r, w))
        b = sem if sem is not None else w[0]
        if b.sem is None:
            b.sem = {}
        if q not in b.sem:
            b.sem[q] = [self._newsem("d_" + b.name + "_" + q), 0]
            self.dma_sems.append(b.sem[q])
        b.sem[q][1] += 16
        sem, v = b.sem[q]
        self.streams[q].append(lambda e, out=out, in_=in_, sem=sem, kw=kw: e.dma_start(out=out, in_=in_, **kw).then_inc(sem, 16))
        self._record((sem, v), r, w)

    def idma(self, out, in_, idx_ap, r=(), w=(), sem=None):
        q = "pool"
        self.n_ins += 1
        self._emit_waits(q, self._deps(r, w))
        b = sem if sem is not None else w[0]
        if b.sem is None:
            b.sem = {}
        if "idma" not in b.sem:
            b.sem["idma"] = [self._newsem("i_" + b.name), 0]
            self.dma_sems.append(b.sem["idma"])
        b.sem["idma"][1] += 16
        sm, v = b.sem["idma"]
        self.streams[q].append(lambda e: e.indirect_dma_start(out=out, out_offset=None, in_=in_, in_offset=bass.IndirectOffsetOnAxis(ap=idx_ap, axis=0)).then_inc(sm, 16))
        self._record((sm, v), r, w)

    def raw(self, eng, fn):
        self.streams[eng].append(fn)

    def barrier(self, engines=("pe", "act", "dve", "pool", "sp")):
        evs = {}
        for e in ("pe", "act", "dve", "pool"):
            if self.cnt[e] > 0:
                evs[self.cur_sem[e]] = self.cnt[e]
        for sm, v in self.dma_sems:
            evs[sm] = v
        for e in engines:
            for s, v in evs.items():
                if e == "pe" and s is self.cur_sem["pe"]:
                    continue
                if self.seen[e].get(s, 0) >= v:
                    continue
                self.seen[e][s] = v
                self.streams[e].append(lambda en, s=s, v=v: en.wait_ge(s, v))

    def replay(self):
        nc = self.nc
        with nc.Block() as block:
            @block.tensor
            def _(e):
                for f in self.streams["pe"]:
                    f(e)

            @block.scalar
            def _(e):
                for f in self.streams["act"]:
                    f(e)

            @block.vector
            def _(e):
                for f in self.streams["dve"]:
                    f(e)

            @block.gpsimd
            def _(e):
                for f in self.streams["pool"]:
                    f(e)

            @block.sync
            def _(e):
                for f in self.streams["sp"]:
                    f(e)


class T:
    def __init__(self, nc, es, name, shape, dtype, psum=False):
        self.b = Buf(name)
        if psum:
            self.b.excl = True
            self.t = es.enter_context(nc.psum_tensor(name, shape, dtype))
        else:
            self.t = es.enter_context(nc.sbuf_tensor(name, shape, dtype))

    def __getitem__(self, k):
        return self.t[k]


def _consts():
    c = {}
    half = 32
    freqs = (np.float32(10000.0) ** (-np.arange(half, dtype=np.float32) / np.float32(half))).astype(np.float32)
    pos = np.concatenate([np.arange(SEQ), [PAST]]).astype(np.float32)
    ang = (pos[:, None] * freqs[None, :]).astype(np.float32)
    c["k_cos"] = np.cos(ang).astype(np.float32)
    c["k_sin"] = np.sin(ang).astype(np.float32)
    c["k_ident"] = np.eye(128, dtype=np.float32)
    c["k_ones"] = np.full((128, 128), 1.0 / D, dtype=np.float32)
    p = np.arange(128)[:, None]
    f = np.arange(512)[None, :]
    ms = []
    for i in range(4):
        ms.append((f - p - 128 * i >= 0))
    for i in range(4):
        ms.append((f - p - 128 * i <= 0))
    for t in range(4):
        ms.append((512 * t + f - 16 * p - 31 >= 0))
    c["k_masks"] = np.stack(ms).astype(np.float32)
    x = np.arange(SEQ)[None, :]
    j = np.arange(32)[:, None]
    c["k_e32"] = (x // 64 == j).astype(np.float32)
    q = np.arange(SEQ)[:, None]
    jj = np.arange(32)[None, :]
    cur = q // 64
    valid = jj * 64 <= q
    forced = (jj == 0) | (jj == cur) | (jj == cur - 1)
    c["k_sb"] = np.where(valid, np.where(forced, 1.0e4, 0.0), -1.0e30).astype(np.float32)
    cc = np.arange(128)[:, None]
    cov = ((cc * 16 < (jj + 1) * 64) & (cc * 16 + 32 > jj * 64)) & (cc < 127)
    c["k_cover"] = cov.astype(np.float32)
    pg = np.arange(128)[:, None, None]
    jq = np.arange(8)[None, :, None]
    j2 = np.arange(256)[None, None, :]
    cs = pg * 8 + jq
    covs = ((cs * 16 < (j2 + 1) * 64) & (cs * 16 + 32 > j2 * 64)) & (cs < 1023)
    c["k_covs"] = covs.astype(np.float32)
    sm = np.zeros((128, 32), np.float32)
    sm[:, 0] = np.arange(128) % 64
    sm[:, 1] = (np.arange(128) < 64)
    sm[:, 8:16] = 1.0
    sm[127, 15] = 0.0
    sm[0:4, 16:20] = np.eye(4)
    sm[:, 20] = np.arange(128)
    c["k_small"] = sm
    c["k_iota16"] = np.tile(np.arange(16, dtype=np.float32)[None, :], (128, 1))
    sbs = np.zeros((1, 256), np.float32)
    sbs[0, 0] = 1.0e4
    sbs[0, 255] = 1.0e4
    c["k_sbs"] = sbs
    a16 = np.zeros((16, 128), np.float32)
    for j_ in range(16):
        a16[j_, (j_ % 2) * 64:(j_ % 2) * 64 + 64] = 1.0
    c["k_a16"] = a16
    m2 = np.zeros((16, 8), np.float32)
    for j_ in range(16):
        m2[j_, j_ // 2] = 1.0
    c["k_m2"] = m2
    return c


CONST_SHAPES = {"k_cos": [SEQ + 1, 32], "k_sin": [SEQ + 1, 32], "k_ident": [128, 128], "k_ones": [128, 128],
                "k_masks": [12, 128, 512], "k_e32": [32, SEQ], "k_sb": [SEQ, 32], "k_cover": [128, 32],
                "k_covs": [128, 8, 256], "k_small": [128, 32], "k_iota16": [128, 16], "k_sbs": [1, 256],
                "k_a16": [16, 128], "k_m2": [16, 8]}

IN_SHAPES = {
    "xp": ([NSEQ, SEQ, D], F32), "xs": ([NSMP, D], F32),
    "ccmp": ([NPOOL, 128, 256], F32), "csel": ([NPOOL, 128, 256], F32), "cwin": ([NSMP, 512, 256], F32),
    "sconv": ([NSMP, 3, D], F32), "sh": ([NSMP, D], F32), "ptab": ([NSMP, 128], I32),
    "ffn1_w_gate": ([D, DFF], F32), "ffn1_w_up": ([D, DFF], F32), "ffn1_w_down": ([DFF, D], F32),
    "ln1_g": ([D], F32), "ln1_b": ([D], F32), "w_in": ([D, INW], F32),
    "cmp_pe_k": ([32, 64], F32), "cmp_w1_k": ([2048, 64], F32), "cmp_w2_k": ([64, 64], F32),
    "cmp_pe_v": ([32, 64], F32), "cmp_w1_v": ([2048, 64], F32), "cmp_w2_v": ([64, 64], F32),
    "conv_w": ([4, D], F32), "conv_b": ([D], F32), "rg_w_a": ([16, 64, 64], F32), "rg_b_a": ([D], F32),
    "rg_w_x": ([16, 64, 64], F32), "rg_b_x": ([D], F32), "rg_lam": ([D], F32),
    "w_br_attn": ([512, D], F32), "w_br_rnn": ([D, D], F32), "w_out": ([D, D], F32),
    "ln2_g": ([D], F32), "ln2_b": ([D], F32),
    "ffn2_w_gate": ([D, DFF], F32), "ffn2_w_up": ([D, DFF], F32), "ffn2_w_down": ([DFF, D], F32),
    "ln3_g": ([D], F32), "ln3_b": ([D], F32),
}
OUT_SHAPES = {
    "y_p": [NSEQ, SEQ, D], "y_s": [NSMP, D], "cmp_p": [NSEQ, SEQ, 256], "cmp_s": [NSMP, 256],
    "sel_p": [NSEQ, SEQ, 256], "sel_s": [NSMP, 256], "win_p": [NSEQ, 512, 256], "win_s": [NSMP, 512, 256],
    "conv_p": [NSEQ, 3, D], "conv_s": [NSMP, 3, D], "h_p": [NSEQ, D], "h_s": [NSMP, D],
}


GROUPS = {}
PASS_ORDER = []


def _mk_groups():
    def add(name, src, k0, kc, c0, nb):
        GROUPS[name] = (src, k0, kc, c0, nb)
        PASS_ORDER.append(name)

    def ffn(f):
        for i in range(6):
            nb = 512 if i < 5 else 256
            add(f"g{f}_{i}", f"ffn{f}_w_gate", 0, 8, 512 * i, nb)
            add(f"u{f}_{i}", f"ffn{f}_w_up", 0, 8, 512 * i, nb)
        for h in range(2):
            for kh in range(3):
                add(f"d{f}_{h}{kh}", f"ffn{f}_w_down", kh * 1024, 8 if kh < 2 else 6, 512 * h, 512)
    ffn(1)
    add("in_q", "w_in", 0, 8, 0, 512)
    add("in_kv", "w_in", 0, 8, 512, 512)
    add("in_kv2", "w_in", 0, 8, 1024, 280)
    add("w1bd_k", "cmp_w1_k", 0, 32, 0, 128)
    add("w1bd_v", "cmp_w1_v", 0, 32, 0, 128)
    add("in_xr0", "w_in", 0, 8, C_XR, 512)
    add("in_gr0", "w_in", 0, 8, C_GR, 512)
    add("in_xr1", "w_in", 0, 8, C_XR + 512, 512)
    add("in_gr1", "w_in", 0, 8, C_GR + 512, 512)
    for j in range(2):
        add(f"in_ga{j}", "w_in", 0, 8, C_GA + 512 * j, 512)
        add(f"in_gb{j}", "w_in", 0, 8, C_GB + 512 * j, 512)
        add(f"ba{j}", "w_br_attn", 0, 4, 512 * j, 512)
        add(f"br{j}", "w_br_rnn", 0, 8, 512 * j, 512)
    add("wo0", "w_out", 0, 8, 0, 512)
    add("wo1", "w_out", 0, 8, 512, 512)
    ffn(2)


_mk_groups()


class TC(T):
    def __init__(self, nc, es, name, shape, dtype):
        super().__init__(nc, es, name, shape, dtype)
        self.bs = [Buf(f"{name}{i}") for i in range(shape[1])]


class CV:
    def __init__(self, parent, c0, n, name):
        self.p, self.c0, self.n = parent, c0, n
        self.bs = [Buf(f"{name}{i}") for i in range(n)]
        self.b = Buf(name)

    def __getitem__(self, k):
        p, c, t = k
        if isinstance(c, slice):
            c = slice((c.start or 0) + self.c0, (c.stop if c.stop is not None else self.n) + self.c0)
        else:
            c = c + self.c0
        return self.p.t[p, c, t]


class WPipe:
    def __init__(self, P, slots, scratch, order):
        self.P, self.slots, self.scratch, self.order = P, slots, scratch, order
        self.pos = 0
        self.issued = 0
        for _ in range(min(len(slots), len(order))):
            self._issue()

    def _issue(self):
        i = self.issued
        name = self.order[i]
        slot = self.slots[i % len(self.slots)]
        _, _, kc, _, nb = GROUPS[name]
        n = kc * nb
        self.P.dma("sp", slot[:, 0:n], self.scratch[name], w=[slot.b])
        self.issued += 1

    def acquire(self, name):
        assert self.order[self.pos] == name, (self.order[self.pos], name)
        i = self.pos
        self.pos += 1
        return i, self.slots[i % len(self.slots)]

    def release(self, i):
        j = i + len(self.slots)
        assert j == self.issued or j >= len(self.order), (j, self.issued)
        if j < len(self.order):
            self._issue()


def build(cfg):
    nc = bass.Bass("TRN2", target_bir_lowering=False)
    es = contextlib.ExitStack()
    dr = {}
    npool = cfg.get("npool", NPOOL)
    for k, (shape, dt) in IN_SHAPES.items():
        if k in ("ccmp", "csel"):
            shape = [npool] + shape[1:]
        dr[k] = nc.dram_tensor(k, shape, dt, kind="ExternalInput").ap()
    for k, shape in CONST_SHAPES.items():
        dr[k] = nc.dram_tensor(k, shape, F32, kind="ExternalInput").ap()
    out = {}
    outb = {}
    for k, shape in OUT_SHAPES.items():
        out[k] = nc.dram_tensor(k, shape, F32, kind="ExternalOutput").ap()
        outb[k] = Buf("o_" + k, track_w=False)
    scratch = {}
    for name, (src, k0, kc, c0, nb) in GROUPS.items():
        scratch[name] = nc.dram_tensor("ws_" + name, [128, kc * nb], BF16, kind="Internal").ap()
    scr_buf = Buf("scratch", track_w=False)

    P = Prog(nc, es)
    tiles = cfg["tiles"]
    do_sample = cfg.get("sample", False)
    stop_after = cfg.get("stop_after")

    identf = T(nc, es, "identf", [128, 128], F32)
    onesb = T(nc, es, "onesb", [128, 128], BF16)
    lnp = T(nc, es, "lnp", [128, 6, 8], F32)
    PS = [T(nc, es, f"ps{i}", [128, 512], F32, psum=True) for i in range(8)]

    W2BD = [T(nc, es, f"w2bd{i}", [128, 128], BF16) for i in range(2)]
    PEB = T(nc, es, "peb", [128, 2], F32)
    CVA = T(nc, es, "cva", [128, 2, 97], BF16)
    RGP = T(nc, es, "rgp", [128, 8, 8], F32)
    WAX = [T(nc, es, f"wax{i}", [128, 8, 128], BF16) for i in range(2)]
    P.dma("sp", identf[:, :], dr["k_ident"], w=[identf.b])
    for i, nm in enumerate(["ln1_g", "ln1_b", "ln2_g", "ln2_b", "ln3_g", "ln3_b"]):
        P.dma("sp", lnp[:, i, :], dr[nm].rearrange("(c p) -> p c", p=128), w=[lnp.b], allow_slow_non_contiguous=True)

    with contextlib.ExitStack() as es1:
        st = [T(nc, es1, f"pst{i}", [128, SLOT], F32) for i in range(2)]
        sb = [T(nc, es1, f"psb{i}", [128, SLOT], BF16) for i in range(2)]
        sbb = [[Buf(f"psb{i}_{j}") for j in range(3)] for i in range(2)]
        onesf = T(nc, es1, "onesf", [128, 128], F32)
        P.dma("sp", onesf[:, :], dr["k_ones"], w=[onesf.b])
        P.op("dve", lambda e: e.tensor_copy(out=onesb[:, :], in_=onesf[:, :]), r=[onesf.b], w=[onesb.b])
        cvs = T(nc, es1, "cvs", [128, 32], F32)
        P.dma("sp", cvs[:, :], dr["k_cover"], w=[cvs.b])
        P.op("pool", lambda e: e.memset(CVA[:, :, :], 1.0), w=[CVA.b])
        for g in range(2):
            P.op("dve", lambda e, g=g: e.tensor_copy(out=CVA[:, g, 65:97], in_=cvs[:, :]), r=[cvs.b], w=[CVA.b])
        pes = T(nc, es1, "pes", [128, 2, 32], F32)
        peb16 = T(nc, es1, "peb16", [128, 2, 32], BF16)
        w1s = T(nc, es1, "w1s", [128, 32, 128], F32)
        w1b = T(nc, es1, "w1b", [128, 32, 128], BF16)
        for kv, nm in enumerate(["k", "v"]):
            w2s = T(nc, es1, f"w2s{kv}", [128, 128], F32)
            P.op("pool", lambda e, w2s=w2s: e.memset(w2s[:, :], 0.0), w=[w2s.b])
            P.dma("sp", w2s[0:64, 0:64], dr[f"cmp_w2_{nm}"], w=[w2s.b])
            P.dma("sp", w2s[64:128, 64:128], dr[f"cmp_w2_{nm}"], w=[w2s.b])
            P.op("dve", lambda e, w2s=w2s, kv=kv: e.tensor_copy(out=W2BD[kv][:, :], in_=w2s[:, :]), r=[w2s.b], w=[W2BD[kv].b])
            for g in range(2):
                P.dma("sp", pes[64 * g:64 * g + 64, kv, :], dr[f"cmp_pe_{nm}"].rearrange("r d -> d r"), w=[pes.b], allow_slow_non_contiguous=True)
        P.op("dve", lambda e: e.tensor_copy(out=peb16[:, :, :], in_=pes[:, :, :]), r=[pes.b], w=[peb16.b])
        for kv, nm in enumerate(["k", "v"]):
            P.op("pool", lambda e: e.memset(w1s[:, :, :], 0.0), w=[w1s.b])
            wsrc = dr[f"cmp_w1_{nm}"].rearrange("(r d) h -> d r h", d=64)
            P.dma("sp", w1s[0:64, :, 0:64], wsrc, w=[w1s.b])
            P.dma("sp", w1s[64:128, :, 64:128], wsrc, w=[w1s.b])
            P.op("dve", lambda e: e.tensor_copy(out=w1b[:, :, :], in_=w1s[:, :, :]), r=[w1s.b], w=[w1b.b])
            for rs in range(32):
                mm_ = lambda e, rs=rs, kv=kv: e.matmul(PS[0][:, kv * 2:kv * 2 + 2], lhsT=w1b[:, rs, :], rhs=peb16[:, kv, rs:rs + 1].to_broadcast([128, 2]) if False else peb16[:, kv, rs:rs + 1], start=(rs == 0), stop=(rs == 31))
                P.op("pe", lambda e, rs=rs, kv=kv: e.matmul(PS[0][:, kv:kv + 1], lhsT=w1b[:, rs, :], rhs=peb16[:, kv, rs:rs + 1], start=(rs == 0), stop=(rs == 31)),
                     r=[w1b.b, peb16.b], w=[PS[0].b])
            P.op("dve", lambda e, kv=kv: e.tensor_copy(out=PEB[:, kv:kv + 1], in_=PS[0][:, kv:kv + 1]), r=[PS[0].b], w=[PEB.b])
        for k in range(4):
            P.dma("sp", RGP[:, k, :], dr["conv_w"][k].rearrange("(c p) -> p c", p=128), w=[RGP.b], allow_slow_non_contiguous=True)
        for k, nm in enumerate(["conv_b", "rg_b_a", "rg_b_x", "rg_lam"]):
            P.dma("sp", RGP[:, 4 + k, :], dr[nm].rearrange("(c p) -> p c", p=128), w=[RGP.b], allow_slow_non_contiguous=True)
        P.op("act", lambda e: e.activation(out=RGP[:, 7, :], in_=RGP[:, 7, :], func=AF.Exp, scale=-1.0), r=[RGP.b], w=[RGP.b])
        P.op("act", lambda e: e.activation(out=RGP[:, 7, :], in_=RGP[:, 7, :], func=AF.Ln, bias=1.0), r=[RGP.b], w=[RGP.b])
        P.op("dve", lambda e: e.tensor_scalar(out=RGP[:, 7, :], in0=RGP[:, 7, :], scalar1=-8.0, scalar2=None, op0=ALU.mult), r=[RGP.b], w=[RGP.b])
        for i, nm in enumerate(["rg_w_a", "rg_w_x"]):
            ws_ = T(nc, es1, f"waxs{i}", [128, 8, 128], F32)
            P.op("pool", lambda e, ws_=ws_: e.memset(ws_[:, :, :], 0.0), w=[ws_.b])
            wv = dr[nm].rearrange("(i j) d e -> j d i e", j=2)
            for j in range(2):
                P.dma("sp", ws_[64 * j:64 * j + 64, :, 64 * j:64 * j + 64], wv[j], w=[ws_.b])
            P.op("dve", lambda e, ws_=ws_, i=i: e.tensor_copy(out=WAX[i][:, :, :], in_=ws_[:, :, :]), r=[ws_.b], w=[WAX[i].b])
        for gi, name in enumerate(PASS_ORDER):
            src, k0, kc, c0, nb = GROUPS[name]
            n = kc * nb
            s, o, ob = st[gi % 2], sb[gi % 2], sbb[gi % 2]
            if name.startswith("w1bd"):
                P.op("pool", lambda e, s=s: e.memset(s[:, 0:4096], 0.0), w=[s.b])
                sv = s[:, 0:4096].rearrange("p (r h) -> p r h", h=128)
                wsrc = dr[src].rearrange("(r d) h -> d r h", d=64)
                P.dma("sp", sv[0:64, :, 0:64], wsrc, w=[s.b])
                P.dma("sp", sv[64:128, :, 64:128], wsrc, w=[s.b])
            else:
                P.dma("sp", s[:, 0:n].rearrange("p (k n) -> p k n", k=kc),
                      dr[src][k0:k0 + kc * 128, c0:c0 + nb].rearrange("(k p) n -> p k n", p=128), w=[s.b])
            if name == "in_q":
                for k_ in range(8):
                    eng = ("dve", "pool", "dve")[k_ % 3]
                    P.op(eng, lambda e, s=s, o=o, k_=k_: e.tensor_copy(out=o[:, k_ * 512:(k_ + 1) * 512].rearrange("p (hh j d) -> p hh j d", hh=4, j=2, d=64),
                                                                        in_=s[:, k_ * 512:(k_ + 1) * 512].rearrange("p (j hh d) -> p hh j d", hh=4, j=2, d=64)),
                         r=[s.b], w=[ob[k_ % 3]])
                P.dma("act", scratch[name], o[:, 0:n], r=ob, w=[scr_buf], sem=ob[0])
                continue
            a = (n // 3) // 2 * 2
            cuts = [0, a, 2 * a, n]
            P.op("dve", lambda e, s=s, o=o, c=cuts: e.tensor_copy(out=o[:, c[0]:c[1]], in_=s[:, c[0]:c[1]]), r=[s.b], w=[ob[0]])
            P.op("act", lambda e, s=s, o=o, c=cuts: e.activation(out=o[:, c[1]:c[2]], in_=s[:, c[1]:c[2]], func=AF.Copy), r=[s.b], w=[ob[1]])
            P.op("pool", lambda e, s=s, o=o, c=cuts: e.tensor_copy(out=o[:, c[2]:c[3]], in_=s[:, c[2]:c[3]]), r=[s.b], w=[ob[2]])
            P.dma("act", scratch[name], o[:, 0:n], r=ob, w=[scr_buf], sem=ob[0])
        P.barrier()

    R = TC(nc, es, "R", [128, 8, TT], F32)
    A = TC(nc, es, "A", [128, 8, TT], BF16)
    H = TC(nc, es, "H", [128, 22, TT], BF16)
    WS = [T(nc, es, f"wslot{i}", [128, SLOT], BF16) for i in range(NSLOT)]
    STG = [T(nc, es, f"stg{i}", [128, 1024], F32) for i in range(2)]
    TMPF = [T(nc, es, f"tmpf{i}", [128, TT], F32) for i in range(4)]
    TMPB = [T(nc, es, f"tmpb{i}", [128, TT], BF16) for i in range(4)]
    RSTD = T(nc, es, "rstd", [128, TT], F32)
    NMR = T(nc, es, "nmr", [128, TT], F32)
    CS = T(nc, es, "cs", [128, 4, 64], F32)
    ZS = [T(nc, es, f"zs{i}", [128, 512], F32) for i in range(3)]
    RT = [T(nc, es, f"rt{i}", [128, 256], F32) for i in range(4)]
    GATES = T(nc, es, "gates", [128, 4, 24], F32)
    QR = T(nc, es, "qr", [128, 4, TT], BF16)
    QW = T(nc, es, "qw", [128, 4, TT], BF16)
    GU = T(nc, es, "gu", [128, 2, 128], BF16)
    CKT = T(nc, es, "ckt", [128, 128], BF16)
    SM = [T(nc, es, f"sm{i}", [128, 160], F32) for i in range(4)]
    XH = T(nc, es, "xh", [128, 8, 3], F32)
    HST = T(nc, es, "hst", [128, 8], F32)
    XB = [T(nc, es, f"xb{i}", [128, 3 + TT], F32) for i in range(2)]
    XCB = [T(nc, es, f"xcb{i}", [128, TT], BF16) for i in range(2)]
    SG = [T(nc, es, f"sg{i}", [128, 4, TT], BF16) for i in range(2)]
    es_p = contextlib.ExitStack()
    MSK = T(nc, es_p, "msk", [128, 12, 512], BF16)
    E32 = T(nc, es_p, "e32", [32, SEQ], BF16)
    SELK = T(nc, es_p, "selk", [128, SEQ], BF16)
    WINK = T(nc, es_p, "wink", [128, SEQ], BF16)
    SELV = T(nc, es_p, "selv", [128, 16, 2, 65], BF16)
    WINV = T(nc, es_p, "winv", [128, 16, 2, 65], BF16)
    CMPT = T(nc, es_p, "cmpt", [128, 2, SEQ + 16], BF16)
    EX = [T(nc, es_p, f"ex{i}", [128, 512], BF16) for i in range(3)]
    PT = [T(nc, es_p, f"pt{i}", [128, 512], BF16) for i in range(3)]
    MKS = [T(nc, es_p, f"mks{i}", [128, 512], BF16) for i in range(2)]
    OA = [T(nc, es_p, f"oa{i}", [128, 512], F32) for i in range(4)]
    IMP = T(nc, es_p, "imp", [128, 4, 2, 32], F32)
    SBT = T(nc, es_p, "sbt", [128, 4, 32], F32)
    SELT = T(nc, es_p, "selt", [32, 2, TT], BF16)
    YR = CV(H, 0, 8, "yr")
    M = CV(H, 8, 8, "m")
    OT = CV(H, 16, 4, "ot")
    for mi in range(12):
        tm = TMPF[mi % 4]
        P.dma("sp", tm[:, 0:512], dr["k_masks"][mi], w=[tm.b])
        P.op("dve" if mi % 2 == 0 else "pool", lambda e, tm=tm, mi=mi: e.tensor_copy(out=MSK[:, mi, :], in_=tm[:, 0:512]), r=[tm.b], w=[MSK.b])
    for qi in range(4):
        tm = TMPF[qi % 4]
        P.dma("sp", tm[0:32, 0:512], dr["k_e32"][:, qi * 512:(qi + 1) * 512], w=[tm.b])
        P.op("dve", lambda e, tm=tm, qi=qi: e.tensor_copy(out=E32[:, qi * 512:(qi + 1) * 512], in_=tm[0:32, 0:512]), r=[tm.b], w=[E32.b])
    for tt_ in (SELV, WINV):
        P.op("pool", lambda e, tt_=tt_: e.memset(tt_[:, :, :, :], 1.0), w=[tt_.b])
    P.op("pool", lambda e: e.memset(CMPT[:, :, :], 0.0), w=[CMPT.b])

    npass = len(tiles) + (1 if do_sample else 0)
    WP = WPipe(P, WS, scratch, PASS_ORDER * npass)
    ctr = {"stg": 0, "tf": 0, "tb": 0, "zs": 0, "rt": 0, "ex": 0, "pt": 0, "mks": 0, "sm": 0, "xb": 0, "xcb": 0}

    def nxt(key, lst):
        i = ctr[key]
        ctr[key] += 1
        return lst[i % len(lst)]

    def mm(ps_ap, lhsT, rhs, start, stop, r, w, **kw):
        P.op("pe", lambda e: e.matmul(ps_ap, lhsT=lhsT, rhs=rhs, start=start, stop=stop, **kw), r=r, w=w, signal=True)

    def load_x(src_rows_fn, ntok):
        nsub = (ntok + 127) // 128
        for sub in range(nsub):
            m = min(128, ntok - sub * 128)
            stg = nxt("stg", STG)
            P.dma("sp", stg[0:m, :], src_rows_fn(sub, m), w=[stg.b])
            for half in range(2):
                ps = PS[half]
                for j in range(4):
                    c = half * 4 + j
                    P.op("pe", lambda e, ps=ps, stg=stg, j=j, c=c, m=m: e.transpose(out=ps[:, j * 128:j * 128 + m], in_=stg[0:m, c * 128:(c + 1) * 128], identity=identf[0:m, 0:m]),
                         r=[stg.b, identf.b], w=[ps.b], signal=True)
                src = ps[:, :].rearrange("p (j t) -> p j t", j=4)[:, :, 0:m]
                P.op("dve", lambda e, src=src, half=half, sub=sub, m=m: e.tensor_copy(out=R[:, half * 4:half * 4 + 4, sub * 128:sub * 128 + m], in_=src),
                     r=[ps.b], w=R.bs[half * 4:half * 4 + 4])
                P.op("act", lambda e, src=src, half=half, sub=sub, m=m: e.activation(out=A[:, half * 4:half * 4 + 4, sub * 128:sub * 128 + m], in_=src, func=AF.Copy),
                     r=[ps.b], w=A.bs[half * 4:half * 4 + 4])

    def store_y(dst_rows_fn, ntok, ob):
        nsub = (ntok + 127) // 128
        for sub in range(nsub):
            m = min(128, ntok - sub * 128)
            stg = nxt("stg", STG)
            for half in range(2):
                ps = PS[half]
                for j in range(4):
                    c = half * 4 + j
                    P.op("pe", lambda e, ps=ps, j=j, c=c, m=m, sub=sub: e.transpose(out=ps[0:m, j * 128:(j + 1) * 128], in_=R[:, c, sub * 128:sub * 128 + m], identity=identf[:, :]),
                         r=[R.bs[c], identf.b], w=[ps.b], signal=True)
                if half == 0:
                    P.op("dve", lambda e, ps=ps, stg=stg, m=m: e.tensor_copy(out=stg[0:m, 0:512], in_=ps[0:m, :]), r=[ps.b], w=[stg.b])
                else:
                    P.op("act", lambda e, ps=ps, stg=stg, m=m: e.activation(out=stg[0:m, 512:1024], in_=ps[0:m, :], func=AF.Copy), r=[ps.b], w=[stg.b])
            P.dma("act", dst_rows_fn(sub, m), stg[0:m, :], r=[stg.b], w=[ob], sem=stg.b)

    def layernorm(li, ntok):
        n = ntok
        pm, pq = PS[2], PS[3]
        for c in range(8):
            zb = nxt("tb", TMPB)
            sq = nxt("tb", TMPB)
            P.op("act", lambda e, zb=zb, c=c: e.activation(out=zb[:, 0:n], in_=R[:, c, 0:n], func=AF.Copy), r=[R.bs[c]], w=[zb.b])
            P.op("pool", lambda e, sq=sq, c=c: e.tensor_tensor(out=sq[:, 0:n], in0=R[:, c, 0:n], in1=R[:, c, 0:n], op=ALU.mult), r=[R.bs[c]], w=[sq.b])
            mm(pm[:, 0:n], onesb[:, :], zb[:, 0:n], c == 0, c == 7, [onesb.b, zb.b], [pm.b])
            mm(pq[:, 0:n], onesb[:, :], sq[:, 0:n], c == 0, c == 7, [onesb.b, sq.b], [pq.b])
        t1 = nxt("tf", TMPF)
        t2 = nxt("tf", TMPF)
        P.op("act", lambda e: e.activation(out=t1[:, 0:n], in_=pm[:, 0:n], func=AF.Square), r=[pm.b], w=[t1.b])
        P.op("dve", lambda e: e.tensor_tensor(out=t2[:, 0:n], in0=pq[:, 0:n], in1=t1[:, 0:n], op=ALU.subtract), r=[pq.b, t1.b], w=[t2.b])
        P.op("dve", lambda e: e.tensor_scalar(out=t2[:, 0:n], in0=t2[:, 0:n], scalar1=EPS2, scalar2=None, op0=ALU.add), r=[t2.b], w=[t2.b])
        P.op("act", lambda e: e.activation(out=t2[:, 0:n], in_=t2[:, 0:n], func=AF.Sqrt), r=[t2.b], w=[t2.b])
        P.op("dve", lambda e: e.reciprocal(out=RSTD[:, 0:n], in_=t2[:, 0:n]), r=[t2.b], w=[RSTD.b])
        P.op("dve", lambda e: e.scalar_tensor_tensor(out=NMR[:, 0:n], in0=pm[:, 0:n], scalar=-1.0, in1=RSTD[:, 0:n], op0=ALU.mult, op1=ALU.mult),
             r=[pm.b, RSTD.b], w=[NMR.b])
        for c in range(8):
            t = nxt("tf", TMPF)
            eng = "dve" if c % 2 == 0 else "pool"
            P.op(eng, lambda e, t=t, c=c: e.tensor_tensor(out=t[:, 0:n], in0=R[:, c, 0:n], in1=RSTD[:, 0:n], op=ALU.mult), r=[R.bs[c], RSTD.b], w=[t.b])
            P.op(eng, lambda e, t=t: e.tensor_tensor(out=t[:, 0:n], in0=t[:, 0:n], in1=NMR[:, 0:n], op=ALU.add), r=[t.b, NMR.b], w=[t.b])
            P.op("dve", lambda e, t=t, c=c: e.tensor_scalar(out=R[:, c, 0:n], in0=t[:, 0:n], scalar1=lnp[:, 2 * li, c:c + 1], scalar2=lnp[:, 2 * li + 1, c:c + 1], op0=ALU.mult, op1=ALU.add),
                 r=[t.b, lnp.b], w=[R.bs[c]])
            P.op("act", lambda e, c=c: e.activation(out=A[:, c, 0:n], in_=R[:, c, 0:n], func=AF.Copy), r=[R.bs[c]], w=[A.bs[c]])

    def ffn(f, ntok):
        n = ntok
        for i in range(6):
            nb = 512 if i < 5 else 256
            ig, sg = WP.acquire(f"g{f}_{i}")
            iu, su = WP.acquire(f"u{f}_{i}")
            for j in range(nb // 128):
                c = i * 4 + j
                pg, pu = PS[c % 2], PS[2 + c % 2]
                for kc in range(8):
                    mm(pg[:, 0:n], sg[:, kc * nb + j * 128:kc * nb + (j + 1) * 128], A[:, kc, 0:n], kc == 0, kc == 7, [sg.b, A.bs[kc]], [pg.b])
                for kc in range(8):
                    mm(pu[:, 0:n], su[:, kc * nb + j * 128:kc * nb + (j + 1) * 128], A[:, kc, 0:n], kc == 0, kc == 7, [su.b, A.bs[kc]], [pu.b])
                t = nxt("tf", TMPF)
                P.op("act", lambda e, t=t, pg=pg: e.activation(out=t[:, 0:n], in_=pg[:, 0:n], func=AF.Silu), r=[pg.b], w=[t.b])
                P.op("dve", lambda e, t=t, pu=pu, c=c: e.tensor_tensor(out=H[:, c, 0:n], in0=t[:, 0:n], in1=pu[:, 0:n], op=ALU.mult), r=[t.b, pu.b], w=[H.bs[c]])
            WP.release(ig)
            WP.release(iu)
        for h in range(2):
            banks = PS[4:8] if h == 0 else PS[0:4]
            for kh in range(3):
                idd, sd = WP.acquire(f"d{f}_{h}{kh}")
                for j in range(4):
                    for kc in range(8 if kh < 2 else 6):
                        kg = kh * 8 + kc
                        mm(banks[j][:, 0:n], sd[:, kc * 512 + j * 128:kc * 512 + (j + 1) * 128], H[:, kg, 0:n], kg == 0, kg == 21, [sd.b, H.bs[kg]], [banks[j].b])
                WP.release(idd)
            for j in range(4):
                c = h * 4 + j
                eng = "dve" if j % 2 == 0 else "pool"
                if eng == "pool":
                    t = nxt("tf", TMPF)
                    P.op("act", lambda e, t=t, j=j, banks=banks: e.activation(out=t[:, 0:n], in_=banks[j][:, 0:n], func=AF.Copy, scale=C_RES), r=[banks[j].b], w=[t.b])
                    P.op("pool", lambda e, t=t, c=c: e.tensor_tensor(out=R[:, c, 0:n], in0=R[:, c, 0:n], in1=t[:, 0:n], op=ALU.add), r=[t.b, R.bs[c]], w=[R.bs[c]])
                else:
                    P.op("dve", lambda e, j=j, c=c, banks=banks: e.scalar_tensor_tensor(out=R[:, c, 0:n], in0=banks[j][:, 0:n], scalar=C_RES, in1=R[:, c, 0:n], op0=ALU.mult, op1=ALU.add),
                         r=[banks[j].b, R.bs[c]], w=[R.bs[c]])

    def evac(eng, out_ap, in_ap, r, w, **kw):
        if eng == "act":
            P.op("act", lambda e: e.activation(out=out_ap, in_=in_ap, func=AF.Copy, **kw), r=r, w=w)
        else:
            P.op(eng, lambda e: e.tensor_copy(out=out_ap, in_=in_ap), r=r, w=w)

    def tt(eng, out_ap, a, b, op, r, w):
        P.op(eng, lambda e: e.tensor_tensor(out=out_ap, in0=a, in1=b, op=op), r=r, w=w)

    def rope_inplace(zs, view, cos, sin, shape):
        nd = len(shape)
        x1 = view[(slice(None),) * nd + (slice(0, 32),)]
        x2 = view[(slice(None),) * nd + (slice(32, 64),)]
        m = shape[0]
        nel = int(np.prod(shape[1:])) * 32
        tshape = list(shape) + [32]
        c, s_ = cos, sin
        for _ in range(nd - 1):
            c = c.unsqueeze(1)
            s_ = s_.unsqueeze(1)
        cb = c.to_broadcast(tshape)
        sb_ = s_.to_broadcast(tshape)
        ts = [nxt("rt", RT) for _ in range(4)]

        def tv(t_):
            v = t_[0:m, 0:nel]
            if nd == 2:
                return v.rearrange("p (a d) -> p a d", d=32)
            return v.rearrange("p (a b d) -> p a b d", a=shape[1], d=32)
        tt("dve", tv(ts[0]), x1, cb, ALU.mult, [zs.b, CS.b], [ts[0].b])
        tt("pool", tv(ts[1]), x2, sb_, ALU.mult, [zs.b, CS.b], [ts[1].b])
        tt("dve", tv(ts[2]), x2, cb, ALU.mult, [zs.b, CS.b], [ts[2].b])
        tt("pool", tv(ts[3]), x1, sb_, ALU.mult, [zs.b, CS.b], [ts[3].b])
        tt("dve", x1, tv(ts[0]), tv(ts[1]), ALU.subtract, [ts[0].b, ts[1].b], [zs.b])
        tt("pool", x2, tv(ts[2]), tv(ts[3]), ALU.add, [ts[2].b, ts[3].b], [zs.b])

    prev_post = [None]

    def win_tok_prompt(seq, t):
        P.dma("sp", CS[:, :, 0:32], dr["k_cos"][t * TT:(t + 1) * TT].rearrange("(s p) d -> p s d", p=128), w=[CS.b])
        P.dma("sp", CS[:, :, 32:64], dr["k_sin"][t * TT:(t + 1) * TT].rearrange("(s p) d -> p s d", p=128), w=[CS.b])
        iq, sq = WP.acquire("in_q")
        for sub in range(4):
            ps = PS[sub % 2]
            for kc in range(8):
                mm(ps[:, :], A[:, kc, sub * 128:(sub + 1) * 128], sq[:, kc * 512:(kc + 1) * 512], kc == 0, kc == 7, [A.bs[kc], sq.b], [ps.b])
            zs = nxt("zs", ZS)
            evac("act", zs[:, :], ps[:, :], [ps.b], [zs.b])
            rope_inplace(zs, zs[:, :].rearrange("p (h d) -> p h d", d=64), CS[:, sub, 0:32], CS[:, sub, 32:64], [128, 8])
            def post(sub=sub, zs=zs):
                pt_ = PS[2 + sub % 2]
                for hh in range(4):
                    P.op("pe", lambda e, pt_=pt_, zs=zs, hh=hh: e.transpose(out=pt_[:, hh * 128:(hh + 1) * 128], in_=zs[:, hh * 128:(hh + 1) * 128], identity=identf[:, :]),
                         r=[zs.b, identf.b], w=[pt_.b])
                evac("act", QR[:, :, sub * 128:(sub + 1) * 128], pt_[:, :].rearrange("p (h t) -> p h t", h=4), [pt_.b], [QR.b])
            if prev_post[0] is not None:
                prev_post[0]()
            prev_post[0] = post
        prev_post[0]()
        prev_post[0] = None
        for hh in range(4):
            pf = PS[4 + hh % 2]
            for kc in range(8):
                lw = sq[:, kc * 512 + hh * 128:kc * 512 + (hh + 1) * 128]
                mm(pf[:, :], lw, A[:, kc, :], kc == 0, kc == 7, [sq.b, A.bs[kc]], [pf.b])
            evac("dve", QW[:, hh, :], pf[:, :], [pf.b], [QW.b])
        WP.release(iq)
        ikv, skv = WP.acquire("in_kv")
        for sub in range(4):
            ps = PS[sub % 2]
            row0 = t * TT + sub * 128
            for kc in range(8):
                mm(ps[:, :], A[:, kc, sub * 128:(sub + 1) * 128], skv[:, kc * 512:(kc + 1) * 512], kc == 0, kc == 7, [A.bs[kc], skv.b], [ps.b])
            zs = nxt("zs", ZS)
            evac("act", zs[:, :], ps[:, :], [ps.b], [zs.b])
            P.dma("act", out["cmp_p"][seq, row0:row0 + 128, :], zs[:, 0:256], r=[zs.b], w=[outb["cmp_p"]], sem=zs.b)
            rope_inplace(zs, zs[:, 256:384].rearrange("p (h d) -> p h d", d=64), CS[:, sub, 0:32], CS[:, sub, 32:64], [128, 2])
            P.dma("act", out["sel_p"][seq, row0:row0 + 128, :], zs[:, 256:512], r=[zs.b], w=[outb["sel_p"]], sem=zs.b)
            def post(sub=sub, zs=zs, row0=row0):
                pt_ = PS[2 + sub % 2]
                P.op("pe", lambda e, pt_=pt_, zs=zs: e.transpose(out=pt_[:, 0:128], in_=zs[:, 256:384], identity=identf[:, :]), r=[zs.b, identf.b], w=[pt_.b])
                evac("dve", SELK[:, row0:row0 + 128], pt_[:, 0:128], [pt_.b], [SELK.b])
                evac("pool", SELV[:, 4 * t + sub, :, 0:64], zs[:, 384:512].rearrange("p (g d) -> p g d", g=2), [zs.b], [SELV.b])
            if prev_post[0] is not None:
                prev_post[0]()
            prev_post[0] = post
        prev_post[0]()
        prev_post[0] = None
        for kv in range(2):
            pf = PS[4 + kv]
            for kc in range(8):
                mm(pf[:, :], skv[:, kc * 512 + kv * 128:kc * 512 + (kv + 1) * 128], A[:, kc, :], kc == 0, kc == 7, [skv.b, A.bs[kc]], [pf.b])
            evac("dve" if kv == 0 else "act", CMPT[:, kv, t * TT:(t + 1) * TT], pf[:, :], [pf.b], [CMPT.b])
        WP.release(ikv)
        ik2, sk2 = WP.acquire("in_kv2")
        for sub in range(4):
            ps = PS[sub % 2]
            for kc in range(8):
                mm(ps[:, 0:280], A[:, kc, sub * 128:(sub + 1) * 128], sk2[:, kc * 280:(kc + 1) * 280], kc == 0, kc == 7, [A.bs[kc], sk2.b], [ps.b])
            zs = nxt("zs", ZS)
            evac("act", zs[:, 0:256], ps[:, 0:256], [ps.b], [zs.b])
            P.op("act", lambda e, ps=ps, sub=sub: e.activation(out=GATES[:, sub, :], in_=ps[:, 256:280], func=AF.Sigmoid), r=[ps.b], w=[GATES.b])
            rope_inplace(zs, zs[:, 0:128].rearrange("p (h d) -> p h d", d=64), CS[:, sub, 0:32], CS[:, sub, 32:64], [128, 2])
            if t == NT - 1:
                P.dma("act", out["win_p"][seq, sub * 128:(sub + 1) * 128, :], zs[:, 0:256], r=[zs.b], w=[outb["win_p"]], sem=zs.b)
            def post(sub=sub, zs=zs):
                pt_ = PS[2 + sub % 2]
                P.op("pe", lambda e, pt_=pt_, zs=zs: e.transpose(out=pt_[:, 0:128], in_=zs[:, 0:128], identity=identf[:, :]), r=[zs.b, identf.b], w=[pt_.b])
                row0 = t * TT + sub * 128
                evac("dve", WINK[:, row0:row0 + 128], pt_[:, 0:128], [pt_.b], [WINK.b])
                evac("pool", WINV[:, 4 * t + sub, :, 0:64], zs[:, 128:256].rearrange("p (g d) -> p g d", g=2), [zs.b], [WINV.b])
            if prev_post[0] is not None:
                prev_post[0]()
            prev_post[0] = post
        prev_post[0]()
        prev_post[0] = None
        WP.release(ik2)

    def compress(ncol, cmpt_rhs_fn):
        for kv in range(2):
            iw, sw = WP.acquire("w1bd_k" if kv == 0 else "w1bd_v")
            pu = PS[6 + kv]
            for rs in range(32):
                mm(pu[:, 0:ncol], sw[:, rs * 128:(rs + 1) * 128], cmpt_rhs_fn(kv, rs), rs == 0, rs == 31, [sw.b, CMPT.b], [pu.b])
            WP.release(iw)
            P.op("act", lambda e, pu=pu, kv=kv: e.activation(out=GU[:, kv, 0:ncol], in_=pu[:, 0:ncol], func=AF.Gelu_apprx_tanh, bias=PEB[:, kv:kv + 1]),
                 r=[pu.b, PEB.b], w=[GU.b])

    def compress_prompt():
        compress(128, lambda kv, rs: CMPT[:, kv, rs:rs + 2033:16])
        mm(PS[4][:, 0:128], W2BD[0][:, :], GU[:, 0, :], True, True, [W2BD[0].b, GU.b], [PS[4].b])
        evac("dve", CKT[:, :], PS[4][:, 0:128], [PS[4].b], [CKT.b])
        mm(PS[5][:, 0:128], GU[:, 1, :], W2BD[1][:, :], True, True, [W2BD[1].b, GU.b], [PS[5].b])
        evac("act", CVA[:, :, 0:64], PS[5][:, 0:128].rearrange("p (g d) -> p g d", g=2), [PS[5].b], [CVA.b])

    def attn_evac(g, br, ncol, first_branch, imp_sub=None):
        for sub in range(4):
            pv = PS[sub][:, 0:4 * ncol].rearrange("p (h c) -> p h c", c=ncol)
            sm = nxt("sm", SM)
            rs_ = sm[:, 0:4]
            rg = sm[:, 4:8]
            P.op("dve", lambda e, pv=pv, rs_=rs_: e.tensor_scalar(out=rs_.unsqueeze(2), in0=pv[:, :, 64:65], scalar1=1e-30, scalar2=None, op0=ALU.max), r=[PS[sub].b], w=[sm.b])
            P.op("dve", lambda e, rs_=rs_: e.reciprocal(out=rs_, in_=rs_), r=[sm.b], w=[sm.b])
            gv = GATES[:, sub, :].rearrange("p (g h b) -> p g h b", g=2, h=4, b=3)[:, g, :, br]
            tt("dve", rg, rs_, gv, ALU.mult, [sm.b, GATES.b], [sm.b])
            oav = OA[sub][:, g * 256:(g + 1) * 256].rearrange("p (h d) -> p h d", d=64)
            rgb = rg.unsqueeze(2).to_broadcast([128, 4, 64])
            if first_branch:
                tt("dve", oav, pv[:, :, 0:64], rgb, ALU.mult, [PS[sub].b, sm.b], [OA[sub].b])
            else:
                t_ = nxt("tf", TMPF)
                tv = t_[:, 0:256].rearrange("p (h d) -> p h d", d=64)
                tt("dve", tv, pv[:, :, 0:64], rgb, ALU.mult, [PS[sub].b, sm.b], [t_.b])
                tt("pool", oav, oav, tv, ALU.add, [t_.b, OA[sub].b], [OA[sub].b])
            if ncol == 97:
                iv = sm[:, 8:136].rearrange("p (h j) -> p h j", j=32)
                tt("dve", iv, pv[:, :, 65:97], rs_.unsqueeze(2).to_broadcast([128, 4, 32]), ALU.mult, [PS[sub].b, sm.b], [sm.b])
                P.op("dve", lambda e, iv=iv, sub=sub: e.tensor_reduce(out=IMP[:, sub, g, :], in_=iv.rearrange("p h j -> p j h"), axis=AX.X, op=ALU.add), r=[sm.b], w=[IMP.b])

    pend = []
    kctr = [0]
    SBANK = (4, 5, 7)

    def flush_pv(keep=0):
        while len(pend) > keep:
            pend.pop(0)()

    def attn_chunk(g, hh, kT, q, mask, mbufs, vaug, ncol, subs, first, scale=0.125):
        hp = slice(64 * g, 64 * g + 64)
        S = PS[SBANK[kctr[0] % 3]]
        kctr[0] += 1
        mm(S[:, :], kT[0][hp, kT[1]], q[hp, hh, :], True, True, [kT[2], q.b], [S.b])
        ex = nxt("ex", EX)
        P.op("act", lambda e: e.activation(out=ex[:, :], in_=S[:, :], func=AF.Exp, scale=scale), r=[S.b], w=[ex.b])
        if mask is not None:
            pt = nxt("pt", PT)
            tt("dve" if hh % 2 == 0 else "pool", pt[:, :], ex[:, :], mask, ALU.mult, [ex.b] + mbufs, [pt.b])
        else:
            pt = ex
        vb = list(vaug_b[0])
        subs = list(subs)

        def pv():
            for sub in subs:
                P.op("pe", lambda e, sub=sub, st=first[sub]: e.matmul(PS[sub][:, hh * ncol:(hh + 1) * ncol], lhsT=pt[:, sub * 128:(sub + 1) * 128], rhs=vaug, start=st, stop=True, skip_group_check=True),
                     r=[pt.b] + vb, w=[PS[sub].b])
                first[sub] = False
        flush_pv(keep=1)
        pend.append(pv)

    vaug_b = [[]]

    def attention_prompt(seq, t):
        P.dma("sp", SBT[:, :, :], dr["k_sb"][t * TT:(t + 1) * TT].rearrange("(s p) j -> p s j", p=128), w=[SBT.b])
        for g in range(2):
            first = [True] * 4
            vaug_b[0] = [CVA.b]
            for hh in range(4):
                attn_chunk(g, hh, (CKT, slice(0, 128), CKT.b), QW, MSK[:, 8 + t, :], [MSK.b], CVA[:, g, :], 97, range(4), first)
            flush_pv()
            attn_evac(g, 0, 97, True)
            if t >= 2:
                for sub in range(4):
                    sm = nxt("sm", SM)
                    sc, sc2, m1, m2, fl = sm[:, 0:32], sm[:, 32:64], sm[:, 64:72], sm[:, 72:80], sm[:, 96:128]
                    tt("dve", sc, IMP[:, sub, g, :], SBT[:, sub, :], ALU.add, [IMP.b, SBT.b], [sm.b])
                    P.op("dve", lambda e, sc=sc, m1=m1: e.max(out=m1, in_=sc), r=[sm.b], w=[sm.b])
                    P.op("dve", lambda e, sc=sc, sc2=sc2, m1=m1: e.match_replace(out=sc2, in_to_replace=m1, in_values=sc, imm_value=-1.0e30), r=[sm.b], w=[sm.b])
                    P.op("dve", lambda e, sc2=sc2, m2=m2: e.max(out=m2, in_=sc2), r=[sm.b], w=[sm.b])
                    P.op("dve", lambda e, sc=sc, m2=m2, fl=fl: e.tensor_scalar(out=fl, in0=sc, scalar1=m2[:, 7:8], scalar2=None, op0=ALU.is_ge), r=[sm.b], w=[sm.b])
                    P.op("pe", lambda e, fl=fl, sub=sub: e.transpose(out=PS[7][0:32, sub * 128:(sub + 1) * 128], in_=fl, identity=identf[:, :]), r=[sm.b, identf.b], w=[PS[7].b])
                evac("act", SELT[:, g, :], PS[7][0:32, :], [PS[7].b], [SELT.b])
            first = [True] * 4
            vaug_b[0] = [WINV.b]
            for c in range(max(0, 4 * t - 4), 4 * t + 4):
                if c >= 4 * t:
                    i = c - 4 * t
                    mask, subs = MSK[:, i, :], range(i, 4)
                else:
                    i = c - (4 * t - 4)
                    mask, subs = MSK[:, 4 + i, :], range(0, i + 1)
                for hh in range(4):
                    attn_chunk(g, hh, (WINK, slice(c * 128, (c + 1) * 128), WINK.b), QR, mask, [MSK.b], WINV[:, c, g, :], 65, subs, first)
            flush_pv()
            attn_evac(g, 2, 65, False)
            first = [True] * 4
            vaug_b[0] = [SELV.b]
            for c in range(0, 4 * t + 4):
                i = c - 4 * t
                subs = range(max(i, 0), 4)
                mbufs = [MSK.b]
                if t >= 2:
                    mm(PS[6][:, :], E32[:, c * 128:(c + 1) * 128], SELT[:, g, :], True, True, [E32.b, SELT.b], [PS[6].b])
                    mk = nxt("mks", MKS)
                    if i >= 0:
                        tt("dve", mk[:, :], PS[6][:, :], MSK[:, i, :], ALU.mult, [PS[6].b, MSK.b], [mk.b])
                    else:
                        evac("act", mk[:, :], PS[6][:, :], [PS[6].b], [mk.b])
                    mask, mbufs = mk[:, :], [mk.b]
                else:
                    mask = MSK[:, i, :] if i >= 0 else None
                for hh in range(4):
                    attn_chunk(g, hh, (SELK, slice(c * 128, (c + 1) * 128), SELK.b), QR, mask, mbufs, SELV[:, c, g, :], 65, subs, first)
            flush_pv()
            attn_evac(g, 1, 65, False)
        for sub in range(4):
            for i in range(4):
                P.op("pe", lambda e, sub=sub, i=i: e.transpose(out=PS[7][:, i * 128:(i + 1) * 128], in_=OA[sub][:, i * 128:(i + 1) * 128], identity=identf[:, :]),
                     r=[OA[sub].b, identf.b], w=[PS[7].b])
            evac("act", OT[:, :, sub * 128:(sub + 1) * 128], PS[7][:, :].rearrange("p (i t) -> p i t", i=4), [PS[7].b], [OT.b])

    def rglru(n, first_tile):
        for half in range(2):
            ix, sx = WP.acquire(f"in_xr{half}")
            ig, sgr = WP.acquire(f"in_gr{half}")
            for j in range(4):
                i = half * 4 + j
                px, pg = PS[j % 2], PS[2 + j % 2]
                for kc in range(8):
                    mm(px[:, 0:n], sx[:, kc * 512 + j * 128:kc * 512 + (j + 1) * 128], A[:, kc, 0:n], kc == 0, kc == 7, [sx.b, A.bs[kc]], [px.b])
                for kc in range(8):
                    mm(pg[:, 0:n], sgr[:, kc * 512 + j * 128:kc * 512 + (j + 1) * 128], A[:, kc, 0:n], kc == 0, kc == 7, [sgr.b, A.bs[kc]], [pg.b])
                xb = nxt("xb", XB)
                evac("pool", xb[:, 0:3], XH[:, i, :], [XH.b], [xb.b])
                evac("act", xb[:, 3:3 + n], px[:, 0:n], [px.b], [xb.b])
                evac("pool", XH[:, i, :], xb[:, n:n + 3], [xb.b], [XH.b])
                xc = nxt("tf", TMPF)
                e1 = "dve"
                P.op(e1, lambda e, xb=xb, xc=xc, i=i: e.tensor_scalar(out=xc[:, 0:n], in0=xb[:, 0:n], scalar1=RGP[:, 0, i:i + 1], scalar2=RGP[:, 4, i:i + 1], op0=ALU.mult, op1=ALU.add),
                     r=[xb.b, RGP.b], w=[xc.b])
                for k in range(1, 4):
                    P.op(e1, lambda e, xb=xb, xc=xc, i=i, k=k: e.scalar_tensor_tensor(out=xc[:, 0:n], in0=xb[:, k:k + n], scalar=RGP[:, k, i:i + 1], in1=xc[:, 0:n], op0=ALU.mult, op1=ALU.add),
                         r=[xb.b, RGP.b, xc.b], w=[xc.b])
                xcb = nxt("xcb", XCB)
                evac("act", xcb[:, 0:n], xc[:, 0:n], [xc.b], [xcb.b])
                pr, pi = PS[4 + j % 2], PS[6 + j % 2]
                mm(pr[:, 0:n], WAX[0][:, i, :], xcb[:, 0:n], True, True, [WAX[0].b, xcb.b], [pr.b])
                mm(pi[:, 0:n], WAX[1][:, i, :], xcb[:, 0:n], True, True, [WAX[1].b, xcb.b], [pi.b])
                ra = nxt("tf", TMPF)
                ii = nxt("tf", TMPF)
                P.op("act", lambda e, ra=ra, pr=pr, i=i: e.activation(out=ra[:, 0:n], in_=pr[:, 0:n], func=AF.Sigmoid, bias=RGP[:, 5, i:i + 1]), r=[pr.b, RGP.b], w=[ra.b])
                P.op("act", lambda e, ii=ii, pi=pi, i=i: e.activation(out=ii[:, 0:n], in_=pi[:, 0:n], func=AF.Sigmoid, bias=RGP[:, 6, i:i + 1]), r=[pi.b, RGP.b], w=[ii.b])
                P.op("act", lambda e, ra=ra, i=i: e.activation(out=ra[:, 0:n], in_=ra[:, 0:n], func=AF.Exp, scale=RGP[:, 7, i:i + 1]), r=[ra.b, RGP.b], w=[ra.b])
                sq_ = nxt("tf", TMPF)
                tt("pool", sq_[:, 0:n], ra[:, 0:n], ra[:, 0:n], ALU.mult, [ra.b], [sq_.b])
                P.op("act", lambda e, sq_=sq_: e.activation(out=sq_[:, 0:n], in_=sq_[:, 0:n], func=AF.Sqrt, scale=-1.0, bias=1.0), r=[sq_.b], w=[sq_.b])
                tt("pool", ii[:, 0:n], ii[:, 0:n], sq_[:, 0:n], ALU.mult, [ii.b, sq_.b], [ii.b])
                tt("dve", ii[:, 0:n], ii[:, 0:n], xc[:, 0:n], ALU.mult, [ii.b, xc.b], [ii.b])
                P.op("dve", lambda e, sq_=sq_, ra=ra, ii=ii, i=i: e.tensor_tensor_scan(out=sq_[:, 0:n], data0=ra[:, 0:n], data1=ii[:, 0:n], initial=HST[:, i:i + 1], op0=ALU.mult, op1=ALU.add),
                     r=[ra.b, ii.b, HST.b, sq_.b], w=[sq_.b])
                evac("pool", HST[:, i:i + 1], sq_[:, n - 1:n], [sq_.b], [HST.b])
                P.op("act", lambda e, ra=ra, pg=pg: e.activation(out=ra[:, 0:n], in_=pg[:, 0:n], func=AF.Gelu_apprx_tanh), r=[pg.b, ra.b], w=[ra.b])
                tt("dve", YR[:, i, 0:n], sq_[:, 0:n], ra[:, 0:n], ALU.mult, [sq_.b, ra.b], [YR.bs[i]])
            WP.release(ix)
            WP.release(ig)

    def merge(n, ot_fn):
        for j in range(2):
            iga, sga = WP.acquire(f"in_ga{j}")
            igb, sgb = WP.acquire(f"in_gb{j}")
            for q in range(4):
                for which, sw in ((0, sga), (1, sgb)):
                    ps = PS[2 * which + q % 2]
                    for kc in range(8):
                        mm(ps[:, 0:n], sw[:, kc * 512 + q * 128:kc * 512 + (q + 1) * 128], A[:, kc, 0:n], kc == 0, kc == 7, [sw.b, A.bs[kc]], [ps.b])
                    P.op("act", lambda e, ps=ps, which=which, q=q: e.activation(out=SG[which][:, q, 0:n], in_=ps[:, 0:n], func=AF.Sigmoid), r=[ps.b], w=[SG[which].b])
            WP.release(iga)
            WP.release(igb)
            iba, sba = WP.acquire(f"ba{j}")
            ibr, sbr = WP.acquire(f"br{j}")
            for q in range(4):
                c = j * 4 + q
                pa, pb = PS[4 + q % 2], PS[6 + q % 2]
                ots = ot_fn()
                for kc, (oap, obufs, krow0, kn) in enumerate(ots):
                    mm(pa[:, 0:n], sba[krow0 % 128:krow0 % 128 + kn, (krow0 // 128) * 512 + q * 128:(krow0 // 128) * 512 + (q + 1) * 128], oap, kc == 0, kc == len(ots) - 1, [sba.b] + obufs, [pa.b])
                for kc in range(8):
                    mm(pb[:, 0:n], sbr[:, kc * 512 + q * 128:kc * 512 + (q + 1) * 128], YR[:, kc, 0:n], kc == 0, kc == 7, [sbr.b, YR.bs[kc]], [pb.b])
                t1 = nxt("tf", TMPF)
                t2 = nxt("tf", TMPF)
                tt("dve", t1[:, 0:n], pa[:, 0:n], SG[0][:, q, 0:n], ALU.mult, [pa.b, SG[0].b], [t1.b])
                tt("dve", t2[:, 0:n], pb[:, 0:n], SG[1][:, q, 0:n], ALU.mult, [pb.b, SG[1].b], [t2.b])
                tt("pool", M[:, c, 0:n], t1[:, 0:n], t2[:, 0:n], ALU.add, [t1.b, t2.b], [M.bs[c]])
            WP.release(iba)
            WP.release(ibr)
        for h in range(2):
            iw, sw = WP.acquire(f"wo{h}")
            for q in range(4):
                c = h * 4 + q
                ps = PS[q] if h == 0 else PS[4 + q]
                for kc in range(8):
                    mm(ps[:, 0:n], sw[:, kc * 512 + q * 128:kc * 512 + (q + 1) * 128], M[:, kc, 0:n], kc == 0, kc == 7, [sw.b, M.bs[kc]], [ps.b])
                P.op("dve", lambda e, ps=ps, c=c: e.scalar_tensor_tensor(out=R[:, c, 0:n], in0=ps[:, 0:n], scalar=1.0 / ALPHA, in1=R[:, c, 0:n], op0=ALU.mult, op1=ALU.add),
                     r=[ps.b, R.bs[c]], w=[R.bs[c]])
            WP.release(iw)

    class _Stop(Exception):
        pass

    def ck(tag):
        if cfg.get("sstop") == tag:
            raise _Stop()

    def sample_pass():
        ess = contextlib.ExitStack()
        try:
            sample_body(ess)
        except _Stop:
            pass
        P.barrier()
        ess.close()

    def sample_body(ess):
        n = NSMP

        def TS(name, shape, dt):
            return T(nc, ess, "s_" + name, shape, dt)
        PTB, PTF = TS("ptb", [128, 4], I32), TS("ptf", [128, 4], F32)
        PTI, PTHL = TS("pti", [128, 4, 2], I32), TS("pthl", [128, 4, 2], BF16)
        IDXF, IDXG = TS("idxf", [128, 16], F32), TS("idxg", [128, 16], I32)
        KSM, K16 = TS("ksm", [128, 32], F32), TS("k16", [128, 16], F32)
        PGS = [TS(f"pg{i}", [128, 8 * 256], F32) for i in range(2)]
        PG = PGS[0]
        XT = [TS(f"xt{i}", [128, 2, 8, 128], BF16) for i in range(2)]
        CKS, GUS = TS("cks", [128, 1024], BF16), TS("gus", [128, 2, 1024], BF16)
        CVS = TS("cvs", [128, 8, 2, 65], BF16)
        COV = TS("cov", [128, 8, 256], BF16)
        RS = [TS(f"rs{i}", [128, 256], F32) for i in range(2)]
        KTS, VAS = TS("kts", [128, 1024], BF16), TS("vas", [128, 8, 65], BF16)
        KTW, VAW = TS("ktw", [128, 512], BF16), TS("vaw", [128, 4, 2, 65], BF16)
        KSELF, VSELF = TS("kself", [128, 2, 4], BF16), TS("vself", [4, 2, 2, 65], BF16)
        GT, OASP, OTS = TS("gt", [4, 2, 3, 4], F32), TS("oasp", [4, 4, 2, 128], F32), TS("ots", [128, 2, 4, 4], BF16)
        EXS, PTS = TS("exs", [128, 8, 4], F32), TS("pts", [128, 8, 4], BF16)
        EX4, PT4 = TS("ex4", [4, 4], F32), TS("pt4", [4, 4], BF16)
        COVF = TS("covf", [4, 256], F32)
        SBS, SC, SC2 = TS("sbs", [1, 256], F32), TS("sc", [1, 256], F32), TS("sc2", [1, 256], F32)
        MX, IXU, JROW, JROWB = TS("mx", [1, 16], F32), TS("ixu", [1, 16], U32), TS("jrow", [1, 16], F32), TS("jrowb", [1, 16], BF16)
        ONES1 = TS("ones1", [1, 128], BF16)
        JB, JBI, OH = TS("jb", [128, 16], F32), TS("jbi", [128, 16], I32), TS("oh", [128, 16], BF16)
        JF, JI, VAL, BB = TS("jf", [16, 1], F32), TS("ji", [16, 1], I32), TS("val", [16, 3], F32), TS("bb", [16, 24], BF16)
        A16F, A16B, M2 = TS("a16f", [16, 128], F32), TS("a16b", [16, 128], BF16), TS("m2", [16, 8], F32)
        ROWF, ROWI = TS("rowf", [128, 8], F32), TS("rowi", [128, 8], I32)
        SCV, SHS, XRS, HS = TS("scv", [128, 8, 3, 4], F32), TS("shs", [128, 8, 4], F32), TS("xrs", [128, 8, 4], F32), TS("hs", [128, 8, 4], F32)
        d2d = Buf("d2d", track_w=False)

        P.dma("sp", KSM[:, :], dr["k_small"], w=[KSM.b])
        P.dma("sp", K16[:, :], dr["k_iota16"], w=[K16.b])
        P.dma("sp", SBS[:, :], dr["k_sbs"], w=[SBS.b])
        P.dma("sp", A16F[:, :], dr["k_a16"], w=[A16F.b])
        P.dma("sp", M2[:, :], dr["k_m2"], w=[M2.b])
        evac("dve", A16B[:, :], A16F[:, :], [A16F.b], [A16B.b])
        P.op("pool", lambda e: e.memset(ONES1[:, :], 1.0), w=[ONES1.b])
        P.dma("sp", PTB[:, :], dr["ptab"].rearrange("b p -> p b"), w=[PTB.b], allow_slow_non_contiguous=True)
        evac("dve", PTF[:, :], PTB[:, :], [PTB.b], [PTF.b])
        P.op("dve", lambda e: e.tensor_scalar(out=PTI[:, :, 0], in0=PTB[:, :], scalar1=6, scalar2=None, op0=ALU.arith_shift_right), r=[PTB.b], w=[PTI.b])
        P.op("dve", lambda e: e.tensor_scalar(out=PTI[:, :, 1], in0=PTB[:, :], scalar1=63, scalar2=None, op0=ALU.bitwise_and), r=[PTB.b], w=[PTI.b])
        evac("dve", PTHL[:, :, :], PTI[:, :, :], [PTI.b], [PTHL.b])
        P.op("pool", lambda e: e.memset(CVS[:, :, :, :], 1.0), w=[CVS.b])
        P.dma("sp", PG[:, :], dr["k_covs"].rearrange("p j c -> p (j c)"), w=[PG.b])
        evac("dve", COV[:, :, :], PG[:, :].rearrange("p (j c) -> p j c", j=8), [PG.b], [COV.b])
        P.op("pool", lambda e: e.memset(VSELF[:, :, :, :], 1.0), w=[VSELF.b])
        P.op("pool", lambda e: e.memset(VAS[:, :, :], 1.0), w=[VAS.b])
        P.op("pool", lambda e: e.memset(VAW[:, :, :, :], 1.0), w=[VAW.b])
        for b in range(n):
            for k in range(3):
                P.dma("sp", SCV[:, :, k, b], dr["sconv"][b, k].rearrange("(c p) -> p c", p=128), w=[SCV.b], allow_slow_non_contiguous=True)
            P.dma("sp", SHS[:, :, b], dr["sh"][b].rearrange("(c p) -> p c", p=128), w=[SHS.b], allow_slow_non_contiguous=True)
            P.dma("sp", out["win_s"][b, 0:511, :], dr["cwin"][b, 1:512, :], w=[d2d], sem=KTW.b)
            P.dma("sp", out["conv_s"][b, 0:2, :], dr["sconv"][b, 1:3, :], w=[d2d], sem=KTW.b)

        load_x(lambda sub, m: dr["xs"][0:m, :], n)
        ffn(1, n)
        layernorm(0, n)

        ck("s1")
        for b in range(n):
            P.dma("sp", CS[b:b + 1, 0, 0:32], dr["k_cos"][SEQ:SEQ + 1, :], w=[CS.b])
            P.dma("sp", CS[b:b + 1, 0, 32:64], dr["k_sin"][SEQ:SEQ + 1, :], w=[CS.b])
        cosv, sinv = CS[0:n, 0, 0:32], CS[0:n, 0, 32:64]
        iq, sq = WP.acquire("in_q")
        ps = PS[0]
        for kc in range(8):
            mm(ps[0:n, :], A[:, kc, 0:n], sq[:, kc * 512:(kc + 1) * 512], kc == 0, kc == 7, [A.bs[kc], sq.b], [ps.b])
        zs = nxt("zs", ZS)
        evac("act", zs[0:n, :], ps[0:n, :], [ps.b], [zs.b])
        rope_inplace(zs, zs[0:n, :].rearrange("p (h d) -> p h d", d=64), cosv, sinv, [n, 8])
        for hh in range(4):
            P.op("pe", lambda e, zs=zs, hh=hh: e.transpose(out=PS[2][:, hh * n:(hh + 1) * n], in_=zs[0:n, hh * 128:(hh + 1) * 128], identity=identf[0:n, 0:n]),
                 r=[zs.b, identf.b], w=[PS[2].b])
        evac("act", QR[:, :, 0:n], PS[2][:, 0:4 * n].rearrange("p (h t) -> p h t", h=4), [PS[2].b], [QR.b])
        for hh in range(4):
            pf = PS[4 + hh % 2]
            for kc in range(8):
                mm(pf[:, 0:n], sq[:, kc * 512 + hh * 128:kc * 512 + (hh + 1) * 128], A[:, kc, 0:n], kc == 0, kc == 7, [sq.b, A.bs[kc]], [pf.b])
            evac("dve", QW[:, hh, 0:n], pf[:, 0:n], [pf.b], [QW.b])
        WP.release(iq)
        ikv, skv = WP.acquire("in_kv")
        ps = PS[1]
        for kc in range(8):
            mm(ps[0:n, :], A[:, kc, 0:n], skv[:, kc * 512:(kc + 1) * 512], kc == 0, kc == 7, [A.bs[kc], skv.b], [ps.b])
        zs = nxt("zs", ZS)
        evac("act", zs[0:n, :], ps[0:n, :], [ps.b], [zs.b])
        P.dma("act", out["cmp_s"][:, :], zs[0:n, 0:256], r=[zs.b], w=[outb["cmp_s"]], sem=zs.b)
        rope_inplace(zs, zs[0:n, 256:384].rearrange("p (h d) -> p h d", d=64), cosv, sinv, [n, 2])
        P.dma("act", out["sel_s"][:, :], zs[0:n, 256:512], r=[zs.b], w=[outb["sel_s"]], sem=zs.b)
        P.op("pe", lambda e, zs=zs: e.transpose(out=PS[3][:, 0:n], in_=zs[0:n, 256:384], identity=identf[0:n, 0:n]), r=[zs.b, identf.b], w=[PS[3].b])
        evac("dve", KSELF[:, 0, :], PS[3][:, 0:n], [PS[3].b], [KSELF.b])
        evac("pool", VSELF[0:n, 0, :, 0:64], zs[0:n, 384:512].rearrange("p (g d) -> p g d", g=2), [zs.b], [VSELF.b])
        WP.release(ikv)
        ik2, sk2 = WP.acquire("in_kv2")
        ps = PS[0]
        for kc in range(8):
            mm(ps[0:n, 0:280], A[:, kc, 0:n], sk2[:, kc * 280:(kc + 1) * 280], kc == 0, kc == 7, [A.bs[kc], sk2.b], [ps.b])
        zs = nxt("zs", ZS)
        evac("act", zs[0:n, 0:256], ps[0:n, 0:256], [ps.b], [zs.b])
        P.op("act", lambda e, ps=ps: e.activation(out=GATES[0:n, 0, :], in_=ps[0:n, 256:280], func=AF.Sigmoid), r=[ps.b], w=[GATES.b])
        rope_inplace(zs, zs[0:n, 0:128].rearrange("p (h d) -> p h d", d=64), cosv, sinv, [n, 2])
        P.dma("act", out["win_s"][:, 511, :], zs[0:n, 0:256], r=[zs.b], w=[outb["win_s"]], sem=zs.b)
        P.op("pe", lambda e, zs=zs: e.transpose(out=PS[3][:, 0:n], in_=zs[0:n, 0:128], identity=identf[0:n, 0:n]), r=[zs.b, identf.b], w=[PS[3].b])
        evac("dve", KSELF[:, 1, :], PS[3][:, 0:n], [PS[3].b], [KSELF.b])
        evac("pool", VSELF[0:n, 1, :, 0:64], zs[0:n, 128:256].rearrange("p (g d) -> p g d", g=2), [zs.b], [VSELF.b])
        WP.release(ik2)
        gv = GATES[0:n, 0, :].rearrange("p (g h b) -> p g h b", g=2, h=4, b=3)
        for g in range(2):
            for br in range(3):
                P.op("pe", lambda e, g=g, br=br: e.transpose(out=PS[3][0:4, (g * 3 + br) * 4:(g * 3 + br) * 4 + 4], in_=gv[:, g, :, br], identity=identf[0:n, 0:n]),
                     r=[GATES.b, identf.b], w=[PS[3].b])
        evac("dve", GT[:, :, :, :], PS[3][0:4, 0:24].rearrange("p (g r b) -> p g r b", g=2, r=3), [PS[3].b], [GT.b])

        ck("s2")
        ik, swk = WP.acquire("w1bd_k")
        iv, swv = WP.acquire("w1bd_v")
        sw1 = (swk, swv)
        ccv = dr["ccmp"].rearrange("n (c r) x -> (n c) (r x)", r=8)
        crow = dr["csel"].rearrange("n r x -> (n r) x")

        def small_evac(bank, ncol, g, br, b, first_branch):
            sm = nxt("sm", SM)
            rs_, rg = sm[0:4, 0:1], sm[0:4, 1:2]
            P.op("dve", lambda e: e.tensor_scalar(out=rs_, in0=bank[0:4, 64:65], scalar1=1e-30, scalar2=None, op0=ALU.max), r=[bank.b], w=[sm.b])
            P.op("dve", lambda e: e.reciprocal(out=rs_, in_=rs_), r=[sm.b], w=[sm.b])
            tt("dve", rg, rs_, GT[0:4, g, br, b:b + 1], ALU.mult, [sm.b, GT.b], [sm.b])
            if first_branch:
                P.op("dve", lambda e: e.tensor_scalar(out=OASP[0:4, b, g, 0:64], in0=bank[0:4, 0:64], scalar1=rg, scalar2=None, op0=ALU.mult), r=[bank.b, sm.b], w=[OASP.b])
            else:
                P.op("dve", lambda e: e.scalar_tensor_tensor(out=OASP[0:4, b, g, 0:64], in0=bank[0:4, 0:64], scalar=rg, in1=OASP[0:4, b, g, 0:64], op0=ALU.mult, op1=ALU.add),
                     r=[bank.b, sm.b, OASP.b], w=[OASP.b])
            return sm, rs_

        def small_attn(b, g, kt, kcols, nchunk, vfn, kself_i, br, last_mask, first_branch):
            hp = slice(64 * g, 64 * g + 64)
            qv = QR[hp, :, b]
            for c in range(nchunk):
                S = PS[4 + c % 2]
                mm(S[:, 0:4], kt[hp, c * 128:(c + 1) * 128], qv, True, True, [kt.b, QR.b], [S.b])
                P.op("act", lambda e, S=S, c=c: e.activation(out=EXS[:, c, :], in_=S[:, 0:4], func=AF.Exp, scale=0.125), r=[S.b], w=[EXS.b])
                if last_mask and c == nchunk - 1:
                    P.op("dve", lambda e, c=c: e.tensor_scalar(out=PTS[:, c, :], in0=EXS[:, c, :], scalar1=KSM[:, 1:2], scalar2=None, op0=ALU.mult), r=[EXS.b, KSM.b], w=[PTS.b])
                else:
                    evac("dve", PTS[:, c, :], EXS[:, c, :], [EXS.b], [PTS.b])
                vap, vb = vfn(c)
                P.op("pe", lambda e, c=c, vap=vap: e.matmul(PS[0][0:4, 0:65], lhsT=PTS[:, c, :], rhs=vap, start=(c == 0), stop=False, skip_group_check=True), r=[PTS.b, vb], w=[PS[0].b])
            S = PS[6]
            mm(S[0:4, 0:4], KSELF[hp, kself_i, :], qv, True, True, [KSELF.b, QR.b], [S.b])
            P.op("act", lambda e: e.activation(out=EX4[:, :], in_=S[0:4, 0:4], func=AF.Exp, scale=0.125), r=[S.b], w=[EX4.b])
            P.op("dve", lambda e: e.tensor_scalar(out=PT4[:, :], in0=EX4[:, :], scalar1=KSM[0:4, 16 + b:17 + b], scalar2=None, op0=ALU.mult), r=[EX4.b, KSM.b], w=[PT4.b])
            P.op("pe", lambda e: e.matmul(PS[0][0:4, 0:65], lhsT=PT4[:, :], rhs=VSELF[0:4, kself_i, g, :], start=False, stop=True, skip_group_check=True), r=[PT4.b, VSELF.b], w=[PS[0].b])
            small_evac(PS[0], 65, g, br, b, first_branch)

        for b in range(n):
            P.op("dve", lambda e, b=b: e.tensor_scalar(out=IDXF[:, :], in0=K16[:, :], scalar1=PTF[:, b:b + 1], scalar2=None, op0=ALU.bypass) if False else
                 e.scalar_tensor_tensor(out=IDXF[:, :], in0=PTF[:, b:b + 1].to_broadcast([128, 16]), scalar=16.0, in1=K16[:, :], op0=ALU.mult, op1=ALU.add),
                 r=[PTF.b, K16.b, IDXG.b], w=[IDXF.b])
            evac("dve", IDXG[:, :], IDXF[:, :], [IDXF.b], [IDXG.b])
            started = [False] * 4
            for jj in range(8):
                for sh in range(2):
                    PGc = PGS[(jj * 2 + sh) % 2]
                    P.idma(PGc[:, :], ccv, IDXG[:, jj * 2 + sh:jj * 2 + sh + 1], r=[IDXG.b], w=[PGc.b])
                    xt = XT[(jj * 2 + sh) % 2]
                    for kv in range(2):
                        for rq in range(2):
                            bank = PS[4 + kv * 2 + rq]
                            for r4 in range(4):
                                row = rq * 4 + r4
                                P.op("pe", lambda e, bank=bank, r4=r4, row=row, kv=kv, PGc=PGc: e.transpose(out=bank[:, r4 * 128:(r4 + 1) * 128], in_=PGc[:, row * 256 + kv * 128:row * 256 + (kv + 1) * 128], identity=identf[:, :]),
                                     r=[PGc.b, identf.b], w=[bank.b])
                            evac("act" if rq == 0 else "dve", xt[:, kv, rq * 4:(rq + 1) * 4, :], bank[:, :].rearrange("p (r c) -> p r c", r=4), [bank.b], [xt.b])
                    for kv in range(2):
                        for sl in range(8):
                            s_ = sh * 8 + sl
                            bk = kv * 2 + jj // 4
                            P.op("pe", lambda e, bk=bk, kv=kv, sl=sl, s_=s_, jj=jj, xt=xt, st=not started[bk]: e.matmul(PS[bk][:, (jj % 4) * 128:(jj % 4 + 1) * 128], lhsT=sw1[kv][:, s_ * 128:(s_ + 1) * 128], rhs=xt[:, kv, sl, :], start=st, stop=False, skip_group_check=True),
                                 r=[sw1[kv].b, xt.b], w=[PS[bk].b])
                            started[bk] = True
                            if jj >= 1:
                                j1 = jj - 1
                                bk = kv * 2 + j1 // 4
                                oap, rap = PS[bk][:, (j1 % 4) * 128:(j1 % 4 + 1) * 128], xt[:, kv, sl, :]
                            else:
                                bk = kv * 2 + 1
                                oap, rap = PS[bk][:, 3 * 128:3 * 128 + 127], xt[:, kv, sl, 1:128]
                            P.op("pe", lambda e, oap=oap, rap=rap, kv=kv, s_=s_, st=not started[bk]: e.matmul(oap, lhsT=sw1[kv][:, (16 + s_) * 128:(17 + s_) * 128], rhs=rap, start=st, stop=False, skip_group_check=True),
                                 r=[sw1[kv].b, xt.b], w=[PS[bk].b])
                            started[bk] = True
            for kv in range(2):
                for hb in range(2):
                    bank = PS[kv * 2 + hb]
                    P.op("act", lambda e, bank=bank, kv=kv, hb=hb: e.activation(out=GUS[:, kv, hb * 512:(hb + 1) * 512], in_=bank[:, :], func=AF.Gelu_apprx_tanh, bias=PEB[:, kv:kv + 1]),
                         r=[bank.b, PEB.b], w=[GUS.b])
            for hb in range(2):
                mm(PS[4 + hb][:, :], W2BD[0][:, :], GUS[:, 0, hb * 512:(hb + 1) * 512], True, True, [W2BD[0].b, GUS.b], [PS[4 + hb].b])
                evac("dve", CKS[:, hb * 512:(hb + 1) * 512], PS[4 + hb][:, :], [PS[4 + hb].b], [CKS.b])
            for jj in range(8):
                bank = PS[6 + jj % 2]
                mm(bank[:, 0:128], GUS[:, 1, jj * 128:(jj + 1) * 128], W2BD[1][:, :], True, True, [W2BD[1].b, GUS.b], [bank.b])
                evac("act", CVS[:, jj, :, 0:64], bank[:, 0:128].rearrange("p (g d) -> p g d", g=2), [bank.b], [CVS.b])
            ck("s3")
            WGv = PG[:, 0:1024].rearrange("p (c x) -> p c x", c=4)
            P.dma("sp", WGv, dr["cwin"][b].rearrange("(c p) x -> p c x", p=128), w=[PG.b])
            for c in range(4):
                bank = PS[2 + c % 2]
                P.op("pe", lambda e, bank=bank, c=c: e.transpose(out=bank[:, 0:128], in_=PG[:, c * 256:c * 256 + 128], identity=identf[:, :]), r=[PG.b, identf.b], w=[bank.b])
                evac("dve", KTW[:, c * 128:(c + 1) * 128], bank[:, 0:128], [bank.b], [KTW.b])
                evac("pool", VAW[:, c, :, 0:64], PG[:, c * 256 + 128:c * 256 + 256].rearrange("p (g d) -> p g d", g=2), [PG.b], [VAW.b])
            for g in range(2):
                hp = slice(64 * g, 64 * g + 64)
                qw = QW[hp, :, b]
                for jj in range(8):
                    S = PS[4 + jj % 2]
                    mm(S[:, 0:4], CKS[hp, jj * 128:(jj + 1) * 128], qw, True, True, [CKS.b, QW.b], [S.b])
                    P.op("act", lambda e, S=S, jj=jj: e.activation(out=EXS[:, jj, :], in_=S[:, 0:4], func=AF.Exp, scale=0.125), r=[S.b], w=[EXS.b])
                    P.op("dve", lambda e, jj=jj: e.tensor_scalar(out=PTS[:, jj, :], in0=EXS[:, jj, :], scalar1=KSM[:, 8 + jj:9 + jj], scalar2=None, op0=ALU.mult), r=[EXS.b, KSM.b], w=[PTS.b])
                    P.op("pe", lambda e, jj=jj, g=g: e.matmul(PS[0][0:4, 0:65], lhsT=PTS[:, jj, :], rhs=CVS[:, jj, g, :], start=(jj == 0), stop=(jj == 7), skip_group_check=True), r=[PTS.b, CVS.b], w=[PS[0].b])
                    P.op("pe", lambda e, jj=jj: e.matmul(PS[0][0:4, 65:321], lhsT=PTS[:, jj, :], rhs=COV[:, jj, :], start=False, stop=(jj == 7), skip_group_check=True), r=[PTS.b, COV.b], w=[PS[0].b])
                sm, rs_ = small_evac(PS[0], 322, g, 0, b, True)
                evac("act", COVF[:, :], PS[0][0:4, 65:321], [PS[0].b], [COVF.b])
                mm(PS[1][0:1, 0:256], rs_, COVF[:, :], True, True, [sm.b, COVF.b], [PS[1].b])
                ck("s4")
                tt("dve", SC[:, :], PS[1][0:1, 0:256], SBS[:, :], ALU.add, [PS[1].b, SBS.b], [SC.b])
                P.op("dve", lambda e: e.max(out=MX[:, 0:8], in_=SC[:, :]), r=[SC.b], w=[MX.b])
                P.op("dve", lambda e: e.max_index(out=IXU[:, 0:8], in_max=MX[:, 0:8], in_values=SC[:, :]), r=[SC.b, MX.b], w=[IXU.b])
                P.op("dve", lambda e: e.match_replace(out=SC2[:, :], in_to_replace=MX[:, 0:8], in_values=SC[:, :], imm_value=-1.0e30), r=[SC.b, MX.b], w=[SC2.b])
                P.op("dve", lambda e: e.max(out=MX[:, 8:16], in_=SC2[:, :]), r=[SC2.b], w=[MX.b])
                P.op("dve", lambda e: e.max_index(out=IXU[:, 8:16], in_max=MX[:, 8:16], in_values=SC2[:, :]), r=[SC2.b, MX.b], w=[IXU.b])
                evac("dve", JROW[:, :], IXU[:, :], [IXU.b], [JROW.b])
                evac("dve", JROWB[:, :], JROW[:, :], [JROW.b], [JROWB.b])
                mm(PS[1][:, 256:272], ONES1[:, :], JROWB[:, :], True, True, [ONES1.b, JROWB.b], [PS[1].b])
                evac("dve", JBI[:, :], PS[1][:, 256:272], [PS[1].b], [JBI.b])
                P.op("dve", lambda e: e.tensor_scalar(out=JBI[:, :], in0=JBI[:, :], scalar1=1, scalar2=None, op0=ALU.arith_shift_right), r=[JBI.b], w=[JBI.b])
                evac("dve", JB[:, :], JBI[:, :], [JBI.b], [JB.b])
                P.op("dve", lambda e: e.tensor_scalar(out=OH[:, :], in0=JB[:, :], scalar1=KSM[:, 20:21], scalar2=None, op0=ALU.is_equal), r=[JB.b, KSM.b], w=[OH.b])
                mm(PS[1][0:16, 272:274], OH[:, :], PTHL[:, b, :], True, True, [OH.b, PTHL.b], [PS[1].b])
                evac("dve", VAL[:, 0:2], PS[1][0:16, 272:274], [PS[1].b], [VAL.b])
                P.op("pe", lambda e: e.transpose(out=PS[1][0:16, 274:275], in_=JROW[0:1, :], identity=identf[0:1, 0:1]), r=[JROW.b, identf.b], w=[PS[1].b])
                evac("dve", JI[:, :], PS[1][0:16, 274:275], [PS[1].b], [JI.b])
                P.op("dve", lambda e: e.tensor_scalar(out=JI[:, :], in0=JI[:, :], scalar1=1, scalar2=None, op0=ALU.bitwise_and), r=[JI.b], w=[JI.b])
                evac("dve", VAL[:, 2:3], JI[:, :], [JI.b], [VAL.b])
                for w_ in range(3):
                    P.op("dve", lambda e, w_=w_: e.tensor_scalar(out=BB[:, w_ * 8:(w_ + 1) * 8], in0=M2[:, :], scalar1=VAL[:, w_:w_ + 1], scalar2=None, op0=ALU.mult), r=[M2.b, VAL.b], w=[BB.b])
                mm(PS[1][:, 280:304], A16B[:, :], BB[:, :], True, True, [A16B.b, BB.b], [PS[1].b])
                P.op("dve", lambda e: e.tensor_scalar(out=ROWF[:, :], in0=PS[1][:, 280:288], scalar1=8192.0, scalar2=KSM[:, 0:1], op0=ALU.mult, op1=ALU.add), r=[PS[1].b, KSM.b], w=[ROWF.b])
                P.op("dve", lambda e: e.scalar_tensor_tensor(out=ROWF[:, :], in0=PS[1][:, 288:296], scalar=128.0, in1=ROWF[:, :], op0=ALU.mult, op1=ALU.add), r=[PS[1].b, ROWF.b], w=[ROWF.b])
                P.op("dve", lambda e: e.scalar_tensor_tensor(out=ROWF[:, :], in0=PS[1][:, 296:304], scalar=64.0, in1=ROWF[:, :], op0=ALU.mult, op1=ALU.add), r=[PS[1].b, ROWF.b], w=[ROWF.b])
                evac("dve", ROWI[:, :], ROWF[:, :], [ROWF.b], [ROWI.b])
                ck("s5")
                for c in range(8):
                    rs = RS[c % 2]
                    P.idma(rs[:, :], crow, ROWI[:, c:c + 1], r=[ROWI.b], w=[rs.b])
                    bank = PS[2 + c % 2]
                    P.op("pe", lambda e, bank=bank, rs=rs: e.transpose(out=bank[:, 0:128], in_=rs[:, 0:128], identity=identf[:, :]), r=[rs.b, identf.b], w=[bank.b])
                    evac("dve", KTS[hp, c * 128:(c + 1) * 128], bank[hp, 0:128], [bank.b], [KTS.b])
                    evac("act", VAS[:, c, 0:64], rs[:, 128 + g * 64:192 + g * 64], [rs.b], [VAS.b])
                ck("s6")
                small_attn(b, g, KTS, None, 8, lambda c: (VAS[:, c, :], VAS.b), 0, 1, True, False)
                ck("s7")
                small_attn(b, g, KTW, None, 4, lambda c, g=g: (VAW[:, c, g, :], VAW.b), 1, 2, False, False)
                ck("s8")
        ck("s9")
        WP.release(ik)
        WP.release(iv)
        evac("dve", OASP[:, :, :, 64:128], OASP[:, :, :, 0:64], [OASP.b], [OASP.b])
        for b in range(n):
            for g in range(2):
                P.op("pe", lambda e, b=b, g=g: e.transpose(out=PS[7][:, (b * 2 + g) * 4:(b * 2 + g) * 4 + 4], in_=OASP[0:4, b, g, :], identity=identf[0:4, 0:4]), r=[OASP.b, identf.b], w=[PS[7].b])
        for b in range(n):
            pv8 = PS[7][:, b * 8:(b + 1) * 8].rearrange("p (i e) -> p i e", e=2)
            evac("dve", OTS[0:64, 0, :, b], pv8[0:64, :, 0], [PS[7].b], [OTS.b])
            evac("dve", OTS[64:128, 0, :, b], pv8[64:128, :, 1], [PS[7].b], [OTS.b])

        ck("s10")
        for half in range(2):
            ix, sx = WP.acquire(f"in_xr{half}")
            ig, sgr = WP.acquire(f"in_gr{half}")
            for j in range(4):
                i = half * 4 + j
                px, pg = PS[j % 2], PS[2 + j % 2]
                for kc in range(8):
                    mm(px[:, 0:n], sx[:, kc * 512 + j * 128:kc * 512 + (j + 1) * 128], A[:, kc, 0:n], kc == 0, kc == 7, [sx.b, A.bs[kc]], [px.b])
                for kc in range(8):
                    mm(pg[:, 0:n], sgr[:, kc * 512 + j * 128:kc * 512 + (j + 1) * 128], A[:, kc, 0:n], kc == 0, kc == 7, [sgr.b, A.bs[kc]], [pg.b])
                evac("act", XRS[:, i, :], px[:, 0:n], [px.b], [XRS.b])
                xc = nxt("tf", TMPF)
                P.op("dve", lambda e, xc=xc, i=i: e.tensor_scalar(out=xc[:, 0:n], in0=XRS[:, i, :], scalar1=RGP[:, 3, i:i + 1], scalar2=RGP[:, 4, i:i + 1], op0=ALU.mult, op1=ALU.add), r=[XRS.b, RGP.b], w=[xc.b])
                for k in range(3):
                    P.op("dve", lambda e, xc=xc, i=i, k=k: e.scalar_tensor_tensor(out=xc[:, 0:n], in0=SCV[:, i, k, :], scalar=RGP[:, k, i:i + 1], in1=xc[:, 0:n], op0=ALU.mult, op1=ALU.add),
                         r=[SCV.b, RGP.b, xc.b], w=[xc.b])
                xcb = nxt("xcb", XCB)
                evac("act", xcb[:, 0:n], xc[:, 0:n], [xc.b], [xcb.b])
                pr, pi = PS[4 + j % 2], PS[6 + j % 2]
                mm(pr[:, 0:n], WAX[0][:, i, :], xcb[:, 0:n], True, True, [WAX[0].b, xcb.b], [pr.b])
                mm(pi[:, 0:n], WAX[1][:, i, :], xcb[:, 0:n], True, True, [WAX[1].b, xcb.b], [pi.b])
                ra, ii, sq_ = nxt("tf", TMPF), nxt("tf", TMPF), nxt("tf", TMPF)
                P.op("act", lambda e, ra=ra, pr=pr, i=i: e.activation(out=ra[:, 0:n], in_=pr[:, 0:n], func=AF.Sigmoid, bias=RGP[:, 5, i:i + 1]), r=[pr.b, RGP.b], w=[ra.b])
                P.op("act", lambda e, ii=ii, pi=pi, i=i: e.activation(out=ii[:, 0:n], in_=pi[:, 0:n], func=AF.Sigmoid, bias=RGP[:, 6, i:i + 1]), r=[pi.b, RGP.b], w=[ii.b])
                P.op("act", lambda e, ra=ra, i=i: e.activation(out=ra[:, 0:n], in_=ra[:, 0:n], func=AF.Exp, scale=RGP[:, 7, i:i + 1]), r=[ra.b, RGP.b], w=[ra.b])
                tt("pool", sq_[:, 0:n], ra[:, 0:n], ra[:, 0:n], ALU.mult, [ra.b], [sq_.b])
                P.op("act", lambda e, sq_=sq_: e.activation(out=sq_[:, 0:n], in_=sq_[:, 0:n], func=AF.Sqrt, scale=-1.0, bias=1.0), r=[sq_.b], w=[sq_.b])
                tt("pool", ii[:, 0:n], ii[:, 0:n], sq_[:, 0:n], ALU.mult, [ii.b, sq_.b], [ii.b])
                tt("dve", ii[:, 0:n], ii[:, 0:n], xc[:, 0:n], ALU.mult, [ii.b, xc.b], [ii.b])
                tt("dve", ra[:, 0:n], ra[:, 0:n], SHS[:, i, :], ALU.mult, [ra.b, SHS.b], [ra.b])
                tt("dve", HS[:, i, :], ra[:, 0:n], ii[:, 0:n], ALU.add, [ra.b, ii.b], [HS.b])
                P.op("act", lambda e, sq_=sq_, pg=pg: e.activation(out=sq_[:, 0:n], in_=pg[:, 0:n], func=AF.Gelu_apprx_tanh), r=[pg.b, sq_.b], w=[sq_.b])
                tt("dve", YR[:, i, 0:n], HS[:, i, :], sq_[:, 0:n], ALU.mult, [HS.b, sq_.b], [YR.bs[i]])
            WP.release(ix)
            WP.release(ig)
        for b in range(n):
            P.dma("act", out["conv_s"][b, 2].rearrange("(c p) -> p c", p=128), XRS[:, :, b], r=[XRS.b], w=[outb["conv_s"]], sem=XRS.b, allow_slow_non_contiguous=True)
            P.dma("act", out["h_s"][b].rearrange("(c p) -> p c", p=128), HS[:, :, b], r=[HS.b], w=[outb["h_s"]], sem=HS.b, allow_slow_non_contiguous=True)

        ck("s11")
        merge(n, lambda: [(OTS[:, 0, kc, :], [OTS.b], kc * 128, 128) for kc in range(4)])
        layernorm(1, n)
        ffn(2, n)
        layernorm(2, n)
        store_y(lambda sub, m: out["y_s"][0:m, :], n, outb["y_s"])

    for (seq, t) in tiles:
        load_x(lambda sub, m, seq=seq, t=t: dr["xp"][seq, t * TT + sub * 128:t * TT + sub * 128 + m, :], TT)
        ffn(1, TT)
        layernorm(0, TT)
        if stop_after == "ln1":
            store_y(lambda sub, m, seq=seq, t=t: out["y_p"][seq, t * TT + sub * 128:t * TT + sub * 128 + m, :], TT, outb["y_p"])
            skip = PASS_ORDER[18:]
            for name in skip:
                i, _ = WP.acquire(name)
                WP.release(i)
            continue
        if t == 0:
            P.op("pool", lambda e: e.memset(XH[:, :, :], 0.0), w=[XH.b])
            P.op("pool", lambda e: e.memset(HST[:, :], 0.0), w=[HST.b])
        win_tok_prompt(seq, t)
        compress_prompt()
        attention_prompt(seq, t)
        rglru(TT, t == 0)
        if t == NT - 1:
            for k in range(3):
                P.dma("act", out["conv_p"][seq, k].rearrange("(c p) -> p c", p=128), XH[:, :, k], r=[XH.b], w=[outb["conv_p"]], sem=XH.b, allow_slow_non_contiguous=True)
            P.dma("act", out["h_p"][seq].rearrange("(c p) -> p c", p=128), HST[:, :], r=[HST.b], w=[outb["h_p"]], sem=HST.b, allow_slow_non_contiguous=True)
        merge(TT, lambda: [(OT[:, kc, :], [OT.b], kc * 128, 128) for kc in range(4)])
        layernorm(1, TT)
        ffn(2, TT)
        layernorm(2, TT)
        store_y(lambda sub, m, seq=seq, t=t: out["y_p"][seq, t * TT + sub * 128:t * TT + sub * 128 + m, :], TT, outb["y_p"])

    P.barrier()
    es_p.close()
    if do_sample:
        sample_pass()
    P.barrier(engines=("act",))
    P.replay()
    es.close()
    return nc


_NC_CACHE = {}


def kernel(**inputs):
    cfg = dict(tiles=[(s_, t_) for s_ in range(NSEQ) for t_ in range(NT)], sample=True)
    if "nc" not in _NC_CACHE:
        _NC_CACHE["nc"] = build(cfg)
    nc = _NC_CACHE["nc"]
    f32 = lambda a: np.ascontiguousarray(np.asarray(a), dtype=np.float32)
    consts = _consts()
    ccmp = f32(inputs["cache_cmp_kv"])[0].reshape(NPOOL, 128, 256)
    csel = f32(inputs["cache_sel_kv"])[0].reshape(NPOOL, 128, 256)
    xp = f32(inputs["x_prompt"])
    xs = f32(inputs["x_sample"])[:, 0]
    cwin = f32(inputs["cache_win_kv"])[0].reshape(NCORES * NSMP, 512, 256)
    sconv = f32(inputs["state_conv"])[0]
    sh = f32(inputs["state_h"])[0]
    ptab = np.ascontiguousarray(np.asarray(inputs["page_table"]), dtype=np.int32)
    shared = {}
    for k in IN_SHAPES:
        if k in ("xp", "xs", "ccmp", "csel", "cwin", "sconv", "sh", "ptab"):
            continue
        shared[k] = f32(inputs[k])[0]
    in_maps = []
    for c in range(NCORES):
        m = dict(shared)
        m.update(consts)
        m["xp"] = xp[NSEQ * c:NSEQ * (c + 1)]
        m["xs"] = xs[NSMP * c:NSMP * (c + 1)]
        m["ccmp"] = ccmp
        m["csel"] = csel
        m["cwin"] = cwin[NSMP * c:NSMP * (c + 1)]
        m["sconv"] = sconv[NSMP * c:NSMP * (c + 1)]
        m["sh"] = sh[NSMP * c:NSMP * (c + 1)]
        m["ptab"] = ptab[NSMP * c:NSMP * (c + 1)]
        in_maps.append(m)
    res = run_bass_kernel_spmd(nc, in_maps, core_ids=list(range(NCORES)))
    r = res.results
    cat = lambda k: np.concatenate([np.asarray(r[c][k]) for c in range(NCORES)], axis=0)
    B, BS = NCORES * NSEQ, NCORES * NSMP
    return (
        cat("y_p").reshape(B, SEQ, D).astype(np.float32),
        cat("y_s").reshape(BS, 1, D).astype(np.float32),
        cat("cmp_p").reshape(1, B, SEQ, 2, 2, 64).astype(np.float32),
        cat("cmp_s").reshape(1, BS, 1, 2, 2, 64).astype(np.float32),
        cat("sel_p").reshape(1, B, SEQ, 2, 2, 64).astype(np.float32),
        cat("sel_s").reshape(1, BS, 1, 2, 2, 64).astype(np.float32),
        cat("win_p").reshape(1, B, 512, 2, 2, 64).astype(np.float32),
        cat("win_s").reshape(1, BS, 512, 2, 2, 64).astype(np.float32),
        cat("conv_p").reshape(1, B, 3, D).astype(np.float32),
        cat("conv_s").reshape(1, BS, 3, D).astype(np.float32),
        cat("h_p").reshape(1, B, D).astype(np.float32),
        cat("h_s").reshape(1, BS, D).astype(np.float32),
    )
```
